# Optimizing a Trainium2 kernel written in Bass

```python
import jax, jax.numpy as jnp
from jax import lax
import numpy as np

D_MODEL = 1024
BATCH = 16
SEQ = 2048
DEPTH = 4

N_MIXERS = 2
N_MLA = (DEPTH + 1) // 2
N_HGRN = DEPTH // 2

MLA_HEADS = 16
QK_NOPE = 64
QK_ROPE = 32
V_HEAD = 64
Q_LORA = 768
KV_LORA = 256
ROPE_THETA = 10000.0
Q_BLOCK = 128

HGRN_EXPAND = 128
HGRN_HEADS = D_MODEL // HGRN_EXPAND
HGRN_V = D_MODEL // HGRN_HEADS
HGRN_CHUNK = 64

D_FF = -(-8 * D_MODEL // (3 * 256)) * 256

ALPHA = (2.0 * DEPTH) ** 0.25
BETA = (8.0 * DEPTH) ** -0.25
LN_EPS = 1e-5
RMS_EPS = 1e-6

kernel_name = 'hybrid_mla_hgrn2_deepnorm_adaln'


def layer_norm(x, g, b):
    xf = x.astype(jnp.float32)
    mu = jnp.mean(xf, -1, keepdims=True)
    var = jnp.mean(jnp.square(xf - mu), -1, keepdims=True)
    return ((xf - mu) * lax.rsqrt(var + LN_EPS) * g + b).astype(x.dtype)


def rms_norm(x, g):
    xf = x.astype(jnp.float32)
    ms = jnp.mean(jnp.square(xf), -1, keepdims=True)
    return (xf * lax.rsqrt(ms + RMS_EPS) * g).astype(x.dtype)


def rope_cos_sin(positions):
    inv_freq = ROPE_THETA ** (-jnp.arange(0, QK_ROPE, 2, dtype=jnp.float32) / QK_ROPE)
    ang = positions.astype(jnp.float32)[..., None] * inv_freq
    return jnp.cos(ang), jnp.sin(ang)


def apply_rope(x, cos, sin):
    x1, x2 = jnp.split(x.astype(jnp.float32), 2, axis=-1)
    return jnp.concatenate([x1 * cos - x2 * sin, x1 * sin + x2 * cos], -1).astype(x.dtype)


def causal_mla_attention(q_nope, q_rope, k_nope, k_rope, v):
    S = q_nope.shape[1]
    scale = (QK_NOPE + QK_ROPE) ** -0.5
    neg = jnp.finfo(jnp.float32).min
    outs = []
    for blk in range(S // Q_BLOCK):
        q0 = blk * Q_BLOCK
        kend = q0 + Q_BLOCK
        s = (jnp.einsum('bqhd,bkhd->bhqk', q_nope[:, q0:kend], k_nope[:, :kend])
             + jnp.einsum('bqhr,bkr->bhqk', q_rope[:, q0:kend], k_rope[:, :kend]))
        s = s.astype(jnp.float32) * scale
        mask = (q0 + jnp.arange(Q_BLOCK))[:, None] >= jnp.arange(kend)[None, :]
        p = jax.nn.softmax(jnp.where(mask, s, neg), axis=-1).astype(v.dtype)
        outs.append(jnp.einsum('bhqk,bkhd->bqhd', p, v[:, :kend]))
    return jnp.concatenate(outs, axis=1)


def mla(h, cos, sin, w_in, q_norm_g, w_qb, kv_norm_g, w_kvb, w_o):
    B, S, _ = h.shape
    proj = h @ w_in
    q_lat, kv_lat, k_rope = jnp.split(proj, [Q_LORA, Q_LORA + KV_LORA], axis=-1)
    q = (rms_norm(q_lat, q_norm_g) @ w_qb).reshape(B, S, MLA_HEADS, QK_NOPE + QK_ROPE)
    kv = (rms_norm(kv_lat, kv_norm_g) @ w_kvb).reshape(B, S, MLA_HEADS, QK_NOPE + V_HEAD)
    q_nope, q_rope = jnp.split(q, [QK_NOPE], axis=-1)
    k_nope, v = jnp.split(kv, [QK_NOPE], axis=-1)
    q_rope = apply_rope(q_rope, cos[:, :, None, :], sin[:, :, None, :])
    k_rope = apply_rope(k_rope, cos, sin)
    o = causal_mla_attention(q_nope, q_rope, k_nope, k_rope, v)
    return o.reshape(B, S, MLA_HEADS * V_HEAD) @ w_o


def chunk_gated_recurrence(q, k, v, log_f):
    B, S, H, K = q.shape
    V = v.shape[-1]
    C = HGRN_CHUNK
    N = S // C

    def to_chunks(t):
        return t.reshape(B, N, C, H, t.shape[-1]).transpose(1, 0, 3, 2, 4)

    causal = jnp.tril(jnp.ones((C, C), dtype=bool))[:, :, None]

    def step(state, inp):
        q_c, k_c, v_c, g_c = inp
        b = jnp.cumsum(g_c, axis=-2)
        diff = b[..., :, None, :] - b[..., None, :, :]
        decay = jnp.where(causal, jnp.exp(jnp.where(causal, diff, 0.0)), 0.0)
        attn = jnp.einsum('bhtk,bhsk,bhtsk->bhts', q_c, k_c, decay)
        o = (jnp.einsum('bhts,bhsv->bhtv', attn, v_c)
             + jnp.einsum('bhtk,bhkv->bhtv', q_c * jnp.exp(b), state))
        b_last = b[..., -1:, :]
        state = (jnp.exp(b_last[..., 0, :])[..., None] * state
                 + jnp.einsum('bhsk,bhsv->bhkv', k_c * jnp.exp(b_last - b), v_c))
        return state, o

    state0 = jnp.zeros((B, H, K, V), jnp.float32)
    _, o = lax.scan(step, state0, (to_chunks(q), to_chunks(k), to_chunks(v), to_chunks(log_f)))
    return o.transpose(1, 0, 3, 2, 4).reshape(B, S, H, V)


def hgrn2(h, lb, w_in, g_norm_g, w_o):
    B, S, _ = h.shape
    HK = HGRN_HEADS * HGRN_EXPAND
    HV = HGRN_HEADS * HGRN_V
    q, fx, i, g = jnp.split(h @ w_in, [HK, 2 * HK, 2 * HK + HV], axis=-1)
    q = jax.nn.silu(q.astype(jnp.float32)).reshape(B, S, HGRN_HEADS, HGRN_EXPAND)
    fx = fx.astype(jnp.float32).reshape(B, S, HGRN_HEADS, HGRN_EXPAND)
    lb = lb.astype(jnp.float32).reshape(HGRN_HEADS, HGRN_EXPAND)
    sig = jax.nn.sigmoid(fx)
    f = lb + (1.0 - lb) * sig
    log_f = jnp.log(f)
    k = 1.0 - f
    v = i.astype(jnp.float32).reshape(B, S, HGRN_HEADS, HGRN_V)
    o = chunk_gated_recurrence(q, k, v, log_f)
    o = rms_norm(o, g_norm_g).reshape(B, S, HV).astype(h.dtype)
    return (o * jax.nn.silu(g)) @ w_o


def swiglu(h, w_in, w_out):
    gate, up = jnp.split(h @ w_in, 2, axis=-1)
    return (jax.nn.silu(gate) * up) @ w_out


def ada_mod(c, w, b):
    mod = (jax.nn.silu(c) @ w + b)[:, None, :]
    shift, scale, gate = jnp.split(mod, 3, axis=-1)
    return shift, scale, gate


def _w(k, shape, fan_in, scale=1.0):
    return jax.random.normal(k, shape, jnp.float32) * (scale * fan_in ** -0.5)


def setup_inputs(seed: int = 0) -> dict:
    key = jax.random.key(seed)
    ks = jax.random.split(key, 24)
    D = D_MODEL
    x = jax.random.normal(ks[0], (BATCH, SEQ, D), jnp.float32)
    c = jax.random.normal(ks[1], (BATCH, D), jnp.float32)
    offsets = jax.random.randint(ks[2], (BATCH, 1), 0, 4096, dtype=jnp.int32)
    positions = offsets + jnp.arange(SEQ, dtype=jnp.int32)[None, :]
    mla_w_in = _w(ks[3], (N_MLA, D, Q_LORA + KV_LORA + QK_ROPE), D)
    mla_q_norm = 1.0 + 0.02 * jax.random.normal(ks[4], (N_MLA, Q_LORA), jnp.float32)
    mla_w_qb = _w(ks[5], (N_MLA, Q_LORA, MLA_HEADS * (QK_NOPE + QK_ROPE)), Q_LORA)
    mla_kv_norm = 1.0 + 0.02 * jax.random.normal(ks[6], (N_MLA, KV_LORA), jnp.float32)
    mla_w_kvb = _w(ks[7], (N_MLA, KV_LORA, MLA_HEADS * (QK_NOPE + V_HEAD)), KV_LORA)
    mla_w_o = _w(ks[8], (N_MLA, MLA_HEADS * V_HEAD, D), MLA_HEADS * V_HEAD, BETA)
    hgrn_lb = 0.5 * jax.random.normal(ks[9], (N_HGRN, HGRN_HEADS * HGRN_EXPAND), jnp.float32)
    hgrn_w_in = _w(ks[10], (N_HGRN, D, 2 * HGRN_HEADS * HGRN_EXPAND + HGRN_HEADS * HGRN_V + D), D)
    hgrn_g_norm = 1.0 + 0.02 * jax.random.normal(ks[11], (N_HGRN, HGRN_V), jnp.float32)
    hgrn_w_o = _w(ks[12], (N_HGRN, HGRN_HEADS * HGRN_V, D), HGRN_HEADS * HGRN_V, BETA)
    ffn_w_in = _w(ks[13], (DEPTH, D, 2 * D_FF), D)
    ffn_w_out = _w(ks[14], (DEPTH, D_FF, D), D_FF, BETA)
    ada_w = _w(ks[15], (DEPTH, 2, D, 3 * D), D, 0.1)
    ada_b = 0.01 * jax.random.normal(ks[16], (DEPTH, 2, 3 * D), jnp.float32)
    ln_g = 1.0 + 0.02 * jax.random.normal(ks[17], (DEPTH, 2, D), jnp.float32)
    ln_b = 0.01 * jax.random.normal(ks[18], (DEPTH, 2, D), jnp.float32)
    return {'x': x, 'c': c, 'positions': positions,
            'mla_w_in': mla_w_in, 'mla_q_norm': mla_q_norm, 'mla_w_qb': mla_w_qb,
            'mla_kv_norm': mla_kv_norm, 'mla_w_kvb': mla_w_kvb, 'mla_w_o': mla_w_o,
            'hgrn_lb': hgrn_lb, 'hgrn_w_in': hgrn_w_in, 'hgrn_g_norm': hgrn_g_norm, 'hgrn_w_o': hgrn_w_o,
            'ffn_w_in': ffn_w_in, 'ffn_w_out': ffn_w_out,
            'ada_w': ada_w, 'ada_b': ada_b, 'ln_g': ln_g, 'ln_b': ln_b}


def reference(x, c, positions, mla_w_in, mla_q_norm, mla_w_qb, mla_kv_norm, mla_w_kvb, mla_w_o,
              hgrn_lb, hgrn_w_in, hgrn_g_norm, hgrn_w_o, ffn_w_in, ffn_w_out,
              ada_w, ada_b, ln_g, ln_b):
    cos, sin = rope_cos_sin(positions)
    lb_soft = jax.nn.softmax(hgrn_lb.astype(jnp.float32), axis=0)
    lower_bounds = jnp.cumsum(lb_soft, axis=0) - lb_soft[0]
    for layer in range(DEPTH):
        j = layer // N_MIXERS
        shift, scale, gate = ada_mod(c, ada_w[layer, 0], ada_b[layer, 0])
        h = x * (1.0 + scale) + shift
        if layer % N_MIXERS == 0:
            y = mla(h, cos, sin, mla_w_in[j], mla_q_norm[j], mla_w_qb[j],
                    mla_kv_norm[j], mla_w_kvb[j], mla_w_o[j])
        else:
            y = hgrn2(h, lower_bounds[j], hgrn_w_in[j], hgrn_g_norm[j], hgrn_w_o[j])
        x = layer_norm(ALPHA * x + (1.0 + gate) * y, ln_g[layer, 0], ln_b[layer, 0])
        shift, scale, gate = ada_mod(c, ada_w[layer, 1], ada_b[layer, 1])
        h = x * (1.0 + scale) + shift
        y = swiglu(h, ffn_w_in[layer], ffn_w_out[layer])
        x = layer_norm(ALPHA * x + (1.0 + gate) * y, ln_g[layer, 1], ln_b[layer, 1])
    return x
```

```python
import math
from contextlib import ExitStack

import numpy as np
import concourse.bass as bass
import concourse.mybir as mybir
from concourse.bass_utils import run_bass_kernel_spmd

F32 = mybir.dt.float32
BF16 = mybir.dt.bfloat16
I32 = mybir.dt.int32
AF = mybir.ActivationFunctionType
ALU = mybir.AluOpType

D = 1024
S = 2048
DEPTH = 4
NCORES = 8
H = 16
QL = 768
KVL = 256
DFF = 2816
ALPHA = (2.0 * DEPTH) ** 0.25
LN_EPS = 1e-5
RMS_EPS = 1e-6
ATT_SCALE = 96.0 ** -0.5
NSLOT = 4
DEBUG_STOP = 0
LN_POOL = False
LN_DEFER = True
SLOT = 2048
MASKNEG = -30000.0

ENGS = ("pe", "act", "dve", "pool", "sp")


def I(name, *args, **kw):
    def fn(e):
        return getattr(e, name)(*args, **kw)
    return fn


class Op:
    __slots__ = ("id", "eng", "fn", "deps", "is_dma", "dsem", "dcount", "signal", "count")


class Prog:
    def __init__(self, same_eng_sync=True):
        self.ops = []
        self.last_write = {}
        self.readers = {}
        self.dma_counts = {}
        self.same_eng_sync = same_eng_sync
        self.overlaps = {}
        self.touch = {}

    def declare_alias(self, name, others):
        for o in others:
            self.overlaps.setdefault(name, set()).add(o)
            self.overlaps.setdefault(o, set()).add(name)

    def _add(self, eng, fn, reads, writes, is_dma, dsem):
        op = Op()
        op.id = len(self.ops)
        op.eng = eng
        op.fn = fn
        op.is_dma = is_dma
        op.dsem = dsem
        op.signal = False
        op.count = 0
        op.dcount = 0
        if is_dma:
            self.dma_counts[dsem] = self.dma_counts.get(dsem, 0) + 16
            op.dcount = self.dma_counts[dsem]
        deps = set()
        names = set()
        for k in reads:
            names.add(k[0] if isinstance(k, tuple) else k)
            w = self.last_write.get(k)
            if w is not None:
                deps.add(w)
            if isinstance(k, tuple) and k[0] == "ps":
                for ek, r in self.readers.get(k, {}).items():
                    if ek != eng:
                        deps.add(r)
        for k in writes:
            names.add(k[0] if isinstance(k, tuple) else k)
            w = self.last_write.get(k)
            if w is not None:
                deps.add(w)
            for r in self.readers.get(k, {}).values():
                deps.add(r)
        rk = ("dma", op.id) if is_dma else eng
        for n in names:
            for o in self.overlaps.get(n, ()):
                t = self.touch.get(o)
                if t:
                    deps.update(t.values())
        for n in names:
            if n in self.overlaps:
                t = self.touch.setdefault(n, {})
                t[rk] = op.id
                if len(t) > 24:
                    ks = sorted((k for k in t if isinstance(k, tuple)), key=lambda k: t[k])
                    for k in ks[:-8]:
                        del t[k]
        deps.discard(op.id)
        op.deps = deps
        for k in reads:
            self.readers.setdefault(k, {})[rk] = op.id
        for k in writes:
            self.last_write[k] = op.id
            self.readers[k] = {}
        self.ops.append(op)
        return op

    def op(self, eng, fn, reads=(), writes=()):
        return self._add(eng, fn, tuple(reads), tuple(writes), False, None)

    def dma(self, eng, fn, dsem, reads=(), writes=()):
        return self._add(eng, fn, tuple(reads), tuple(writes), True, dsem)

    def emit(self, nc, final_wait_eng="sp"):
        ops = self.ops
        ses = self.same_eng_sync

        def skip(a, op):
            return a.eng == op.eng and (not op.is_dma) and (a.eng == "pe" or not ses)

        for op in ops:
            for d in op.deps:
                a = ops[d]
                if a.is_dma or skip(a, op):
                    continue
                a.signal = True
        cnt = {e: 0 for e in ENGS}
        for op in ops:
            if op.signal:
                cnt[op.eng] += 1
                op.count = cnt[op.eng]
        dsems = sorted(self.dma_counts.keys())
        with ExitStack() as st:
            esem = {e: st.enter_context(nc.semaphore("E_" + e)) for e in ENGS}
            dsem = {d: st.enter_context(nc.semaphore("D_" + str(d))) for d in dsems}
            block = st.enter_context(nc.Block())

            def run_stream(ename, eng):
                waited = {}
                for op in ops:
                    if op.eng != ename:
                        continue
                    need = {}
                    for d in op.deps:
                        a = ops[d]
                        if a.is_dma:
                            key = ("d", a.dsem)
                            val = a.dcount
                        else:
                            if skip(a, op):
                                continue
                            key = ("e", a.eng)
                            val = a.count
                        if val > need.get(key, 0):
                            need[key] = val
                    for key, val in need.items():
                        if waited.get(key, 0) >= val:
                            continue
                        waited[key] = val
                        s = dsem[key[1]] if key[0] == "d" else esem[key[1]]
                        eng.wait_ge(s, val)
                    ins = op.fn(eng)
                    if op.is_dma:
                        ins.then_inc(dsem[op.dsem], 16)
                    elif op.signal:
                        ins.then_inc(esem[ename], 1)
                if ename == final_wait_eng:
                    for d in dsems:
                        if waited.get(("d", d), 0) < self.dma_counts[d]:
                            eng.wait_ge(dsem[d], self.dma_counts[d])

            @block.tensor
            def _(e):
                run_stream("pe", e)

            @block.scalar
            def _(e):
                run_stream("act", e)

            @block.vector
            def _(e):
                run_stream("dve", e)

            @block.gpsimd
            def _(e):
                run_stream("pool", e)

            @block.sync
            def _(e):
                run_stream("sp", e)
        return cnt


def _kc_slab(w, cols):
    K = w.shape[0]
    sub = w[:, cols].reshape(K // 128, 128, len(cols))
    return np.ascontiguousarray(sub.transpose(1, 0, 2)).reshape(128, -1)


def _fm(v):
    return np.ascontiguousarray(v.reshape(-1, 128).T)


VEC_LAYOUT = {}


def _vec_layout():
    if VEC_LAYOUT:
        return VEC_LAYOUT
    off = 0
    for name, n in (("adab", 8 * 24), ("lng", 8 * 8), ("lnb", 8 * 8), ("gq", 2 * 6), ("gkv", 2 * 2),
                    ("lb", 2 * 8), ("gn", 2), ("invf", 1), ("sgn", 1), ("phase", 1)):
        VEC_LAYOUT[name] = (off, n)
        off += n
    VEC_LAYOUT["_total"] = (off, 0)
    return VEC_LAYOUT


def prep_shared(inp):
    a = {}
    f = np.float32
    ar = np.arange
    ada = np.empty((8, 12, 128, SLOT), f)
    for l in range(DEPTH):
        for s2 in range(2):
            w = inp["ada_w"][l, s2]
            for sl in range(12):
                ada[l * 2 + s2, sl] = _kc_slab(w, ar(sl * 256, sl * 256 + 256))
    a["w_ada"] = ada
    mi = np.empty((2, 5, 128, SLOT), f)
    mp = np.empty((2, 8, 128, SLOT), f)
    mo = np.empty((2, 4, 128, SLOT), f)
    for j in range(2):
        w = inp["mla_w_in"][j]
        for sl in range(4):
            mi[j, sl] = _kc_slab(w, ar(sl * 256, sl * 256 + 256))
        rm = ar(1024, 1056)
        rs = np.concatenate([ar(1040, 1056), ar(1024, 1040)])
        mi[j, 4] = 0
        mi[j, 4, :, :1024] = _kc_slab(w, np.concatenate([rm, rm, rs, rs]))
        wq = inp["mla_w_qb"][j]
        wkv = inp["mla_w_kvb"][j]
        for p in range(8):
            h0, h1 = 2 * p, 2 * p + 1
            nope = np.concatenate([ar(h0 * 96, h0 * 96 + 64), ar(h1 * 96, h1 * 96 + 64)])

            def rmain(h):
                return ar(h * 96 + 64, h * 96 + 96)

            def rswap(h):
                return np.concatenate([ar(h * 96 + 80, h * 96 + 96), ar(h * 96 + 64, h * 96 + 80)])
            qcols = np.concatenate([nope, rmain(h0), rmain(h1), rswap(h0), rswap(h1)])
            kn = np.concatenate([ar(h0 * 128, h0 * 128 + 64), ar(h1 * 128, h1 * 128 + 64)])
            vv = np.concatenate([ar(h0 * 128 + 64, h0 * 128 + 128), ar(h1 * 128 + 64, h1 * 128 + 128)])
            mp[j, p, :, :1536] = _kc_slab(wq, qcols)
            mp[j, p, :, 1536:] = _kc_slab(wkv, np.concatenate([kn, vv]))
        for sl in range(4):
            mo[j, sl] = _kc_slab(inp["mla_w_o"][j], ar(sl * 256, sl * 256 + 256))
    a["w_mla_in"] = mi
    a["w_mla_pair"] = mp
    a["w_mla_o"] = mo
    hi = np.empty((2, 8, 2, 128, SLOT), f)
    ho = np.empty((2, 4, 128, SLOT), f)
    for j in range(2):
        w = inp["hgrn_w_in"][j]
        for hd in range(8):
            c0 = ar(hd * 128, hd * 128 + 128)
            hi[j, hd, 0] = _kc_slab(w, np.concatenate([c0, 1024 + c0]))
            hi[j, hd, 1] = _kc_slab(w, np.concatenate([2048 + c0, 3072 + c0]))
        for sl in range(4):
            ho[j, sl] = _kc_slab(inp["hgrn_w_o"][j], ar(sl * 256, sl * 256 + 256))
    a["w_hgrn_in"] = hi
    a["w_hgrn_o"] = ho
    fi = np.empty((DEPTH, 22, 128, SLOT), f)
    fo = np.zeros((DEPTH, 8, 2, 128, SLOT), f)
    for l in range(DEPTH):
        w = inp["ffn_w_in"][l]
        for jj in range(22):
            c0 = ar(jj * 128, jj * 128 + 128)
            fi[l, jj] = _kc_slab(w, np.concatenate([c0, DFF + c0]))
        w2 = inp["ffn_w_out"][l]
        for m in range(8):
            full = _kc_slab(w2, ar(m * 128, m * 128 + 128))
            fo[l, m, 0, :, :1408] = full[:, :1408]
            fo[l, m, 1, :, :1408] = full[:, 1408:]
    a["w_ffn_in"] = fi
    a["w_ffn_out"] = fo
    L = _vec_layout()
    vec = np.zeros((128, L["_total"][0]), f)

    def put(name, arr):
        o, n = L[name]
        vec[:, o:o + n] = arr.reshape(128, n)
    put("adab", np.stack([_fm(inp["ada_b"][l, s2]) for l in range(DEPTH) for s2 in range(2)], 1))
    put("lng", np.stack([_fm(inp["ln_g"][l, s2]) for l in range(DEPTH) for s2 in range(2)], 1))
    put("lnb", np.stack([_fm(inp["ln_b"][l, s2]) for l in range(DEPTH) for s2 in range(2)], 1))
    put("gq", np.stack([_fm(inp["mla_q_norm"][j]) for j in range(2)], 1))
    put("gkv", np.stack([_fm(inp["mla_kv_norm"][j]) for j in range(2)], 1))
    put("lb", np.stack([_fm(inp["hgrn_lb"][j]) for j in range(2)], 1))
    put("gn", np.stack([inp["hgrn_g_norm"][j] for j in range(2)], 1))
    r = ar(128)
    inv_freq = (10000.0 ** (-np.arange(0, 32, 2, dtype=np.float32) / 32)).astype(f)
    put("invf", inv_freq[(r % 32) % 16])
    put("sgn", np.where(r < 64, 1.0, np.where((r % 32) < 16, -1.0, 1.0)).astype(f))
    put("phase", np.where(r < 64, np.pi / 2, 0.0).astype(f))
    a["vecs"] = vec
    cst = np.zeros((128, 4 * 128), f)
    cst[:, 0:128] = 1.0
    cst[:, 128:256] = np.eye(128, dtype=f)
    kk = r[:, None]
    qq = r[None, :]
    cst[:, 256:384] = np.where(qq >= kk, 0.0, MASKNEG)
    cst[:, 384:512] = ((kk // 64 == qq // 64) & (kk <= qq)).astype(f)
    a["consts"] = cst
    return a


def prep_core(inp, b0, nseq):
    f = np.float32
    xs = []
    for b in range(b0, b0 + nseq):
        xt = inp["x"][b].T.reshape(8, 128, S).transpose(1, 0, 2)
        xs.append(np.ascontiguousarray(xt))
    ct = np.stack([_fm(inp["c"][b]) for b in range(b0, b0 + nseq)], 2)
    pos = np.stack([np.broadcast_to(inp["positions"][b][None, :], (128, S)) for b in range(b0, b0 + nseq)], 0)
    return {"x_in": np.stack(xs, 0).astype(f), "c_in": np.ascontiguousarray(ct).astype(f),
            "pos_in": np.ascontiguousarray(pos).astype(np.int32)}


def build(sublayers, nseq, x_is_scaled_input=True):
    nc = bass.Bass("TRN2", target_bir_lowering=False)
    L = _vec_layout()
    NV = L["_total"][0]

    def din(name, shape, dt=F32):
        return nc.dram_tensor(name, list(shape), dt, kind="ExternalInput").ap()
    x_in = din("x_in", [nseq, 128, 8, S])
    c_in = din("c_in", [128, 8, nseq])
    pos_in = din("pos_in", [nseq, 128, S], I32)
    w_ada = din("w_ada", [8, 12, 128, SLOT])
    w_mla_in = din("w_mla_in", [2, 5, 128, SLOT])
    w_mla_pair = din("w_mla_pair", [2, 8, 128, SLOT])
    w_mla_o = din("w_mla_o", [2, 4, 128, SLOT])
    w_hgrn_in = din("w_hgrn_in", [2, 8, 2, 128, SLOT])
    w_hgrn_o = din("w_hgrn_o", [2, 4, 128, SLOT])
    w_ffn_in = din("w_ffn_in", [DEPTH, 22, 128, SLOT])
    w_ffn_out = din("w_ffn_out", [DEPTH, 8, 2, 128, SLOT])
    vecs_in = din("vecs", [128, NV])
    consts_in = din("consts", [128, 512])
    y_out = nc.dram_tensor("y_out", [nseq, 128, 8, S], F32, kind="ExternalOutput").ap()
    ogd = nc.dram_tensor("og_scratch", [8, 128, S], BF16, kind="Internal").ap()

    P = Prog()
    last_sub = sublayers[-1]
    with ExitStack() as st:
        def sb(name, shape, dt):
            return st.enter_context(nc.sbuf_tensor(name, list(shape), dt))
        xp = sb("xp", [128, 8, S], F32)
        hb = sb("hb", [128, 8, S], BF16)
        ring = sb("ring", [128, NSLOT, SLOT], BF16)
        vecs = sb("vecs_sb", [128, NV], F32)
        cst = sb("cst_sb", [128, 512], F32)
        ones_f = cst[:, 0:128]
        mask2_f = cst[:, 384:512]
        ones_bf = sb("ones_bf", [128, 128], BF16)
        ident_bf = sb("ident_bf", [128, 128], BF16)
        maskT_bf = sb("maskT_bf", [128, 128], BF16)
        modall = sb("modall", [128, 8, 24, nseq], F32)
        c_sb = sb("c_sb", [128, 8 * nseq], F32)
        sc_bf = sb("sc_bf", [128, 8, nseq], BF16)
        T_GATE1 = sb("T_GATE1", [128, 8, nseq, 8], F32)
        T_S1 = sb("T_S1", [128, 8, nseq, 8], F32)
        T_SH = sb("T_SH", [128, 8, nseq, 8], F32)
        T_GA = sb("T_GA", [128, 8, 8], F32)
        T_BA = sb("T_BA", [128, 8, 8], F32)
        T_HG = sb("T_HG", [128, 8, nseq, 8], F32)
        T_HB = sb("T_HB", [128, 8, nseq, 8], F32)
        T_H0 = sb("T_H0", [128, nseq, 8], F32)
        T_LB = sb("T_LB", [128, 2, 8], F32)
        T_OML = sb("T_OML", [128, 2, 8], F32)
        T_NOML = sb("T_NOML", [128, 2, 8], F32)
        T_tmp = sb("T_tmp", [128, 64], F32)
        eps_t = sb("eps_t", [128, 2], F32)
        eps_rms = eps_t[:, 0:1]
        remaining = nc.sbuf_bytes_remaining
        ARENA = (remaining - 256) // 4 * 4
        print('ARENA bytes', ARENA)
        arena = sb("arena", [128, ARENA // 4], F32)
        ps = st.enter_context(nc.psum_tensor("ps", [128, 4096], F32))

        class AV:
            def __init__(self, name, off, shape, dt):
                self.name = name
                esz = 2 if dt == BF16 else 4
                n = int(np.prod(shape))
                self.off = off
                self.end = off + n * esz
                assert self.end <= ARENA, (name, self.end, ARENA)
                v = arena[:, off // 4:(off + n * esz) // 4]
                if dt == BF16:
                    v = v.bitcast(BF16)
                if len(shape) == 2:
                    v = v.rearrange("p (a b) -> p a b", b=shape[1])
                elif len(shape) == 3:
                    v = v.rearrange("p (a b c) -> p a b c", b=shape[1], c=shape[2])
                self.ap = v

        avs = []

        def av(name, off, shape, dt):
            a = AV(name, off, shape, dt)
            for o in avs:
                if a.off < o.end and o.off < a.end:
                    P.declare_alias(a.name, [o.name])
            avs.append(a)
            return a

        K = 1024
        A_act = av("f_act", 0, [22, 1024], BF16)
        A_sgt = av("f_sgt", 45056, [2, 1024], BF16)
        LNO = 49152
        A_lsq = av("ln_sq", LNO, [2, 512], F32)
        A_lmean = av("ln_mean", LNO + 4096, [512], F32)
        A_lvar = av("ln_var", LNO + 6144, [512], F32)
        A_lrstd = av("ln_rstd", LNO + 8192, [512], F32)
        A_lt = av("ln_t", LNO + 10240, [2, 512], F32)
        M_qg = av("m_qg", 0, [6, S], BF16)
        M_kvn = av("m_kvn", 24576, [2, S], BF16)
        M_rq = av("m_rq", 32768, [S], F32)
        M_cs = av("m_cs", 40960, [S], F32)
        M_krope = av("m_krope", 63488, [S], BF16)
        M_qk = av("m_qk", 67584, [4, S], BF16)
        M_kvlat = av("m_kvlat", 67584, [2, 1024], F32)
        M_sq = av("m_sq", 75776, [2, 1024], F32)
        M_rkv = av("m_rkv", 49152, [1024], F32)
        M_sd = av("m_sd", 53248, [1024], F32)
        M_vaug = av("m_vaug", 49152, [2, 16, 128], BF16)
        M_pt = av("m_pt", 57344, [4, 512], BF16)
        M_rcp = av("m_rcp", 61440, [512], F32)
        M_tmpa = av("m_tmpa", 83968, [1024], F32)
        M_tmpb = av("m_tmpb", 57344, [1024], F32)
        M_posi = av("m_posi", 0, [S], F32)
        M_ki = av("m_ki", 24576, [S], F32)
        M_kf = av("m_kf", 32768, [S], F32)
        G_T = [av("g_T%d" % i, i * 8192, [S], F32) for i in range(4)]
        G_qt = av("g_qt", 32768, [S], BF16)
        G_kt = av("g_kt", 36864, [S], BF16)
        G_kh = av("g_kh", 40960, [S], BF16)
        G_khT = av("g_khT", 45056, [16, 128], BF16)
        G_vtok = av("g_vtok", 63488, [16, 128], BF16)
        G_sg = av("g_sg", 67584, [S], F32)
        G_sbf = av("g_sbf", 75776, [32, 128], BF16)
        G_smask = av("g_smask", 83968, [S], BF16)
        G_u = av("g_u", 16384, [4096], F32)
        G_eblrep = av("g_eblrep", 49152, [1024], F32)
        G_sout = av("g_sout", 53248, [1024], F32)
        G_asb = av("g_asb", 57344, [4, 128], BF16)
        G_osq = av("g_osq", 49152, [1024], F32)
        G_rstd = av("g_rstd", 53248, [1024], F32)
        G_to = av("g_to", 57344, [1024], F32)
        G_og = av("g_og", 61440, [1024], BF16)
        assert ARENA >= 88064 + 256, ARENA
        G_ebl2 = av("g_ebl2", 88064, [32], F32)

        def xk(c, t0, t1):
            return [("xp", c, q) for q in range(t0 // 512, (t1 + 511) // 512)]

        def hk(c, t0, t1):
            return [("hb", c, q) for q in range(t0 // 512, (t1 + 511) // 512)]

        def pk(*banks):
            return [("ps", b) for b in banks]

        def bank(b, n=512, off=0):
            return ps[:, b * 512 + off: b * 512 + off + n]

        def slot2(s):
            return ps[:, s * 1024:(s + 1) * 1024]

        wlist = []
        wstate = {"issued": 0, "next": 0}

        def wissue(upto):
            while wstate["issued"] < min(upto, len(wlist)):
                i = wstate["issued"]
                src, n = wlist[i]
                sl = i % NSLOT
                if n > 1024:
                    dst = ring[:, sl, 0:n].rearrange("p (a b) -> p a b", a=2)
                    srcv = src[:, 0:n].rearrange("p (a b) -> p a b", a=2)
                else:
                    dst = ring[:, sl, 0:n]
                    srcv = src[:, 0:n]
                P.dma("pool", (I("dma_start", out=dst, in_=srcv)), "ring%d" % sl,
                      writes=[("ring", sl)])
                wstate["issued"] += 1

        def wget():
            i = wstate["next"]
            wstate["next"] += 1
            wissue(i + NSLOT)
            sl = i % NSLOT
            return ring[:, sl, :], ("ring", sl)

        subs_needed = sorted(set(sublayers))
        for sub in subs_needed:
            for sl in range(12):
                wlist.append((w_ada[sub, sl], 2048))
        for s_ in range(nseq):
            for sub in sublayers:
                l, typ = sub // 2, sub % 2
                j = l // 2
                if typ == 1:
                    for half in range(2):
                        for jj in range(22):
                            wlist.append((w_ffn_in[l, jj], 2048))
                        for m in range(8):
                            for kh in range(2):
                                wlist.append((w_ffn_out[l, m, kh], 1408))
                elif l % 2 == 0:
                    for half in range(2):
                        for sl in range(4):
                            wlist.append((w_mla_in[j, sl], 2048))
                        wlist.append((w_mla_in[j, 4], 1024))
                    for p in range(8):
                        wlist.append((w_mla_pair[j, p], 2048))
                    for _h in range(2):
                        for sl in range(4):
                            wlist.append((w_mla_o[j, sl], 2048))
                else:
                    for hd in range(8):
                        wlist.append((w_hgrn_in[j, hd, 0], 2048))
                        wlist.append((w_hgrn_in[j, hd, 1], 2048))
                    for _h in range(2):
                        for sl in range(4):
                            wlist.append((w_hgrn_o[j, sl], 2048))

        P.dma("sp", I("dma_start", out=vecs[:], in_=vecs_in), "ld_v", writes=["vecs"])
        P.dma("sp", I("dma_start", out=cst[:], in_=consts_in), "ld_c", writes=["cst"])
        P.dma("sp", I("dma_start", out=c_sb[:], in_=c_in.rearrange("p a b -> p (a b)")), "ld_cc",
              writes=["c_sb"])
        wissue(NSLOT)
        P.op("dve", I("memset", eps_t[:, 0:1], RMS_EPS), writes=["eps_t"])
        P.op("dve", I("tensor_copy", out=ones_bf[:], in_=cst[:, 0:128]), reads=["cst"], writes=["ones_bf"])
        P.op("dve", I("tensor_copy", out=ident_bf[:], in_=cst[:, 128:256]), reads=["cst"], writes=["ident_bf"])
        P.op("dve", I("tensor_copy", out=maskT_bf[:], in_=cst[:, 256:384]), reads=["cst"], writes=["maskT_bf"])
        P.op("act", I("activation", out=sc_bf[:].rearrange("p a b -> p (a b)"), in_=c_sb[:], func=AF.Silu),
             reads=["c_sb"], writes=["sc_bf"])

        def vcol(name, i):
            o, n = L[name]
            return vecs[:, o + i:o + i + 1]

        def vrange(name, i0, n):
            o, _ = L[name]
            return vecs[:, o + i0:o + i0 + n]

        for sub in subs_needed:
            for sl in range(12):
                w, wk = wget()
                for mi in range(2):
                    m = sl * 2 + mi
                    for kc in range(8):
                        P.op("pe", (I("matmul", ps[:, 7 * 512 + m * nseq: 7 * 512 + (m + 1) * nseq],
                            lhsT=w[:, kc * 256 + mi * 128: kc * 256 + mi * 128 + 128],
                            rhs=sc_bf[:, kc, :], start=(kc == 0), stop=(kc == 7))),
                            reads=[wk, "sc_bf"], writes=pk(7))
            o_ab = L["adab"][0]
            P.op("dve", (I("tensor_tensor", out=modall[:, sub, :, :],
                in0=ps[:, 7 * 512: 7 * 512 + 24 * nseq].rearrange("p (a b) -> p a b", b=nseq),
                in1=vecs[:, o_ab + sub * 24: o_ab + sub * 24 + 24].rearrange("p (a b) -> p a b", b=1).to_broadcast([128, 24, nseq]),
                op=ALU.add)), reads=pk(7) + ["vecs"], writes=[("mod", sub)])
            P.op("dve", (I("tensor_copy", out=T_SH[:, sub, :, :], in_=modall[:, sub, 0:8, :].rearrange("p c s -> p s c"))),
                reads=[("mod", sub)], writes=[("T_SH", sub)])
            P.op("dve", (I("tensor_scalar", out=T_S1[:, sub, :, :], in0=modall[:, sub, 8:16, :].rearrange("p c s -> p s c"),
                scalar1=1.0, scalar2=None, op0=ALU.add)),
                reads=[("mod", sub)], writes=[("T_S1", sub)])
            P.op("dve", (I("tensor_scalar", out=T_GATE1[:, sub, :, :], in0=modall[:, sub, 16:24, :].rearrange("p c s -> p s c"),
                scalar1=1.0, scalar2=None, op0=ALU.add)),
                reads=[("mod", sub)], writes=[("T_GATE1", sub)])
        o_g, o_b = L["lng"][0], L["lnb"][0]
        for idx, sub in enumerate(sublayers):
            is_last = (idx == len(sublayers) - 1)
            fac = 1.0 if is_last else ALPHA
            P.op("dve", (I("tensor_scalar", out=T_GA[:, sub, :], in0=vecs[:, o_g + sub * 8:o_g + sub * 8 + 8], scalar1=fac, scalar2=None, op0=ALU.mult)),
                reads=["vecs"], writes=[("T_GA", sub)])
            P.op("dve", (I("tensor_scalar", out=T_BA[:, sub, :], in0=vecs[:, o_b + sub * 8:o_b + sub * 8 + 8], scalar1=fac, scalar2=None, op0=ALU.mult)),
                reads=["vecs"], writes=[("T_BA", sub)])
            if not is_last:
                nx = sublayers[idx + 1]
                for s_ in range(nseq):
                    P.op("dve", (I("tensor_tensor", out=T_HG[:, sub, s_, :], in0=T_S1[:, nx, s_, :], in1=vecs[:, o_g + sub * 8:o_g + sub * 8 + 8], op=ALU.mult)),
                        reads=["vecs", ("T_S1", nx)], writes=[("T_HG", sub, s_)])
                    P.op("dve", (I("tensor_tensor", out=T_HB[:, sub, s_, :], in0=T_S1[:, nx, s_, :], in1=vecs[:, o_b + sub * 8:o_b + sub * 8 + 8], op=ALU.mult)),
                        reads=["vecs", ("T_S1", nx)], writes=[("T_HB", sub, s_)])
                    P.op("dve", (I("tensor_tensor", out=T_HB[:, sub, s_, :], in0=T_HB[:, sub, s_, :], in1=T_SH[:, nx, s_, :], op=ALU.add)),
                        reads=[("T_SH", nx), ("T_HB", sub, s_)], writes=[("T_HB", sub, s_)])
        o_lb = L["lb"][0]
        lb0 = vecs[:, o_lb:o_lb + 8]
        lb1 = vecs[:, o_lb + 8:o_lb + 16]
        tm = T_tmp[:, 0:8]
        e0 = T_tmp[:, 8:16]
        e1 = T_tmp[:, 16:24]
        ssum = T_tmp[:, 24:32]
        rs_ = T_tmp[:, 32:40]
        s0 = T_tmp[:, 40:48]
        s1 = T_tmp[:, 48:56]
        cum = T_tmp[:, 56:64]
        P.op("dve", I("tensor_tensor", out=tm, in0=lb0, in1=lb1, op=ALU.max), reads=["vecs"], writes=["tt0"])
        P.op("dve", I("tensor_tensor", out=e0, in0=lb0, in1=tm, op=ALU.subtract), reads=["vecs", "tt0"], writes=["tt1"])
        P.op("dve", I("tensor_tensor", out=e1, in0=lb1, in1=tm, op=ALU.subtract), reads=["vecs", "tt0"], writes=["tt2"])
        P.op("act", I("activation", out=T_tmp[:, 8:24], in_=T_tmp[:, 8:24], func=AF.Exp), reads=["tt1", "tt2"], writes=["tt3"])
        P.op("dve", I("tensor_tensor", out=ssum, in0=e0, in1=e1, op=ALU.add), reads=["tt3"], writes=["tt4"])
        P.op("dve", I("reciprocal", out=rs_, in_=ssum), reads=["tt4"], writes=["tt5"])
        P.op("dve", I("tensor_tensor", out=s0, in0=e0, in1=rs_, op=ALU.mult), reads=["tt3", "tt5"], writes=["tt6"])
        P.op("dve", I("tensor_tensor", out=s1, in0=e1, in1=rs_, op=ALU.mult), reads=["tt3", "tt5"], writes=["tt7"])
        P.op("dve", I("tensor_tensor", out=cum, in0=s0, in1=s1, op=ALU.add), reads=["tt6", "tt7"], writes=["tt8"])
        P.op("dve", I("tensor_tensor", out=T_LB[:, 0, :], in0=s0, in1=s0, op=ALU.subtract), reads=["tt6"], writes=["lb_0"])
        P.op("dve", I("tensor_tensor", out=T_LB[:, 1, :], in0=cum, in1=s0, op=ALU.subtract), reads=["tt8", "tt6"], writes=["lb_1"])
        P.op("dve", I("tensor_scalar", out=T_OML[:].rearrange("p a b -> p (a b)"), in0=T_LB[:].rearrange("p a b -> p (a b)"),
                                              scalar1=-1.0, scalar2=1.0, op0=ALU.mult, op1=ALU.add),
             reads=["lb_0", "lb_1"], writes=["oml"])
        P.op("dve", I("tensor_scalar", out=T_NOML[:].rearrange("p a b -> p (a b)"), in0=T_LB[:].rearrange("p a b -> p (a b)"),
                                              scalar1=1.0, scalar2=-1.0, op0=ALU.mult, op1=ALU.add),
             reads=["lb_0", "lb_1"], writes=["noml"])

        def ln_chunk(sub, s_, tq, is_last):
            t0 = tq * 512
            b1, b2 = (2 * tq) % 8, (2 * tq + 1) % 8
            mean = A_lmean.ap
            var = A_lvar.ap
            rstd = A_lrstd.ap
            zs, ss = var, rstd
            for c in range(8):
                sq = A_lsq.ap[:, c % 2, :]
                P.op("act", (I("activation", out=sq, in_=xp[:, c, t0:t0 + 512], func=AF.Square)),
                     reads=xk(c, t0, t0 + 512), writes=[("ln_sq", c % 2)])
                if not LN_POOL:
                    P.op("pe", I("matmul", bank(b1), lhsT=ones_f, rhs=xp[:, c, t0:t0 + 512], start=(c == 0), stop=(c == 7)),
                         reads=xk(c, t0, t0 + 512) + ["cst"], writes=pk(b1))
                    P.op("pe", I("matmul", bank(b2), lhsT=ones_f, rhs=sq, start=(c == 0), stop=(c == 7)),
                         reads=[("ln_sq", c % 2), "cst"], writes=pk(b2))
                elif c == 1:
                    P.op("pool", I("tensor_tensor", out=zs, in0=xp[:, 0, t0:t0 + 512], in1=xp[:, 1, t0:t0 + 512], op=ALU.add),
                         reads=xk(0, t0, t0 + 512) + xk(1, t0, t0 + 512), writes=["ln_var"])
                    P.op("pool", I("tensor_tensor", out=ss, in0=A_lsq.ap[:, 0, :], in1=A_lsq.ap[:, 1, :], op=ALU.add),
                         reads=[("ln_sq", 0), ("ln_sq", 1)], writes=["ln_rstd"])
                elif c > 1:
                    P.op("pool", I("tensor_tensor", out=zs, in0=zs, in1=xp[:, c, t0:t0 + 512], op=ALU.add),
                         reads=xk(c, t0, t0 + 512) + ["ln_var"], writes=["ln_var"])
                    P.op("pool", I("tensor_tensor", out=ss, in0=ss, in1=sq, op=ALU.add),
                         reads=[("ln_sq", c % 2), "ln_rstd"], writes=["ln_rstd"])
            if LN_POOL:
                P.op("pe", I("matmul", bank(b1), lhsT=ones_f, rhs=zs, start=True, stop=True),
                     reads=["ln_var", "cst"], writes=pk(b1))
                P.op("pe", I("matmul", bank(b2), lhsT=ones_f, rhs=ss, start=True, stop=True),
                     reads=["ln_rstd", "cst"], writes=pk(b2))
            P.op("act", I("activation", out=mean, in_=bank(b1), func=AF.Copy, scale=1.0 / D),
                 reads=pk(b1), writes=["ln_mean"])
            P.op("dve", I("tensor_tensor", out=var, in0=mean, in1=mean, op=ALU.mult),
                 reads=["ln_mean"], writes=["ln_var"])
            P.op("dve", I("scalar_tensor_tensor", out=var, in0=bank(b2), scalar=1.0 / D, in1=var,
                                                         op0=ALU.mult, op1=ALU.subtract),
                 reads=pk(b2) + ["ln_var"], writes=["ln_var"])
            P.op("dve", I("tensor_scalar", out=var, in0=var, scalar1=LN_EPS, scalar2=None, op0=ALU.add),
                 reads=["ln_var"], writes=["ln_var"])
            P.op("act", I("activation", out=rstd, in_=var, func=AF.Sqrt),
                 reads=["ln_var"], writes=["ln_rstd"])
            P.op("dve", I("reciprocal", out=rstd, in_=rstd), reads=["ln_rstd"], writes=["ln_rstd"])
            for c in range(8):
                t = A_lt.ap[:, c % 2, :]
                tk = ("ln_t", c % 2)
                P.op("dve", (I("tensor_tensor", out=t, in0=xp[:, c, t0:t0 + 512], in1=mean, op=ALU.subtract)),
                     reads=xk(c, t0, t0 + 512) + ["ln_mean"], writes=[tk])
                P.op("dve", (I("tensor_tensor", out=t, in0=t, in1=rstd, op=ALU.mult)),
                     reads=[tk, "ln_rstd"], writes=[tk])
                P.op("act", (I("activation", out=xp[:, c, t0:t0 + 512], in_=t, func=AF.Identity,
                                                              scale=T_GA[:, sub, c:c + 1], bias=T_BA[:, sub, c:c + 1])),
                     reads=[tk, ("T_GA", sub), ("T_BA", sub)], writes=xk(c, t0, t0 + 512))
                if not is_last:
                    P.op("act", (I("activation", out=hb[:, c, t0:t0 + 512], in_=t, func=AF.Identity,
                                                                  scale=T_HG[:, sub, s_, c:c + 1], bias=T_HB[:, sub, s_, c:c + 1])),
                         reads=[tk, ("T_HG", sub, s_), ("T_HB", sub, s_)], writes=hk(c, t0, t0 + 512))

        def out_proj(sub, s_, is_last):
            cnt = 0
            for hf in range(2):
                th = hf * 1024
                for sl in range(4):
                    w, wk = wget()
                    for mi in range(2):
                        m = sl * 2 + mi
                        sidx = cnt % 4
                        cnt += 1
                        y = slot2(sidx)
                        for tc in range(2):
                            for kc in range(8):
                                P.op("pe", I("matmul", y[:, tc * 512:(tc + 1) * 512],
                                             lhsT=w[:, kc * 256 + mi * 128: kc * 256 + mi * 128 + 128],
                                             rhs=hb[:, kc, th + tc * 512: th + tc * 512 + 512],
                                             start=(kc == 0), stop=(kc == 7)),
                                     reads=[wk] + hk(kc, th + tc * 512, th + tc * 512 + 512),
                                     writes=pk(2 * sidx + tc))
                        P.op("dve", I("scalar_tensor_tensor", out=xp[:, m, th:th + 1024], in0=y,
                                      scalar=T_GATE1[:, sub, s_, m:m + 1], in1=xp[:, m, th:th + 1024],
                                      op0=ALU.mult, op1=ALU.add),
                             reads=pk(2 * sidx, 2 * sidx + 1) + [("T_GATE1", sub)] + xk(m, th, th + 1024),
                             writes=xk(m, th, th + 1024))
                    if LN_DEFER and hf == 1 and sl in (0, 1) and not DEBUG_STOP:
                        ln_chunk(sub, s_, sl, is_last)
                if (not LN_DEFER) and hf == 0 and not DEBUG_STOP:
                    ln_chunk(sub, s_, 0, is_last)
                    ln_chunk(sub, s_, 1, is_last)
            if not DEBUG_STOP:
                ln_chunk(sub, s_, 2, is_last)
                ln_chunk(sub, s_, 3, is_last)

        def ffn(sub, s_, is_last):
            for half in range(2):
                th = half * 1024
                for jj in range(22):
                    w, wk = wget()
                    sg_i, su_i = (2 * jj) % 4, (2 * jj + 1) % 4
                    g_ps, u_ps = slot2(sg_i), slot2(su_i)
                    for tc in range(2):
                        for kc in range(8):
                            rhs = hb[:, kc, th + tc * 512: th + tc * 512 + 512]
                            rk = [wk] + hk(kc, th + tc * 512, th + tc * 512 + 512)
                            P.op("pe", (I("matmul", g_ps[:, tc * 512:(tc + 1) * 512], lhsT=w[:, kc * 256: kc * 256 + 128], rhs=rhs,
                                start=(kc == 0), stop=(kc == 7))), reads=rk, writes=pk(2 * sg_i + tc))
                            P.op("pe", (I("matmul", u_ps[:, tc * 512:(tc + 1) * 512], lhsT=w[:, kc * 256 + 128: kc * 256 + 256], rhs=rhs,
                                start=(kc == 0), stop=(kc == 7))), reads=rk, writes=pk(2 * su_i + tc))
                    sgt = A_sgt.ap[:, jj % 2, :]
                    P.op("act", (I("activation", out=sgt, in_=g_ps, func=AF.Silu)),
                         reads=pk(2 * sg_i, 2 * sg_i + 1), writes=[("f_sgt", jj % 2)])
                    P.op("dve", (I("tensor_tensor", out=A_act.ap[:, jj, :], in0=u_ps, in1=sgt, op=ALU.mult)),
                        reads=pk(2 * su_i, 2 * su_i + 1) + [("f_sgt", jj % 2)], writes=[("f_act", jj)])
                    if LN_DEFER and half == 1 and jj in (1, 3):
                        ln_chunk(sub, s_, jj // 2, is_last)
                for m in range(8):
                    sidx = m % 4
                    y = slot2(sidx)
                    for kh in range(2):
                        w, wk = wget()
                        for tc in range(2):
                            for kk in range(11):
                                kc = kh * 11 + kk
                                P.op("pe", (I("matmul", y[:, tc * 512:(tc + 1) * 512], lhsT=w[:, kk * 128: kk * 128 + 128],
                                    rhs=A_act.ap[:, kc, tc * 512:(tc + 1) * 512],
                                    start=(kc == 0), stop=(kc == 21))),
                                    reads=[wk, ("f_act", kc)], writes=pk(2 * sidx + tc))
                    P.op("dve", (I("scalar_tensor_tensor", out=xp[:, m, th:th + 1024], in0=y, scalar=T_GATE1[:, sub, s_, m:m + 1],
                        in1=xp[:, m, th:th + 1024], op0=ALU.mult, op1=ALU.add)),
                        reads=pk(2 * sidx, 2 * sidx + 1) + [("T_GATE1", sub)] + xk(m, th, th + 1024),
                        writes=xk(m, th, th + 1024))
                if not LN_DEFER:
                    ln_chunk(sub, s_, 2 * half, is_last)
                    ln_chunk(sub, s_, 2 * half + 1, is_last)
                elif half == 1:
                    ln_chunk(sub, s_, 2, is_last)
                    ln_chunk(sub, s_, 3, is_last)

        def dbg_dump(items):
            for c, apx, r0, r1, keys in items:
                n = apx.shape[-1]
                P.op("dve", I("tensor_copy", out=xp[r0:r1, c, 0:n], in_=apx), reads=keys, writes=xk(c, 0, S))

        def rope_tables(s_):
            cs = M_cs.ap
            posi = M_posi.ap.bitcast(I32)
            ki = M_ki.ap.bitcast(I32)
            kf = M_kf.ap
            P.dma("sp", I("dma_start", out=posi, in_=pos_in[s_]), "ld_pos", writes=["m_posi"])
            P.op("dve", I("tensor_copy", out=cs, in_=posi), reads=["m_posi"], writes=["m_cs"])
            P.op("dve", I("tensor_scalar", out=cs, in0=cs, scalar1=vcol("invf", 0), scalar2=None, op0=ALU.mult),
                 reads=["m_cs", "vecs"], writes=["m_cs"])
            P.op("dve", I("tensor_scalar", out=cs, in0=cs, scalar1=vcol("phase", 0), scalar2=None, op0=ALU.add),
                 reads=["m_cs", "vecs"], writes=["m_cs"])
            P.op("dve", I("tensor_scalar", out=kf, in0=cs, scalar1=1.0 / (2 * math.pi), scalar2=None, op0=ALU.mult),
                 reads=["m_cs"], writes=["m_kf"])
            P.op("dve", I("tensor_copy", out=ki, in_=kf), reads=["m_kf"], writes=["m_ki"])
            P.op("dve", I("tensor_copy", out=kf, in_=ki), reads=["m_ki"], writes=["m_kf"])
            P.op("dve", I("scalar_tensor_tensor", out=cs, in0=kf, scalar=-2.0 * math.pi, in1=cs,
                          op0=ALU.mult, op1=ALU.add),
                 reads=["m_kf", "m_cs"], writes=["m_cs"])
            for _ in range(2):
                P.op("dve", I("tensor_scalar", out=kf, in0=cs, scalar1=math.pi, scalar2=-2.0 * math.pi,
                              op0=ALU.is_gt, op1=ALU.mult),
                     reads=["m_cs"], writes=["m_kf"])
                P.op("dve", I("tensor_tensor", out=cs, in0=cs, in1=kf, op=ALU.add),
                     reads=["m_cs", "m_kf"], writes=["m_cs"])
                P.op("dve", I("tensor_scalar", out=kf, in0=cs, scalar1=-math.pi, scalar2=2.0 * math.pi,
                              op0=ALU.is_lt, op1=ALU.mult),
                     reads=["m_cs"], writes=["m_kf"])
                P.op("dve", I("tensor_tensor", out=cs, in0=cs, in1=kf, op=ALU.add),
                     reads=["m_cs", "m_kf"], writes=["m_cs"])
            P.op("dve", I("tensor_scalar", out=cs, in0=cs, scalar1=math.pi, scalar2=-math.pi,
                          op0=ALU.min, op1=ALU.max),
                 reads=["m_cs"], writes=["m_cs"])
            P.op("act", I("activation", out=cs, in_=cs, func=AF.Sin), reads=["m_cs"], writes=["m_cs"])
            P.op("dve", I("tensor_scalar", out=cs, in0=cs, scalar1=vcol("sgn", 0), scalar2=None, op0=ALU.mult),
                 reads=["m_cs", "vecs"], writes=["m_cs"])

        def mla(sub, s_, j, is_last):
            rope_tables(s_)
            cs = M_cs.ap
            rq = M_rq.ap
            gq0 = L["gq"][0] + j * 6
            gkv0 = L["gkv"][0] + j * 2
            if DEBUG_STOP == 1:
                dbg_dump([(0, M_cs.ap, 0, 128, ["m_cs"])])
                return
            for hf in range(2):
                th = hf * 1024
                def emit_ssq_q(mm):
                    sqm = M_sq.ap[:, mm % 2, :]
                    for tc in range(2):
                        P.op("pe", I("matmul", bank(6 + tc), lhsT=ones_f, rhs=sqm[:, tc * 512:(tc + 1) * 512],
                                     start=(mm == 0), stop=(mm == 5)),
                             reads=[("m_sq", mm % 2), "cst"], writes=pk(6 + tc))

                for sl in range(3):
                    w, wk = wget()
                    for mi in range(2):
                        m = sl * 2 + mi
                        sidx = m % 3
                        qps = slot2(sidx)
                        for tc in range(2):
                            for kc in range(8):
                                P.op("pe", (I("matmul", qps[:, tc * 512:(tc + 1) * 512],
                                    lhsT=w[:, kc * 256 + mi * 128: kc * 256 + mi * 128 + 128],
                                    rhs=hb[:, kc, th + tc * 512: th + tc * 512 + 512],
                                    start=(kc == 0), stop=(kc == 7))),
                                    reads=[wk] + hk(kc, th + tc * 512, th + tc * 512 + 512), writes=pk(2 * sidx + tc))
                        P.op("act", (I("activation", out=M_qg.ap[:, m, th:th + 1024], in_=qps, func=AF.Identity, scale=vecs[:, gq0 + m: gq0 + m + 1])),
                            reads=pk(2 * sidx, 2 * sidx + 1) + ["vecs"], writes=[("m_qg", m, hf)])
                        sq = M_sq.ap[:, m % 2, :]
                        P.op("act", (I("activation", out=sq, in_=qps, func=AF.Square)),
                             reads=pk(2 * sidx, 2 * sidx + 1), writes=[("m_sq", m % 2)])
                        if m >= 1:
                            emit_ssq_q(m - 1)

                def finish_q():
                    emit_ssq_q(5)
                    sd = M_sd.ap
                    P.op("dve", I("tensor_scalar", out=sd, in0=slot2(3), scalar1=1.0 / QL, scalar2=RMS_EPS,
                                  op0=ALU.mult, op1=ALU.add), reads=pk(6, 7), writes=["m_sd"])
                    P.op("act", I("activation", out=sd, in_=sd, func=AF.Sqrt), reads=["m_sd"], writes=["m_sd"])
                    P.op("dve", I("reciprocal", out=rq[:, th:th + 1024], in_=sd), reads=["m_sd"], writes=[("m_rq", hf)])
                w, wk = wget()
                for m in range(2):
                    sidx = m
                    kps = slot2(sidx)
                    for tc in range(2):
                        for kc in range(8):
                            P.op("pe", (I("matmul", kps[:, tc * 512:(tc + 1) * 512],
                                lhsT=w[:, kc * 256 + m * 128: kc * 256 + m * 128 + 128],
                                rhs=hb[:, kc, th + tc * 512: th + tc * 512 + 512],
                                start=(kc == 0), stop=(kc == 7))),
                                reads=[wk] + hk(kc, th + tc * 512, th + tc * 512 + 512), writes=pk(2 * sidx + tc))
                    P.op("dve", (I("tensor_copy", out=M_kvlat.ap[:, m, :], in_=kps)),
                         reads=pk(2 * sidx, 2 * sidx + 1), writes=[("m_kvlat", m)])
                    if m == 0:
                        finish_q()
                    sq = M_sq.ap[:, m % 2, :]
                    P.op("act", (I("activation", out=sq, in_=kps, func=AF.Square)),
                         reads=pk(2 * sidx, 2 * sidx + 1), writes=[("m_sq", m % 2)])
                    if m == 1:
                        for mm in range(2):
                            sqm = M_sq.ap[:, mm % 2, :]
                            for tc in range(2):
                                P.op("pe", I("matmul", bank(4 + tc), lhsT=ones_f, rhs=sqm[:, tc * 512:(tc + 1) * 512],
                                             start=(mm == 0), stop=(mm == 1)),
                                     reads=[("m_sq", mm % 2), "cst"], writes=pk(4 + tc))
                rkv = M_rkv.ap
                P.op("dve", I("tensor_scalar", out=rkv, in0=slot2(2), scalar1=1.0 / KVL, scalar2=RMS_EPS,
                                                      op0=ALU.mult, op1=ALU.add),
                     reads=pk(4, 5), writes=["m_rkv"])
                P.op("act", I("activation", out=rkv, in_=rkv, func=AF.Sqrt), reads=["m_rkv"], writes=["m_rkv"])
                P.op("dve", I("reciprocal", out=rkv, in_=rkv), reads=["m_rkv"], writes=["m_rkv"])
                for m in range(2):
                    P.op("dve", (I("scalar_tensor_tensor", out=M_kvn.ap[:, m, th:th + 1024], in0=M_kvlat.ap[:, m, :], scalar=vecs[:, gkv0 + m: gkv0 + m + 1],
                        in1=rkv, op0=ALU.mult, op1=ALU.mult)),
                        reads=[("m_kvlat", m), "m_rkv", "vecs"], writes=[("m_kvn", m, hf)])
                if DEBUG_STOP == 6:
                    return
                w, wk = wget()
                rps = slot2(3)
                for tc in range(2):
                    for kc in range(8):
                        P.op("pe", (I("matmul", rps[:, tc * 512:(tc + 1) * 512], lhsT=w[:, kc * 128: kc * 128 + 128],
                            rhs=hb[:, kc, th + tc * 512: th + tc * 512 + 512], start=(kc == 0), stop=(kc == 7))),
                            reads=[wk] + hk(kc, th + tc * 512, th + tc * 512 + 512), writes=pk(6 + tc))
                if DEBUG_STOP == 7:
                    return
                ta = M_tmpa.ap
                P.op("dve", (I("tensor_tensor", out=ta[0:64, :], in0=rps[0:64, :], in1=cs[0:64, th:th + 1024], op=ALU.mult)),
                     reads=pk(6, 7) + ["m_cs"], writes=["m_tmpa"])
                if DEBUG_STOP == 8:
                    return
                tb = M_tmpb.ap
                P.op("dve", (I("tensor_tensor", out=tb[0:64, :], in0=rps[64:128, :], in1=cs[64:128, th:th + 1024], op=ALU.mult)),
                     reads=pk(6, 7) + ["m_cs"], writes=["m_tmpb"])
                if DEBUG_STOP == 9:
                    return
                P.op("dve", (I("tensor_tensor", out=M_krope.ap[64:96, th:th + 1024], in0=ta[0:32, :],
                                                              in1=tb[0:32, :], op=ALU.add)),
                     reads=["m_tmpa", "m_tmpb"], writes=[("m_krope", hf)])
            P.op("dve", I("tensor_tensor", out=cs, in0=cs, in1=rq, op=ALU.mult),
                 reads=["m_cs", ("m_rq", 0), ("m_rq", 1)], writes=["m_cs"])
            if DEBUG_STOP == 2:
                dbg_dump([(0, M_qg.ap[:, 0, :], 0, 128, [("m_qg", 0, 0), ("m_qg", 0, 1)]),
                          (1, M_kvn.ap[:, 0, :], 0, 128, [("m_kvn", 0, 0), ("m_kvn", 0, 1)]),
                          (2, M_rq.ap, 0, 128, [("m_rq", 0), ("m_rq", 1)]),
                          (3, M_krope.ap[64:96, :], 64, 96, [("m_krope", 0), ("m_krope", 1)]),
                          (4, M_cs.ap, 0, 128, ["m_cs"])])
                return
            P.op("dve", I("memset", M_vaug.ap[:, :, :, 64:128], 1.0), writes=["m_vaug"])
            blk = 0
            for p in range(8):
                w, wk = wget()
                for hf in range(2):
                    th = hf * 1024
                    sa, sbq = 2 * hf, 2 * hf + 1
                    A_ps, B_ps = slot2(sa), slot2(sbq)
                    for tc in range(2):
                        for kc in range(6):
                            rhs = M_qg.ap[:, kc, th + tc * 512: th + tc * 512 + 512]
                            rk = [wk, ("m_qg", kc, hf)]
                            P.op("pe", I("matmul", A_ps[:, tc * 512:(tc + 1) * 512], lhsT=w[:, kc * 256: kc * 256 + 128], rhs=rhs,
                                         start=(kc == 0), stop=(kc == 5)), reads=rk, writes=pk(2 * sa + tc))
                            P.op("pe", I("matmul", B_ps[:, tc * 512:(tc + 1) * 512], lhsT=w[:, kc * 256 + 128: kc * 256 + 256], rhs=rhs,
                                         start=(kc == 0), stop=(kc == 5)), reads=rk, writes=pk(2 * sbq + tc))
                for hf in range(2):
                    th = hf * 1024
                    sa, sbq = 2 * hf, 2 * hf + 1
                    A_ps, B_ps = slot2(sa), slot2(sbq)
                    for hh in range(2):
                        P.op("dve", I("tensor_tensor", out=M_qk.ap[0:64, hh, th:th + 1024], in0=A_ps[hh * 64:hh * 64 + 64, :],
                                      in1=rq[hh * 64:hh * 64 + 64, th:th + 1024], op=ALU.mult),
                             reads=pk(2 * sa, 2 * sa + 1) + [("m_rq", hf)], writes=[("m_qk", hh, hf, "n")])
                    ta = M_tmpa.ap
                    P.op("dve", I("tensor_tensor", out=ta[0:64, :], in0=B_ps[0:64, :], in1=cs[0:64, th:th + 1024], op=ALU.mult),
                         reads=pk(2 * sbq, 2 * sbq + 1) + ["m_cs"], writes=["m_tmpa"])
                    tb = M_tmpb.ap
                    P.op("dve", I("tensor_tensor", out=tb[0:64, :], in0=B_ps[64:128, :], in1=cs[64:128, th:th + 1024], op=ALU.mult),
                         reads=pk(2 * sbq, 2 * sbq + 1) + ["m_cs"], writes=["m_tmpb"])
                    for hh in range(2):
                        P.op("dve", I("tensor_tensor", out=M_qk.ap[64:96, hh, th:th + 1024],
                                      in0=ta[hh * 32:hh * 32 + 32, :], in1=tb[hh * 32:hh * 32 + 32, :], op=ALU.add),
                             reads=["m_tmpa", "m_tmpb"], writes=[("m_qk", hh, hf, "r")])
                for hf in range(2):
                    th = hf * 1024
                    sk, sv = 2 * hf, 2 * hf + 1
                    K_ps, V_ps = slot2(sk), slot2(sv)
                    for tc in range(2):
                        for kc in range(2):
                            rhs = M_kvn.ap[:, kc, th + tc * 512: th + tc * 512 + 512]
                            P.op("pe", I("matmul", K_ps[:, tc * 512:(tc + 1) * 512], lhsT=w[:, 1536 + kc * 256: 1536 + kc * 256 + 128], rhs=rhs,
                                         start=(kc == 0), stop=(kc == 1)), reads=[wk, ("m_kvn", kc, hf)], writes=pk(2 * sk + tc))
                    for tt in range(8):
                        for kc in range(2):
                            P.op("pe", I("matmul", V_ps[:, tt * 128:(tt + 1) * 128],
                                         lhsT=M_kvn.ap[:, kc, th + tt * 128: th + tt * 128 + 128],
                                         rhs=w[:, 1536 + kc * 256 + 128: 1536 + kc * 256 + 256],
                                         start=(kc == 0), stop=(kc == 1)),
                                 reads=[wk, ("m_kvn", kc, hf)], writes=pk(2 * sv + tt // 4))
                for hf in range(2):
                    th = hf * 1024
                    sk, sv = 2 * hf, 2 * hf + 1
                    K_ps, V_ps = slot2(sk), slot2(sv)
                    for hh in range(2):
                        P.op("act", I("activation", out=M_qk.ap[0:64, 2 + hh, th:th + 1024], in_=K_ps[hh * 64:hh * 64 + 64, :], func=AF.Copy),
                             reads=pk(2 * sk, 2 * sk + 1), writes=[("m_qk", 2 + hh, hf, "n")])
                        P.op("act", I("activation", out=M_qk.ap[64:96, 2 + hh, th:th + 1024], in_=M_krope.ap[64:96, th:th + 1024], func=AF.Copy),
                             reads=[("m_krope", hf)], writes=[("m_qk", 2 + hh, hf, "r")])
                        P.op("act", I("activation", out=M_vaug.ap[:, hh, hf * 8:(hf + 1) * 8, 0:64],
                                      in_=V_ps.rearrange("p (t c) -> p t c", c=128)[:, :, hh * 64:hh * 64 + 64], func=AF.Copy),
                             reads=pk(2 * sv, 2 * sv + 1), writes=[("m_vaug", hh, hf)])
                if DEBUG_STOP == 3:
                    ks = lambda i: [("m_qk", i, 0, "n"), ("m_qk", i, 0, "r"), ("m_qk", i, 1, "n"), ("m_qk", i, 1, "r")]
                    dbg_dump([(0, M_qk.ap[0:96, 0, :], 0, 96, ks(0)), (1, M_qk.ap[0:96, 1, :], 0, 96, ks(1)),
                              (2, M_qk.ap[0:96, 2, :], 0, 96, ks(2)), (3, M_qk.ap[0:96, 3, :], 0, 96, ks(3)),
                              (4, M_vaug.ap[:, 0, :, :].rearrange("p a b -> p (a b)"), 0, 128, [("m_vaug", 0, 0), ("m_vaug", 0, 1), "m_vaug"]),
                              (5, M_vaug.ap[:, 1, :, :].rearrange("p a b -> p (a b)"), 0, 128, [("m_vaug", 1, 0), ("m_vaug", 1, 1), "m_vaug"])])
                    return
                blocks = []
                for qgrp in ((0, 1), (2, 3)):
                    for hh in range(2):
                        for qc in qgrp:
                            nkb = 4 * qc + 4
                            for kb in range(nkb):
                                jd = kb - 4 * qc
                                q0 = qc * 512 + max(jd, 0) * 128
                                blocks.append(dict(hh=hh, qc=qc, kb=kb, jd=jd, q0=q0, N=qc * 512 + 512 - q0,
                                                   first=(kb == 0), last=(kb == nkb - 1)))
                LA = 2

                def emit_st(bd, sb_i):
                    hh, qc, kb, jd, q0, N = bd["hh"], bd["qc"], bd["kb"], bd["jd"], bd["q0"], bd["N"]
                    Qh = M_qk.ap[0:96, hh, :]
                    Kh = M_qk.ap[0:96, 2 + hh, :]
                    ST = bank(sb_i)
                    hq, hk_ = qc // 2, kb // 8
                    P.op("pe", I("matmul", ST[:, 0:N], lhsT=Kh[:, kb * 128:(kb + 1) * 128], rhs=Qh[:, q0:q0 + N],
                                 start=True, stop=(jd < 0)),
                         reads=[("m_qk", hh, hq, "n"), ("m_qk", hh, hq, "r"), ("m_qk", 2 + hh, hk_, "n"), ("m_qk", 2 + hh, hk_, "r")],
                         writes=pk(sb_i))
                    if jd >= 0:
                        P.op("pe", I("matmul", ST[:, 0:128], lhsT=ident_bf[:], rhs=maskT_bf[:], start=False, stop=True),
                             reads=["ident_bf", "maskT_bf"], writes=pk(sb_i))
                    PT = M_pt.ap[:, sb_i, :]
                    P.op("act", I("activation", out=PT[:, 0:N], in_=ST[:, 0:N], func=AF.Exp, scale=ATT_SCALE),
                         reads=pk(sb_i), writes=[("m_pt", sb_i)])

                def emit_pv(bd, sb_i):
                    hh, qc, kb, q0, N = bd["hh"], bd["qc"], bd["kb"], bd["q0"], bd["N"]
                    h = 2 * p + hh
                    ob = 4 + (qc % 2)
                    O = bank(ob)
                    PT = M_pt.ap[:, sb_i, :]
                    P.op("pe", I("matmul", O[:, q0 - qc * 512: 512], lhsT=M_vaug.ap[:, hh, kb, :], rhs=PT[:, 0:N],
                                 start=bd["first"], stop=bd["last"]),
                         reads=[("m_pt", sb_i), ("m_vaug", hh, kb // 8), "m_vaug"], writes=pk(ob))
                    if bd["last"]:
                        rcp = M_rcp.ap
                        P.op("dve", I("reciprocal", out=rcp[64:128, :], in_=O[64:128, :]), reads=pk(ob), writes=["m_rcp"])
                        P.op("dve", I("tensor_tensor", out=hb[(h % 2) * 64:(h % 2) * 64 + 64, h // 2, qc * 512:(qc + 1) * 512],
                                      in0=O[0:64, :], in1=rcp[64:128, :], op=ALU.mult),
                             reads=pk(ob) + ["m_rcp"], writes=hk(h // 2, qc * 512, qc * 512 + 512))

                nb = len(blocks)
                base = blk
                for r in range(nb + LA):
                    if r < nb:
                        emit_st(blocks[r], (base + r) % 4)
                    if r >= LA:
                        emit_pv(blocks[r - LA], (base + r - LA) % 4)
                blk += nb
            if DEBUG_STOP == 4:
                dbg_dump([(c, hb[:, c, :], 0, 128, hk(c, 0, S)) for c in range(8)])
                return
            out_proj(sub, s_, is_last)

        def hgrn(sub, s_, j, is_last):
            T0, T1, T2, T3 = [g.ap for g in G_T]
            kT0, kT1, kT2, kT3 = ["g_T0", "g_T1", "g_T2", "g_T3"]
            smask = G_smask.ap
            P.op("dve", I("memset", smask, 1.0), writes=["g_smask"])
            P.op("dve", I("memset", smask.rearrange("p (n c) -> p n c", c=64)[:, :, 0:1], 0.0),
                 writes=["g_smask"])
            gn = vcol("gn", j)
            ebl = G_ebl2.ap
            sbf_keys = [("g_sbf", i) for i in range(5)]

            def stage_a1(hd):
                w0, wk0 = wget()
                for which, (coff, dst, dkey, func) in enumerate(((0, T0, kT0, AF.Silu), (128, T1, kT1, AF.Sigmoid))):
                    for hf in range(2):
                        th = hf * 1024
                        sidx = (which * 2 + hf) % 4
                        pp = slot2(sidx)
                        for tc in range(2):
                            for kc in range(8):
                                P.op("pe", I("matmul", pp[:, tc * 512:(tc + 1) * 512],
                                             lhsT=w0[:, kc * 256 + coff: kc * 256 + coff + 128],
                                             rhs=hb[:, kc, th + tc * 512: th + tc * 512 + 512], start=(kc == 0), stop=(kc == 7)),
                                     reads=[wk0] + hk(kc, th + tc * 512, th + tc * 512 + 512), writes=pk(2 * sidx + tc))
                        P.op("act", I("activation", out=dst[:, th:th + 1024], in_=pp, func=func),
                             reads=pk(2 * sidx, 2 * sidx + 1), writes=[(dkey, hf)])

            def stage_a2(hd):
                w1, wk1 = wget()
                for hf in range(2):
                    th = hf * 1024
                    sidx = hf
                    vp = slot2(sidx)
                    for tt in range(8):
                        for kc in range(8):
                            P.op("pe", I("matmul", vp[:, tt * 128:(tt + 1) * 128],
                                         lhsT=hb[:, kc, th + tt * 128: th + tt * 128 + 128],
                                         rhs=w1[:, kc * 256: kc * 256 + 128], start=(kc == 0), stop=(kc == 7)),
                                 reads=[wk1] + hk(kc, th + tt * 128, th + tt * 128 + 128), writes=pk(2 * sidx + tt // 4))
                    P.op("act", I("activation", out=G_vtok.ap[:, hf * 8:(hf + 1) * 8, :],
                                  in_=vp.rearrange("p (t c) -> p t c", c=128), func=AF.Copy),
                         reads=pk(2 * sidx, 2 * sidx + 1), writes=[("g_vtok", hf)])
                for hf in range(2):
                    th = hf * 1024
                    sidx = 2 + hf
                    gp = slot2(sidx)
                    for tc in range(2):
                        for kc in range(8):
                            P.op("pe", I("matmul", gp[:, tc * 512:(tc + 1) * 512], lhsT=w1[:, kc * 256 + 128: kc * 256 + 256],
                                         rhs=hb[:, kc, th + tc * 512: th + tc * 512 + 512], start=(kc == 0), stop=(kc == 7)),
                                 reads=[wk1] + hk(kc, th + tc * 512, th + tc * 512 + 512), writes=pk(2 * sidx + tc))
                    P.op("act", I("activation", out=G_sg.ap[:, th:th + 1024], in_=gp, func=AF.Silu),
                         reads=pk(2 * sidx, 2 * sidx + 1), writes=[("g_sg", hf)])

            def stage_b(hd):
                lbc = T_LB[:, j, hd:hd + 1]
                omlc = T_OML[:, j, hd:hd + 1]
                nomlc = T_NOML[:, j, hd:hd + 1]
                P.op("act", I("activation", out=T3, in_=T1, func=AF.Identity, scale=nomlc, bias=omlc),
                     reads=[(kT1, 0), (kT1, 1), "oml", "noml"], writes=[kT3])
                P.op("act", I("activation", out=T2, in_=T1, func=AF.Ln, scale=omlc, bias=lbc),
                     reads=[(kT1, 0), (kT1, 1), "oml", "lb_0", "lb_1"], writes=[kT2])
                P.op("dve", I("tensor_tensor_scan", out=T1, data0=smask, data1=T2, initial=0.0, op0=ALU.mult, op1=ALU.add),
                     reads=[kT2, "g_smask"], writes=[(kT1, 0), (kT1, 1)])
                P.op("act", I("activation", out=ebl, in_=T1.rearrange("p (n c) -> p n c", c=64)[:, :, 63], func=AF.Exp),
                     reads=[(kT1, 0), (kT1, 1)], writes=["g_ebl2"])
                P.op("act", I("activation", out=T2, in_=T1, func=AF.Exp), reads=[(kT1, 0), (kT1, 1)], writes=[kT2])
                P.op("dve", I("tensor_tensor", out=G_qt.ap, in0=T0, in1=T2, op=ALU.mult),
                     reads=[(kT0, 0), (kT0, 1), kT2], writes=["g_qt"])
                P.op("act", I("activation", out=T0, in_=T1, func=AF.Exp, scale=-1.0),
                     reads=[(kT1, 0), (kT1, 1)], writes=[(kT0, 0), (kT0, 1)])
                P.op("dve", I("tensor_tensor", out=T3, in0=T3, in1=T0, op=ALU.mult),
                     reads=[kT3, (kT0, 0), (kT0, 1)], writes=[kT3])
                P.op("act", I("activation", out=G_kt.ap, in_=T3, func=AF.Copy), reads=[kT3], writes=["g_kt"])
                P.op("dve", I("tensor_tensor", out=G_kh.ap.rearrange("p (n c) -> p n c", c=64),
                              in0=T3.rearrange("p (n c) -> p n c", c=64),
                              in1=ebl.rearrange("p (n o) -> p n o", o=1).to_broadcast([128, 32, 64]), op=ALU.mult),
                     reads=[kT3, "g_ebl2"], writes=["g_kh"])

            def stage_c(hd):
                tps = slot2(0).bitcast(BF16)
                for tt in range(16):
                    P.op("pe", I("transpose", tps[:, tt * 128:(tt + 1) * 128], G_kh.ap[:, tt * 128:(tt + 1) * 128], ident_bf[:]),
                         reads=["g_kh", "ident_bf"], writes=pk(tt // 8))
                P.op("act", I("activation", out=G_khT.ap.rearrange("p a b -> p (a b)"), in_=tps, func=AF.Copy),
                     reads=pk(0, 1), writes=["g_khT"])
                U3 = G_u.ap.rearrange("p (v n) -> p n v", n=32)
                for g8 in range(4):
                    bE = 2 + 2 * (g8 % 2)
                    bO = bE + 1
                    for i8 in range(8):
                        n = g8 * 8 + i8
                        if n == 31:
                            continue
                        pb = (n % 2) * 64
                        bk = bO if (n % 2) else bE
                        P.op("pe", I("matmul", bank(bk, 128, (i8 // 2) * 128), lhsT=G_khT.ap[pb:pb + 64, n // 2, :],
                                     rhs=G_vtok.ap[pb:pb + 64, n // 2, :], start=True, stop=True),
                             reads=["g_khT", ("g_vtok", n // 16)], writes=pk(bk))
                    P.op("act", I("activation", out=U3[:, g8 * 8: g8 * 8 + 8: 2, :],
                                  in_=bank(bE, 512).rearrange("p (n v) -> p n v", v=128), func=AF.Copy),
                         reads=pk(bE), writes=[("g_u", 2 * g8)])
                    nn = 4 if g8 < 3 else 3
                    P.op("act", I("activation", out=U3[:, g8 * 8 + 1: g8 * 8 + 1 + 2 * nn: 2, :],
                                  in_=bank(bO, nn * 128).rearrange("p (n v) -> p n v", v=128), func=AF.Copy),
                         reads=pk(bO), writes=[("g_u", 2 * g8 + 1)])
                P.op("dve", I("memset", U3[:, 31:32, :], 0.0), writes=[("g_u", 8)])
                er = G_eblrep.ap.rearrange("p (v n) -> p v n", n=32)
                P.op("dve", I("tensor_copy", out=er, in_=ebl.rearrange("p (o n) -> p o n", o=1).to_broadcast([128, 32, 32])),
                     reads=["g_ebl2"], writes=["g_eblrep"])
                P.op("dve", I("memset", er[:, :, 0:1], 0.0), reads=["g_eblrep"], writes=["g_eblrep"])
                sbf_flat = G_sbf.ap.rearrange("p a b -> p (a b)")
                for vq in range(4):
                    P.op("dve", I("tensor_tensor_scan", out=sbf_flat[:, vq * 1024:(vq + 1) * 1024], data0=G_eblrep.ap,
                                  data1=G_u.ap[:, vq * 1024:(vq + 1) * 1024], initial=0.0, op0=ALU.mult, op1=ALU.add),
                         reads=["g_eblrep"] + [("g_u", g) for g in range(9)], writes=[("g_sbf", 1 + vq)])
                LA = 2
                sbf_vn = sbf_flat.rearrange("p (v n) -> p n v", n=32)

                def emit_at(jj):
                    ab = jj % 4
                    AT = bank(ab, 128)
                    P.op("pe", I("matmul", AT, lhsT=G_kt.ap[:, jj * 128:(jj + 1) * 128], rhs=G_qt.ap[:, jj * 128:(jj + 1) * 128],
                                 start=True, stop=True), reads=["g_kt", "g_qt"], writes=pk(ab))
                    P.op("dve", I("tensor_tensor", out=G_asb.ap[:, ab, :], in0=AT, in1=mask2_f, op=ALU.mult),
                         reads=pk(ab) + ["cst"], writes=[("g_asb", ab)])

                def emit_o(jj):
                    ab = jj % 4
                    hf, jt = jj // 8, jj % 8
                    osl = 2 + hf
                    oc = slot2(osl)[:, jt * 128:(jt + 1) * 128]
                    okey = pk(2 * osl + jt // 4)
                    P.op("pe", I("matmul", oc, lhsT=G_vtok.ap[:, jj, :], rhs=G_asb.ap[:, ab, :], start=True, stop=False),
                         reads=[("g_asb", ab), ("g_vtok", hf)], writes=okey)
                    for i2 in range(2):
                        n = 2 * jj + i2
                        if n == 0:
                            continue
                        P.op("pe", I("matmul", oc[:, i2 * 64:(i2 + 1) * 64], lhsT=sbf_vn[:, n - 1, :],
                                     rhs=G_qt.ap[:, n * 64:(n + 1) * 64], start=False, stop=(i2 == 1)),
                             reads=sbf_keys + ["g_qt"], writes=okey)

                for r in range(16 + LA):
                    if r < 16:
                        emit_at(r)
                    if r >= LA:
                        emit_o(r - LA)
                for hf in range(2):
                    th = hf * 1024
                    osl = 2 + hf
                    Ops = slot2(osl)
                    osq = G_osq.ap
                    P.op("act", I("activation", out=osq, in_=Ops, func=AF.Square),
                         reads=pk(2 * osl, 2 * osl + 1), writes=["g_osq"])
                    for tc in range(2):
                        P.op("pe", I("matmul", bank(tc), lhsT=ones_f, rhs=osq[:, tc * 512:(tc + 1) * 512], start=True, stop=True),
                             reads=["g_osq", "cst"], writes=pk(tc))
                    rstd = G_rstd.ap
                    P.op("act", I("activation", out=rstd, in_=slot2(0), func=AF.Sqrt, scale=1.0 / 128, bias=eps_rms),
                         reads=pk(0, 1) + ["eps_t"], writes=["g_rstd"])
                    P.op("dve", I("reciprocal", out=rstd, in_=rstd), reads=["g_rstd"], writes=["g_rstd"])
                    to = G_to.ap
                    P.op("dve", I("scalar_tensor_tensor", out=to, in0=Ops, scalar=gn, in1=rstd, op0=ALU.mult, op1=ALU.mult),
                         reads=pk(2 * osl, 2 * osl + 1) + ["g_rstd", "vecs"], writes=["g_to"])
                    P.op("dve", I("tensor_tensor", out=G_og.ap, in0=to, in1=G_sg.ap[:, th:th + 1024], op=ALU.mult),
                         reads=["g_to", ("g_sg", hf)], writes=["g_og"])
                    P.dma("sp", I("dma_start", out=ogd[hd, :, th:th + 1024], in_=G_og.ap), "st_og",
                          reads=["g_og"], writes=[("ogd", hd, hf)])

            stage_a1(0)
            for hd in range(8):
                stage_a2(hd)
                stage_b(hd)
                if hd + 1 < 8:
                    stage_a1(hd + 1)
                stage_c(hd)
            for c in range(8):
                P.dma("sp", I("dma_start", out=hb[:, c, :], in_=ogd[c]), "ld_og%d" % c,
                      reads=[("ogd", c, 0), ("ogd", c, 1)], writes=hk(c, 0, S))
            if DEBUG_STOP == 13:
                dbg_dump([(c, hb[:, c, :], 0, 128, hk(c, 0, S)) for c in range(8)])
                return
            out_proj(sub, s_, is_last)

        for s_ in range(nseq):
            for c in range(8):
                P.dma("sp", (I("dma_start", out=xp[:, c, :], in_=x_in[s_, :, c, :])), "ld_x%d" % c,
                      writes=xk(c, 0, S))
            first = sublayers[0]
            for c in range(8):
                P.op("act", (I("activation", out=hb[:, c, :], in_=xp[:, c, :], func=AF.Identity,
                                                         scale=T_S1[:, first, s_, c:c + 1], bias=T_SH[:, first, s_, c:c + 1])),
                     reads=xk(c, 0, S) + [("T_S1", first), ("T_SH", first)], writes=hk(c, 0, S))
                P.op("dve", (I("tensor_scalar", out=xp[:, c, :], in0=xp[:, c, :], scalar1=ALPHA, scalar2=None,
                                                            op0=ALU.mult)),
                     reads=xk(c, 0, S), writes=xk(c, 0, S))
            for idx, sub in enumerate(sublayers):
                is_last = (idx == len(sublayers) - 1)
                l, typ = sub // 2, sub % 2
                if typ == 1:
                    ffn(sub, s_, is_last)
                else:
                    if l % 2 == 0:
                        mla(sub, s_, l // 2, is_last)
                    else:
                        hgrn(sub, s_, l // 2, is_last)
            for c in range(8):
                P.dma("sp", (I("dma_start", out=y_out[s_, :, c, :], in_=xp[:, c, :])), "st_y%d" % c,
                      reads=xk(c, 0, S))
        assert DEBUG_STOP or wstate["next"] == len(wlist), (wstate, len(wlist))
        cnt = P.emit(nc)
    return nc, cnt, len(P.ops)


_CACHE = {}


def run(inputs, sublayers, nseq, ncores, x_override=None):
    shared = prep_shared(inputs)
    key = (tuple(sublayers), nseq)
    if key not in _CACHE:
        _CACHE[key] = build(list(sublayers), nseq)
    nc, cnt, nops = _CACHE[key]
    in_maps = []
    for core in range(ncores):
        m = dict(shared)
        inp2 = inputs if x_override is None else dict(inputs, x=x_override)
        m.update(prep_core(inp2, core * nseq, nseq))
        in_maps.append(m)
    res = run_bass_kernel_spmd(nc, in_maps, core_ids=list(range(ncores)))
    outs = []
    for core in range(ncores):
        y = res.results[core]["y_out"]
        for s_ in range(nseq):
            outs.append(np.ascontiguousarray(y[s_].transpose(1, 0, 2).reshape(D, S).T))
    return np.stack(outs, 0)


def kernel(**inputs):
    inputs = {k: np.asarray(v) for k, v in inputs.items()}
    out = run(inputs, list(range(8)), 2, NCORES)
    return out.astype(np.float32)
```

```python
import math
from contextlib import ExitStack

import numpy as np
import concourse.bass as bass
import concourse.mybir as mybir
from concourse.bass_utils import run_bass_kernel_spmd

F32 = mybir.dt.float32
BF16 = mybir.dt.bfloat16
I32 = mybir.dt.int32
AF = mybir.ActivationFunctionType
ALU = mybir.AluOpType

D = 1024
S = 2048
DEPTH = 4
NCORES = 8
H = 16
QL = 768
KVL = 256
DFF = 2816
ALPHA = (2.0 * DEPTH) ** 0.25
LN_EPS = 1e-5
RMS_EPS = 1e-6
ATT_SCALE = 96.0 ** -0.5
NSLOT = 4
DEBUG_STOP = 0
LN_POOL = False
LN_DEFER = True
SLOT = 2048
MASKNEG = -30000.0

ENGS = ("pe", "act", "dve", "pool", "sp")


def I(name, *args, **kw):
    def fn(e):
        return getattr(e, name)(*args, **kw)
    return fn


class Op:
    __slots__ = ("id", "eng", "fn", "deps", "is_dma", "dsem", "dcount", "signal", "count")


class Prog:
    def __init__(self, same_eng_sync=True):
        self.ops = []
        self.last_write = {}
        self.readers = {}
        self.dma_counts = {}
        self.same_eng_sync = same_eng_sync
        self.overlaps = {}
        self.touch = {}

    def declare_alias(self, name, others):
        for o in others:
            self.overlaps.setdefault(name, set()).add(o)
            self.overlaps.setdefault(o, set()).add(name)

    def _add(self, eng, fn, reads, writes, is_dma, dsem):
        op = Op()
        op.id = len(self.ops)
        op.eng = eng
        op.fn = fn
        op.is_dma = is_dma
        op.dsem = dsem
        op.signal = False
        op.count = 0
        op.dcount = 0
        if is_dma:
            self.dma_counts[dsem] = self.dma_counts.get(dsem, 0) + 16
            op.dcount = self.dma_counts[dsem]
        deps = set()
        names = set()
        for k in reads:
            names.add(k[0] if isinstance(k, tuple) else k)
            w = self.last_write.get(k)
            if w is not None:
                deps.add(w)
            if isinstance(k, tuple) and k[0] == "ps":
                for ek, r in self.readers.get(k, {}).items():
                    if ek != eng:
                        deps.add(r)
        for k in writes:
            names.add(k[0] if isinstance(k, tuple) else k)
            w = self.last_write.get(k)
            if w is not None:
                deps.add(w)
            for r in self.readers.get(k, {}).values():
                deps.add(r)
        rk = ("dma", op.id) if is_dma else eng
        for n in names:
            for o in self.overlaps.get(n, ()):
                t = self.touch.get(o)
                if t:
                    deps.update(t.values())
        for n in names:
            if n in self.overlaps:
                t = self.touch.setdefault(n, {})
                t[rk] = op.id
                if len(t) > 24:
                    ks = sorted((k for k in t if isinstance(k, tuple)), key=lambda k: t[k])
                    for k in ks[:-8]:
                        del t[k]
        deps.discard(op.id)
        op.deps = deps
        for k in reads:
            self.readers.setdefault(k, {})[rk] = op.id
        for k in writes:
            self.last_write[k] = op.id
            self.readers[k] = {}
        self.ops.append(op)
        return op

    def op(self, eng, fn, reads=(), writes=()):
        return self._add(eng, fn, tuple(reads), tuple(writes), False, None)

    def dma(self, eng, fn, dsem, reads=(), writes=()):
        return self._add(eng, fn, tuple(reads), tuple(writes), True, dsem)

    def emit(self, nc, final_wait_eng="sp"):
        ops = self.ops
        ses = self.same_eng_sync

        def skip(a, op):
            return a.eng == op.eng and (not op.is_dma) and (a.eng == "pe" or not ses)

        for op in ops:
            for d in op.deps:
                a = ops[d]
                if a.is_dma or skip(a, op):
                    continue
                a.signal = True
        cnt = {e: 0 for e in ENGS}
        for op in ops:
            if op.signal:
                cnt[op.eng] += 1
                op.count = cnt[op.eng]
        dsems = sorted(self.dma_counts.keys())
        with ExitStack() as st:
            esem = {e: st.enter_context(nc.semaphore("E_" + e)) for e in ENGS}
            dsem = {d: st.enter_context(nc.semaphore("D_" + str(d))) for d in dsems}
            block = st.enter_context(nc.Block())

            def run_stream(ename, eng):
                waited = {}
                for op in ops:
                    if op.eng != ename:
                        continue
                    need = {}
                    for d in op.deps:
                        a = ops[d]
                        if a.is_dma:
                            key = ("d", a.dsem)
                            val = a.dcount
                        else:
                            if skip(a, op):
                                continue
                            key = ("e", a.eng)
                            val = a.count
                        if val > need.get(key, 0):
                            need[key] = val
                    for key, val in need.items():
                        if waited.get(key, 0) >= val:
                            continue
                        waited[key] = val
                        s = dsem[key[1]] if key[0] == "d" else esem[key[1]]
                        eng.wait_ge(s, val)
                    ins = op.fn(eng)
                    if op.is_dma:
                        ins.then_inc(dsem[op.dsem], 16)
                    elif op.signal:
                        ins.then_inc(esem[ename], 1)
                if ename == final_wait_eng:
                    for d in dsems:
                        if waited.get(("d", d), 0) < self.dma_counts[d]:
                            eng.wait_ge(dsem[d], self.dma_counts[d])

            @block.tensor
            def _(e):
                run_stream("pe", e)

            @block.scalar
            def _(e):
                run_stream("act", e)

            @block.vector
            def _(e):
                run_stream("dve", e)

            @block.gpsimd
            def _(e):
                run_stream("pool", e)

            @block.sync
            def _(e):
                run_stream("sp", e)
        return cnt


def _kc_slab(w, cols):
    K = w.shape[0]
    sub = w[:, cols].reshape(K // 128, 128, len(cols))
    return np.ascontiguousarray(sub.transpose(1, 0, 2)).reshape(128, -1)


def _fm(v):
    return np.ascontiguousarray(v.reshape(-1, 128).T)


VEC_LAYOUT = {}


def _vec_layout():
    if VEC_LAYOUT:
        return VEC_LAYOUT
    off = 0
    for name, n in (("adab", 8 * 24), ("lng", 8 * 8), ("lnb", 8 * 8), ("gq", 2 * 6), ("gkv", 2 * 2),
                    ("lb", 2 * 8), ("gn", 2), ("invf", 1), ("sgn", 1), ("phase", 1)):
        VEC_LAYOUT[name] = (off, n)
        off += n
    VEC_LAYOUT["_total"] = (off, 0)
    return VEC_LAYOUT


def prep_shared(inp):
    a = {}
    f = np.float32
    ar = np.arange
    ada = np.empty((8, 12, 128, SLOT), f)
    for l in range(DEPTH):
        for s2 in range(2):
            w = inp["ada_w"][l, s2]
            for sl in range(12):
                ada[l * 2 + s2, sl] = _kc_slab(w, ar(sl * 256, sl * 256 + 256))
    a["w_ada"] = ada
    mi = np.empty((2, 5, 128, SLOT), f)
    mp = np.empty((2, 8, 128, SLOT), f)
    mo = np.empty((2, 4, 128, SLOT), f)
    for j in range(2):
        w = inp["mla_w_in"][j]
        for sl in range(4):
            mi[j, sl] = _kc_slab(w, ar(sl * 256, sl * 256 + 256))
        rm = ar(1024, 1056)
        rs = np.concatenate([ar(1040, 1056), ar(1024, 1040)])
        mi[j, 4] = 0
        mi[j, 4, :, :1024] = _kc_slab(w, np.concatenate([rm, rm, rs, rs]))
        wq = inp["mla_w_qb"][j]
        wkv = inp["mla_w_kvb"][j]
        for p in range(8):
            h0, h1 = 2 * p, 2 * p + 1
            nope = np.concatenate([ar(h0 * 96, h0 * 96 + 64), ar(h1 * 96, h1 * 96 + 64)])

            def rmain(h):
                return ar(h * 96 + 64, h * 96 + 96)

            def rswap(h):
                return np.concatenate([ar(h * 96 + 80, h * 96 + 96), ar(h * 96 + 64, h * 96 + 80)])
            qcols = np.concatenate([nope, rmain(h0), rmain(h1), rswap(h0), rswap(h1)])
            kn = np.concatenate([ar(h0 * 128, h0 * 128 + 64), ar(h1 * 128, h1 * 128 + 64)])
            vv = np.concatenate([ar(h0 * 128 + 64, h0 * 128 + 128), ar(h1 * 128 + 64, h1 * 128 + 128)])
            mp[j, p, :, :1536] = _kc_slab(wq, qcols)
            mp[j, p, :, 1536:] = _kc_slab(wkv, np.concatenate([kn, vv]))
        for sl in range(4):
            mo[j, sl] = _kc_slab(inp["mla_w_o"][j], ar(sl * 256, sl * 256 + 256))
    a["w_mla_in"] = mi
    a["w_mla_pair"] = mp
    a["w_mla_o"] = mo
    hi = np.empty((2, 8, 2, 128, SLOT), f)
    ho = np.empty((2, 4, 128, SLOT), f)
    for j in range(2):
        w = inp["hgrn_w_in"][j]
        for hd in range(8):
            c0 = ar(hd * 128, hd * 128 + 128)
            hi[j, hd, 0] = _kc_slab(w, np.concatenate([c0, 1024 + c0]))
            hi[j, hd, 1] = _kc_slab(w, np.concatenate([2048 + c0, 3072 + c0]))
        for sl in range(4):
            ho[j, sl] = _kc_slab(inp["hgrn_w_o"][j], ar(sl * 256, sl * 256 + 256))
    a["w_hgrn_in"] = hi
    a["w_hgrn_o"] = ho
    fi = np.empty((DEPTH, 22, 128, SLOT), f)
    fo = np.zeros((DEPTH, 8, 2, 128, SLOT), f)
    for l in range(DEPTH):
        w = inp["ffn_w_in"][l]
        for jj in range(22):
            c0 = ar(jj * 128, jj * 128 + 128)
            fi[l, jj] = _kc_slab(w, np.concatenate([c0, DFF + c0]))
        w2 = inp["ffn_w_out"][l]
        for m in range(8):
            full = _kc_slab(w2, ar(m * 128, m * 128 + 128))
            fo[l, m, 0, :, :1408] = full[:, :1408]
            fo[l, m, 1, :, :1408] = full[:, 1408:]
    a["w_ffn_in"] = fi
    a["w_ffn_out"] = fo
    L = _vec_layout()
    vec = np.zeros((128, L["_total"][0]), f)

    def put(name, arr):
        o, n = L[name]
        vec[:, o:o + n] = arr.reshape(128, n)
    put("adab", np.stack([_fm(inp["ada_b"][l, s2]) for l in range(DEPTH) for s2 in range(2)], 1))
    put("lng", np.stack([_fm(inp["ln_g"][l, s2]) for l in range(DEPTH) for s2 in range(2)], 1))
    put("lnb", np.stack([_fm(inp["ln_b"][l, s2]) for l in range(DEPTH) for s2 in range(2)], 1))
    put("gq", np.stack([_fm(inp["mla_q_norm"][j]) for j in range(2)], 1))
    put("gkv", np.stack([_fm(inp["mla_kv_norm"][j]) for j in range(2)], 1))
    put("lb", np.stack([_fm(inp["hgrn_lb"][j]) for j in range(2)], 1))
    put("gn", np.stack([inp["hgrn_g_norm"][j] for j in range(2)], 1))
    r = ar(128)
    inv_freq = (10000.0 ** (-np.arange(0, 32, 2, dtype=np.float32) / 32)).astype(f)
    put("invf", inv_freq[(r % 32) % 16])
    put("sgn", np.where(r < 64, 1.0, np.where((r % 32) < 16, -1.0, 1.0)).astype(f))
    put("phase", np.where(r < 64, np.pi / 2, 0.0).astype(f))
    a["vecs"] = vec
    cst = np.zeros((128, 4 * 128), f)
    cst[:, 0:128] = 1.0
    cst[:, 128:256] = np.eye(128, dtype=f)
    kk = r[:, None]
    qq = r[None, :]
    cst[:, 256:384] = np.where(qq >= kk, 0.0, MASKNEG)
    cst[:, 384:512] = ((kk // 64 == qq // 64) & (kk <= qq)).astype(f)
    a["consts"] = cst
    return a


def prep_core(inp, b0, nseq):
    f = np.float32
    xs = []
    for b in range(b0, b0 + nseq):
        xt = inp["x"][b].T.reshape(8, 128, S).transpose(1, 0, 2)
        xs.append(np.ascontiguousarray(xt))
    ct = np.stack([_fm(inp["c"][b]) for b in range(b0, b0 + nseq)], 2)
    pos = np.stack([np.broadcast_to(inp["positions"][b][None, :], (128, S)) for b in range(b0, b0 + nseq)], 0)
    return {"x_in": np.stack(xs, 0).astype(f), "c_in": np.ascontiguousarray(ct).astype(f),
            "pos_in": np.ascontiguousarray(pos).astype(np.int32)}


def build(sublayers, nseq, x_is_scaled_input=True):
    nc = bass.Bass("TRN2", target_bir_lowering=False)
    L = _vec_layout()
    NV = L["_total"][0]

    def din(name, shape, dt=F32):
        return nc.dram_tensor(name, list(shape), dt, kind="ExternalInput").ap()
    x_in = din("x_in", [nseq, 128, 8, S])
    c_in = din("c_in", [128, 8, nseq])
    pos_in = din("pos_in", [nseq, 128, S], I32)
    w_ada = din("w_ada", [8, 12, 128, SLOT])
    w_mla_in = din("w_mla_in", [2, 5, 128, SLOT])
    w_mla_pair = din("w_mla_pair", [2, 8, 128, SLOT])
    w_mla_o = din("w_mla_o", [2, 4, 128, SLOT])
    w_hgrn_in = din("w_hgrn_in", [2, 8, 2, 128, SLOT])
    w_hgrn_o = din("w_hgrn_o", [2, 4, 128, SLOT])
    w_ffn_in = din("w_ffn_in", [DEPTH, 22, 128, SLOT])
    w_ffn_out = din("w_ffn_out", [DEPTH, 8, 2, 128, SLOT])
    vecs_in = din("vecs", [128, NV])
    consts_in = din("consts", [128, 512])
    y_out = nc.dram_tensor("y_out", [nseq, 128, 8, S], F32, kind="ExternalOutput").ap()
    ogd = nc.dram_tensor("og_scratch", [8, 128, S], BF16, kind="Internal").ap()

    P = Prog()
    last_sub = sublayers[-1]
    with ExitStack() as st:
        def sb(name, shape, dt):
            return st.enter_context(nc.sbuf_tensor(name, list(shape), dt))
        xp = sb("xp", [128, 8, S], F32)
        hb = sb("hb", [128, 8, S], BF16)
        ring = sb("ring", [128, NSLOT, SLOT], BF16)
        vecs = sb("vecs_sb", [128, NV], F32)
        cst = sb("cst_sb", [128, 512], F32)
        ones_f = cst[:, 0:128]
        mask2_f = cst[:, 384:512]
        ones_bf = sb("ones_bf", [128, 128], BF16)
        ident_bf = sb("ident_bf", [128, 128], BF16)
        maskT_bf = sb("maskT_bf", [128, 128], BF16)
        modall = sb("modall", [128, 8, 24, nseq], F32)
        c_sb = sb("c_sb", [128, 8 * nseq], F32)
        sc_bf = sb("sc_bf", [128, 8, nseq], BF16)
        T_GATE1 = sb("T_GATE1", [128, 8, nseq, 8], F32)
        T_S1 = sb("T_S1", [128, 8, nseq, 8], F32)
        T_SH = sb("T_SH", [128, 8, nseq, 8], F32)
        T_GA = sb("T_GA", [128, 8, 8], F32)
        T_BA = sb("T_BA", [128, 8, 8], F32)
        T_HG = sb("T_HG", [128, 8, nseq, 8], F32)
        T_HB = sb("T_HB", [128, 8, nseq, 8], F32)
        T_H0 = sb("T_H0", [128, nseq, 8], F32)
        T_LB = sb("T_LB", [128, 2, 8], F32)
        T_OML = sb("T_OML", [128, 2, 8], F32)
        T_NOML = sb("T_NOML", [128, 2, 8], F32)
        T_tmp = sb("T_tmp", [128, 64], F32)
        eps_t = sb("eps_t", [128, 2], F32)
        eps_rms = eps_t[:, 0:1]
        remaining = nc.sbuf_bytes_remaining
        ARENA = (remaining - 256) // 4 * 4
        print('ARENA bytes', ARENA)
        arena = sb("arena", [128, ARENA // 4], F32)
        ps = st.enter_context(nc.psum_tensor("ps", [128, 4096], F32))

        class AV:
            def __init__(self, name, off, shape, dt):
                self.name = name
                esz = 2 if dt == BF16 else 4
                n = int(np.prod(shape))
                self.off = off
                self.end = off + n * esz
                assert self.end <= ARENA, (name, self.end, ARENA)
                v = arena[:, off // 4:(off + n * esz) // 4]
                if dt == BF16:
                    v = v.bitcast(BF16)
                if len(shape) == 2:
                    v = v.rearrange("p (a b) -> p a b", b=shape[1])
                elif len(shape) == 3:
                    v = v.rearrange("p (a b c) -> p a b c", b=shape[1], c=shape[2])
                self.ap = v

        avs = []

        def av(name, off, shape, dt):
            a = AV(name, off, shape, dt)
            for o in avs:
                if a.off < o.end and o.off < a.end:
                    P.declare_alias(a.name, [o.name])
            avs.append(a)
            return a

        K = 1024
        A_act = av("f_act", 0, [22, 1024], BF16)
        A_sgt = av("f_sgt", 45056, [2, 1024], BF16)
        LNO = 49152
        A_lsq = av("ln_sq", LNO, [2, 512], F32)
        A_lmean = av("ln_mean", LNO + 4096, [512], F32)
        A_lvar = av("ln_var", LNO + 6144, [512], F32)
        A_lrstd = av("ln_rstd", LNO + 8192, [512], F32)
        A_lt = av("ln_t", LNO + 10240, [2, 512], F32)
        M_qg = av("m_qg", 0, [6, S], BF16)
        M_kvn = av("m_kvn", 24576, [2, S], BF16)
        M_rq = av("m_rq", 32768, [S], F32)
        M_cs = av("m_cs", 40960, [S], F32)
        M_krope = av("m_krope", 63488, [S], BF16)
        M_qk = av("m_qk", 67584, [4, S], BF16)
        M_kvlat = av("m_kvlat", 67584, [2, 1024], F32)
        M_sq = av("m_sq", 75776, [2, 1024], F32)
        M_rkv = av("m_rkv", 49152, [1024], F32)
        M_sd = av("m_sd", 53248, [1024], F32)
        M_vaug = av("m_vaug", 49152, [2, 16, 128], BF16)
        M_pt = av("m_pt", 57344, [4, 512], BF16)
        M_rcp = av("m_rcp", 61440, [512], F32)
        M_tmpa = av("m_tmpa", 83968, [1024], F32)
        M_tmpb = av("m_tmpb", 57344, [1024], F32)
        M_posi = av("m_posi", 0, [S], F32)
        M_ki = av("m_ki", 24576, [S], F32)
        M_kf = av("m_kf", 32768, [S], F32)
        G_T = [av("g_T%d" % i, i * 8192, [S], F32) for i in range(4)]
        G_qt = av("g_qt", 32768, [S], BF16)
        G_kt = av("g_kt", 36864, [S], BF16)
        G_kh = av("g_kh", 40960, [S], BF16)
        G_khT = av("g_khT", 45056, [16, 128], BF16)
        G_vtok = av("g_vtok", 63488, [16, 128], BF16)
        G_sg = av("g_sg", 67584, [S], F32)
        G_sbf = av("g_sbf", 75776, [32, 128], BF16)
        G_smask = av("g_smask", 83968, [S], BF16)
        G_u = av("g_u", 16384, [4096], F32)
        G_eblrep = av("g_eblrep", 49152, [1024], F32)
        G_sout = av("g_sout", 53248, [1024], F32)
        G_asb = av("g_asb", 57344, [4, 128], BF16)
        G_osq = av("g_osq", 49152, [1024], F32)
        G_rstd = av("g_rstd", 53248, [1024], F32)
        G_to = av("g_to", 57344, [1024], F32)
        G_og = av("g_og", 61440, [1024], BF16)
        assert ARENA >= 88064 + 256, ARENA
        G_ebl2 = av("g_ebl2", 88064, [32], F32)

        def xk(c, t0, t1):
            return [("xp", c, q) for q in range(t0 // 512, (t1 + 511) // 512)]

        def hk(c, t0, t1):
            return [("hb", c, q) for q in range(t0 // 512, (t1 + 511) // 512)]

        def pk(*banks):
            return [("ps", b) for b in banks]

        def bank(b, n=512, off=0):
            return ps[:, b * 512 + off: b * 512 + off + n]

        def slot2(s):
            return ps[:, s * 1024:(s + 1) * 1024]

        wlist = []
        wstate = {"issued": 0, "next": 0}

        def wissue(upto):
            while wstate["issued"] < min(upto, len(wlist)):
                i = wstate["issued"]
                src, n = wlist[i]
                sl = i % NSLOT
                if n > 1024:
                    dst = ring[:, sl, 0:n].rearrange("p (a b) -> p a b", a=2)
                    srcv = src[:, 0:n].rearrange("p (a b) -> p a b", a=2)
                else:
                    dst = ring[:, sl, 0:n]
                    srcv = src[:, 0:n]
                P.dma("pool", (I("dma_start", out=dst, in_=srcv)), "ring%d" % sl,
                      writes=[("ring", sl)])
                wstate["issued"] += 1

        def wget():
            i = wstate["next"]
            wstate["next"] += 1
            wissue(i + NSLOT)
            sl = i % NSLOT
            return ring[:, sl, :], ("ring", sl)

        subs_needed = sorted(set(sublayers))
        for sub in subs_needed:
            for sl in range(12):
                wlist.append((w_ada[sub, sl], 2048))
        for s_ in range(nseq):
            for sub in sublayers:
                l, typ = sub // 2, sub % 2
                j = l // 2
                if typ == 1:
                    for half in range(2):
                        for jj in range(22):
                            wlist.append((w_ffn_in[l, jj], 2048))
                        for m in range(8):
                            for kh in range(2):
                                wlist.append((w_ffn_out[l, m, kh], 1408))
                elif l % 2 == 0:
                    for half in range(2):
                        for sl in range(4):
                            wlist.append((w_mla_in[j, sl], 2048))
                        wlist.append((w_mla_in[j, 4], 1024))
                    for p in range(8):
                        wlist.append((w_mla_pair[j, p], 2048))
                    for _h in range(2):
                        for sl in range(4):
                            wlist.append((w_mla_o[j, sl], 2048))
                else:
                    for hd in range(8):
                        wlist.append((w_hgrn_in[j, hd, 0], 2048))
                        wlist.append((w_hgrn_in[j, hd, 1], 2048))
                    for _h in range(2):
                        for sl in range(4):
                            wlist.append((w_hgrn_o[j, sl], 2048))

        P.dma("sp", I("dma_start", out=vecs[:], in_=vecs_in), "ld_v", writes=["vecs"])
        P.dma("sp", I("dma_start", out=cst[:], in_=consts_in), "ld_c", writes=["cst"])
        P.dma("sp", I("dma_start", out=c_sb[:], in_=c_in.rearrange("p a b -> p (a b)")), "ld_cc",
              writes=["c_sb"])
        wissue(NSLOT)
        P.op("dve", I("memset", eps_t[:, 0:1], RMS_EPS), writes=["eps_t"])
        P.op("dve", I("tensor_copy", out=ones_bf[:], in_=cst[:, 0:128]), reads=["cst"], writes=["ones_bf"])
        P.op("dve", I("tensor_copy", out=ident_bf[:], in_=cst[:, 128:256]), reads=["cst"], writes=["ident_bf"])
        P.op("dve", I("tensor_copy", out=maskT_bf[:], in_=cst[:, 256:384]), reads=["cst"], writes=["maskT_bf"])
        P.op("act", I("activation", out=sc_bf[:].rearrange("p a b -> p (a b)"), in_=c_sb[:], func=AF.Silu),
             reads=["c_sb"], writes=["sc_bf"])

        def vcol(name, i):
            o, n = L[name]
            return vecs[:, o + i:o + i + 1]

        def vrange(name, i0, n):
            o, _ = L[name]
            return vecs[:, o + i0:o + i0 + n]

        for sub in subs_needed:
            for sl in range(12):
                w, wk = wget()
                for mi in range(2):
                    m = sl * 2 + mi
                    for kc in range(8):
                        P.op("pe", (I("matmul", ps[:, 7 * 512 + m * nseq: 7 * 512 + (m + 1) * nseq],
                            lhsT=w[:, kc * 256 + mi * 128: kc * 256 + mi * 128 + 128],
                            rhs=sc_bf[:, kc, :], start=(kc == 0), stop=(kc == 7))),
                            reads=[wk, "sc_bf"], writes=pk(7))
            o_ab = L["adab"][0]
            P.op("dve", (I("tensor_tensor", out=modall[:, sub, :, :],
                in0=ps[:, 7 * 512: 7 * 512 + 24 * nseq].rearrange("p (a b) -> p a b", b=nseq),
                in1=vecs[:, o_ab + sub * 24: o_ab + sub * 24 + 24].rearrange("p (a b) -> p a b", b=1).to_broadcast([128, 24, nseq]),
                op=ALU.add)), reads=pk(7) + ["vecs"], writes=[("mod", sub)])
            P.op("dve", (I("tensor_copy", out=T_SH[:, sub, :, :], in_=modall[:, sub, 0:8, :].rearrange("p c s -> p s c"))),
                reads=[("mod", sub)], writes=[("T_SH", sub)])
            P.op("dve", (I("tensor_scalar", out=T_S1[:, sub, :, :], in0=modall[:, sub, 8:16, :].rearrange("p c s -> p s c"),
                scalar1=1.0, scalar2=None, op0=ALU.add)),
                reads=[("mod", sub)], writes=[("T_S1", sub)])
            P.op("dve", (I("tensor_scalar", out=T_GATE1[:, sub, :, :], in0=modall[:, sub, 16:24, :].rearrange("p c s -> p s c"),
                scalar1=1.0, scalar2=None, op0=ALU.add)),
                reads=[("mod", sub)], writes=[("T_GATE1", sub)])
        o_g, o_b = L["lng"][0], L["lnb"][0]
        for idx, sub in enumerate(sublayers):
            is_last = (idx == len(sublayers) - 1)
            fac = 1.0 if is_last else ALPHA
            P.op("dve", (I("tensor_scalar", out=T_GA[:, sub, :], in0=vecs[:, o_g + sub * 8:o_g + sub * 8 + 8], scalar1=fac, scalar2=None, op0=ALU.mult)),
                reads=["vecs"], writes=[("T_GA", sub)])
            P.op("dve", (I("tensor_scalar", out=T_BA[:, sub, :], in0=vecs[:, o_b + sub * 8:o_b + sub * 8 + 8], scalar1=fac, scalar2=None, op0=ALU.mult)),
                reads=["vecs"], writes=[("T_BA", sub)])
            if not is_last:
                nx = sublayers[idx + 1]
                for s_ in range(nseq):
                    P.op("dve", (I("tensor_tensor", out=T_HG[:, sub, s_, :], in0=T_S1[:, nx, s_, :], in1=vecs[:, o_g + sub * 8:o_g + sub * 8 + 8], op=ALU.mult)),
                        reads=["vecs", ("T_S1", nx)], writes=[("T_HG", sub, s_)])
                    P.op("dve", (I("tensor_tensor", out=T_HB[:, sub, s_, :], in0=T_S1[:, nx, s_, :], in1=vecs[:, o_b + sub * 8:o_b + sub * 8 + 8], op=ALU.mult)),
                        reads=["vecs", ("T_S1", nx)], writes=[("T_HB", sub, s_)])
                    P.op("dve", (I("tensor_tensor", out=T_HB[:, sub, s_, :], in0=T_HB[:, sub, s_, :], in1=T_SH[:, nx, s_, :], op=ALU.add)),
                        reads=[("T_SH", nx), ("T_HB", sub, s_)], writes=[("T_HB", sub, s_)])
        o_lb = L["lb"][0]
        lb0 = vecs[:, o_lb:o_lb + 8]
        lb1 = vecs[:, o_lb + 8:o_lb + 16]
        tm = T_tmp[:, 0:8]
        e0 = T_tmp[:, 8:16]
        e1 = T_tmp[:, 16:24]
        ssum = T_tmp[:, 24:32]
        rs_ = T_tmp[:, 32:40]
        s0 = T_tmp[:, 40:48]
        s1 = T_tmp[:, 48:56]
        cum = T_tmp[:, 56:64]
        P.op("dve", I("tensor_tensor", out=tm, in0=lb0, in1=lb1, op=ALU.max), reads=["vecs"], writes=["tt0"])
        P.op("dve", I("tensor_tensor", out=e0, in0=lb0, in1=tm, op=ALU.subtract), reads=["vecs", "tt0"], writes=["tt1"])
        P.op("dve", I("tensor_tensor", out=e1, in0=lb1, in1=tm, op=ALU.subtract), reads=["vecs", "tt0"], writes=["tt2"])
        P.op("act", I("activation", out=T_tmp[:, 8:24], in_=T_tmp[:, 8:24], func=AF.Exp), reads=["tt1", "tt2"], writes=["tt3"])
        P.op("dve", I("tensor_tensor", out=ssum, in0=e0, in1=e1, op=ALU.add), reads=["tt3"], writes=["tt4"])
        P.op("dve", I("reciprocal", out=rs_, in_=ssum), reads=["tt4"], writes=["tt5"])
        P.op("dve", I("tensor_tensor", out=s0, in0=e0, in1=rs_, op=ALU.mult), reads=["tt3", "tt5"], writes=["tt6"])
        P.op("dve", I("tensor_tensor", out=s1, in0=e1, in1=rs_, op=ALU.mult), reads=["tt3", "tt5"], writes=["tt7"])
        P.op("dve", I("tensor_tensor", out=cum, in0=s0, in1=s1, op=ALU.add), reads=["tt6", "tt7"], writes=["tt8"])
        P.op("dve", I("tensor_tensor", out=T_LB[:, 0, :], in0=s0, in1=s0, op=ALU.subtract), reads=["tt6"], writes=["lb_0"])
        P.op("dve", I("tensor_tensor", out=T_LB[:, 1, :], in0=cum, in1=s0, op=ALU.subtract), reads=["tt8", "tt6"], writes=["lb_1"])
        P.op("dve", I("tensor_scalar", out=T_OML[:].rearrange("p a b -> p (a b)"), in0=T_LB[:].rearrange("p a b -> p (a b)"),
                                              scalar1=-1.0, scalar2=1.0, op0=ALU.mult, op1=ALU.add),
             reads=["lb_0", "lb_1"], writes=["oml"])
        P.op("dve", I("tensor_scalar", out=T_NOML[:].rearrange("p a b -> p (a b)"), in0=T_LB[:].rearrange("p a b -> p (a b)"),
                                              scalar1=1.0, scalar2=-1.0, op0=ALU.mult, op1=ALU.add),
             reads=["lb_0", "lb_1"], writes=["noml"])

        def ln_chunk(sub, s_, tq, is_last):
            t0 = tq * 512
            b1, b2 = (2 * tq) % 8, (2 * tq + 1) % 8
            mean = A_lmean.ap
            var = A_lvar.ap
            rstd = A_lrstd.ap
            zs, ss = var, rstd
            for c in range(8):
                sq = A_lsq.ap[:, c % 2, :]
                P.op("act", (I("activation", out=sq, in_=xp[:, c, t0:t0 + 512], func=AF.Square)),
                     reads=xk(c, t0, t0 + 512), writes=[("ln_sq", c % 2)])
                if not LN_POOL:
                    P.op("pe", I("matmul", bank(b1), lhsT=ones_f, rhs=xp[:, c, t0:t0 + 512], start=(c == 0), stop=(c == 7)),
                         reads=xk(c, t0, t0 + 512) + ["cst"], writes=pk(b1))
                    P.op("pe", I("matmul", bank(b2), lhsT=ones_f, rhs=sq, start=(c == 0), stop=(c == 7)),
                         reads=[("ln_sq", c % 2), "cst"], writes=pk(b2))
                elif c == 1:
                    P.op("pool", I("tensor_tensor", out=zs, in0=xp[:, 0, t0:t0 + 512], in1=xp[:, 1, t0:t0 + 512], op=ALU.add),
                         reads=xk(0, t0, t0 + 512) + xk(1, t0, t0 + 512), writes=["ln_var"])
                    P.op("pool", I("tensor_tensor", out=ss, in0=A_lsq.ap[:, 0, :], in1=A_lsq.ap[:, 1, :], op=ALU.add),
                         reads=[("ln_sq", 0), ("ln_sq", 1)], writes=["ln_rstd"])
                elif c > 1:
                    P.op("pool", I("tensor_tensor", out=zs, in0=zs, in1=xp[:, c, t0:t0 + 512], op=ALU.add),
                         reads=xk(c, t0, t0 + 512) + ["ln_var"], writes=["ln_var"])
                    P.op("pool", I("tensor_tensor", out=ss, in0=ss, in1=sq, op=ALU.add),
                         reads=[("ln_sq", c % 2), "ln_rstd"], writes=["ln_rstd"])
            if LN_POOL:
                P.op("pe", I("matmul", bank(b1), lhsT=ones_f, rhs=zs, start=True, stop=True),
                     reads=["ln_var", "cst"], writes=pk(b1))
                P.op("pe", I("matmul", bank(b2), lhsT=ones_f, rhs=ss, start=True, stop=True),
                     reads=["ln_rstd", "cst"], writes=pk(b2))
            P.op("act", I("activation", out=mean, in_=bank(b1), func=AF.Copy, scale=1.0 / D),
                 reads=pk(b1), writes=["ln_mean"])
            P.op("dve", I("tensor_tensor", out=var, in0=mean, in1=mean, op=ALU.mult),
                 reads=["ln_mean"], writes=["ln_var"])
            P.op("dve", I("scalar_tensor_tensor", out=var, in0=bank(b2), scalar=1.0 / D, in1=var,
                                                         op0=ALU.mult, op1=ALU.subtract),
                 reads=pk(b2) + ["ln_var"], writes=["ln_var"])
            P.op("dve", I("tensor_scalar", out=var, in0=var, scalar1=LN_EPS, scalar2=None, op0=ALU.add),
                 reads=["ln_var"], writes=["ln_var"])
            P.op("act", I("activation", out=rstd, in_=var, func=AF.Sqrt),
                 reads=["ln_var"], writes=["ln_rstd"])
            P.op("dve", I("reciprocal", out=rstd, in_=rstd), reads=["ln_rstd"], writes=["ln_rstd"])
            for c in range(8):
                t = A_lt.ap[:, c % 2, :]
                tk = ("ln_t", c % 2)
                P.op("dve", (I("tensor_tensor", out=t, in0=xp[:, c, t0:t0 + 512], in1=mean, op=ALU.subtract)),
                     reads=xk(c, t0, t0 + 512) + ["ln_mean"], writes=[tk])
                P.op("dve", (I("tensor_tensor", out=t, in0=t, in1=rstd, op=ALU.mult)),
                     reads=[tk, "ln_rstd"], writes=[tk])
                P.op("act", (I("activation", out=xp[:, c, t0:t0 + 512], in_=t, func=AF.Identity,
                                                              scale=T_GA[:, sub, c:c + 1], bias=T_BA[:, sub, c:c + 1])),
                     reads=[tk, ("T_GA", sub), ("T_BA", sub)], writes=xk(c, t0, t0 + 512))
                if not is_last:
                    P.op("act", (I("activation", out=hb[:, c, t0:t0 + 512], in_=t, func=AF.Identity,
                                                                  scale=T_HG[:, sub, s_, c:c + 1], bias=T_HB[:, sub, s_, c:c + 1])),
                         reads=[tk, ("T_HG", sub, s_), ("T_HB", sub, s_)], writes=hk(c, t0, t0 + 512))

        def out_proj(sub, s_, is_last):
            cnt = 0
            for hf in range(2):
                th = hf * 1024
                for sl in range(4):
                    w, wk = wget()
                    for mi in range(2):
                        m = sl * 2 + mi
                        sidx = cnt % 4
                        cnt += 1
                        y = slot2(sidx)
                        for tc in range(2):
                            for kc in range(8):
                                P.op("pe", I("matmul", y[:, tc * 512:(tc + 1) * 512],
                                             lhsT=w[:, kc * 256 + mi * 128: kc * 256 + mi * 128 + 128],
                                             rhs=hb[:, kc, th + tc * 512: th + tc * 512 + 512],
                                             start=(kc == 0), stop=(kc == 7)),
                                     reads=[wk] + hk(kc, th + tc * 512, th + tc * 512 + 512),
                                     writes=pk(2 * sidx + tc))
                        P.op("dve", I("scalar_tensor_tensor", out=xp[:, m, th:th + 1024], in0=y,
                                      scalar=T_GATE1[:, sub, s_, m:m + 1], in1=xp[:, m, th:th + 1024],
                                      op0=ALU.mult, op1=ALU.add),
                             reads=pk(2 * sidx, 2 * sidx + 1) + [("T_GATE1", sub)] + xk(m, th, th + 1024),
                             writes=xk(m, th, th + 1024))
                    if LN_DEFER and hf == 1 and sl in (0, 1) and not DEBUG_STOP:
                        ln_chunk(sub, s_, sl, is_last)
                if (not LN_DEFER) and hf == 0 and not DEBUG_STOP:
                    ln_chunk(sub, s_, 0, is_last)
                    ln_chunk(sub, s_, 1, is_last)
            if not DEBUG_STOP:
                ln_chunk(sub, s_, 2, is_last)
                ln_chunk(sub, s_, 3, is_last)

        def ffn(sub, s_, is_last):
            for half in range(2):
                th = half * 1024
                for jj in range(22):
                    w, wk = wget()
                    sg_i, su_i = (2 * jj) % 4, (2 * jj + 1) % 4
                    g_ps, u_ps = slot2(sg_i), slot2(su_i)
                    for tc in range(2):
                        for kc in range(8):
                            rhs = hb[:, kc, th + tc * 512: th + tc * 512 + 512]
                            rk = [wk] + hk(kc, th + tc * 512, th + tc * 512 + 512)
                            P.op("pe", (I("matmul", g_ps[:, tc * 512:(tc + 1) * 512], lhsT=w[:, kc * 256: kc * 256 + 128], rhs=rhs,
                                start=(kc == 0), stop=(kc == 7))), reads=rk, writes=pk(2 * sg_i + tc))
                            P.op("pe", (I("matmul", u_ps[:, tc * 512:(tc + 1) * 512], lhsT=w[:, kc * 256 + 128: kc * 256 + 256], rhs=rhs,
                                start=(kc == 0), stop=(kc == 7))), reads=rk, writes=pk(2 * su_i + tc))
                    sgt = A_sgt.ap[:, jj % 2, :]
                    P.op("act", (I("activation", out=sgt, in_=g_ps, func=AF.Silu)),
                         reads=pk(2 * sg_i, 2 * sg_i + 1), writes=[("f_sgt", jj % 2)])
                    P.op("dve", (I("tensor_tensor", out=A_act.ap[:, jj, :], in0=u_ps, in1=sgt, op=ALU.mult)),
                        reads=pk(2 * su_i, 2 * su_i + 1) + [("f_sgt", jj % 2)], writes=[("f_act", jj)])
                    if LN_DEFER and half == 1 and jj in (1, 3):
                        ln_chunk(sub, s_, jj // 2, is_last)
                for m in range(8):
                    sidx = m % 4
                    y = slot2(sidx)
                    for kh in range(2):
                        w, wk = wget()
                        for tc in range(2):
                            for kk in range(11):
                                kc = kh * 11 + kk
                                P.op("pe", (I("matmul", y[:, tc * 512:(tc + 1) * 512], lhsT=w[:, kk * 128: kk * 128 + 128],
                                    rhs=A_act.ap[:, kc, tc * 512:(tc + 1) * 512],
                                    start=(kc == 0), stop=(kc == 21))),
                                    reads=[wk, ("f_act", kc)], writes=pk(2 * sidx + tc))
                    P.op("dve", (I("scalar_tensor_tensor", out=xp[:, m, th:th + 1024], in0=y, scalar=T_GATE1[:, sub, s_, m:m + 1],
                        in1=xp[:, m, th:th + 1024], op0=ALU.mult, op1=ALU.add)),
                        reads=pk(2 * sidx, 2 * sidx + 1) + [("T_GATE1", sub)] + xk(m, th, th + 1024),
                        writes=xk(m, th, th + 1024))
                if not LN_DEFER:
                    ln_chunk(sub, s_, 2 * half, is_last)
                    ln_chunk(sub, s_, 2 * half + 1, is_last)
                elif half == 1:
                    ln_chunk(sub, s_, 2, is_last)
                    ln_chunk(sub, s_, 3, is_last)

        def dbg_dump(items):
            for c, apx, r0, r1, keys in items:
                n = apx.shape[-1]
                P.op("dve", I("tensor_copy", out=xp[r0:r1, c, 0:n], in_=apx), reads=keys, writes=xk(c, 0, S))

        def rope_tables(s_):
            cs = M_cs.ap
            posi = M_posi.ap.bitcast(I32)
            ki = M_ki.ap.bitcast(I32)
            kf = M_kf.ap
            P.dma("sp", I("dma_start", out=posi, in_=pos_in[s_]), "ld_pos", writes=["m_posi"])
            P.op("dve", I("tensor_copy", out=cs, in_=posi), reads=["m_posi"], writes=["m_cs"])
            P.op("dve", I("tensor_scalar", out=cs, in0=cs, scalar1=vcol("invf", 0), scalar2=None, op0=ALU.mult),
                 reads=["m_cs", "vecs"], writes=["m_cs"])
            P.op("dve", I("tensor_scalar", out=cs, in0=cs, scalar1=vcol("phase", 0), scalar2=None, op0=ALU.add),
                 reads=["m_cs", "vecs"], writes=["m_cs"])
            P.op("dve", I("tensor_scalar", out=kf, in0=cs, scalar1=1.0 / (2 * math.pi), scalar2=None, op0=ALU.mult),
                 reads=["m_cs"], writes=["m_kf"])
            P.op("dve", I("tensor_copy", out=ki, in_=kf), reads=["m_kf"], writes=["m_ki"])
            P.op("dve", I("tensor_copy", out=kf, in_=ki), reads=["m_ki"], writes=["m_kf"])
            P.op("dve", I("scalar_tensor_tensor", out=cs, in0=kf, scalar=-2.0 * math.pi, in1=cs,
                          op0=ALU.mult, op1=ALU.add),
                 reads=["m_kf", "m_cs"], writes=["m_cs"])
            for _ in range(2):
                P.op("dve", I("tensor_scalar", out=kf, in0=cs, scalar1=math.pi, scalar2=-2.0 * math.pi,
                              op0=ALU.is_gt, op1=ALU.mult),
                     reads=["m_cs"], writes=["m_kf"])
                P.op("dve", I("tensor_tensor", out=cs, in0=cs, in1=kf, op=ALU.add),
                     reads=["m_cs", "m_kf"], writes=["m_cs"])
                P.op("dve", I("tensor_scalar", out=kf, in0=cs, scalar1=-math.pi, scalar2=2.0 * math.pi,
                              op0=ALU.is_lt, op1=ALU.mult),
                     reads=["m_cs"], writes=["m_kf"])
                P.op("dve", I("tensor_tensor", out=cs, in0=cs, in1=kf, op=ALU.add),
                     reads=["m_cs", "m_kf"], writes=["m_cs"])
            P.op("dve", I("tensor_scalar", out=cs, in0=cs, scalar1=math.pi, scalar2=-math.pi,
                          op0=ALU.min, op1=ALU.max),
                 reads=["m_cs"], writes=["m_cs"])
            P.op("act", I("activation", out=cs, in_=cs, func=AF.Sin), reads=["m_cs"], writes=["m_cs"])
            P.op("dve", I("tensor_scalar", out=cs, in0=cs, scalar1=vcol("sgn", 0), scalar2=None, op0=ALU.mult),
                 reads=["m_cs", "vecs"], writes=["m_cs"])

        def mla(sub, s_, j, is_last):
            rope_tables(s_)
            cs = M_cs.ap
            rq = M_rq.ap
            gq0 = L["gq"][0] + j * 6
            gkv0 = L["gkv"][0] + j * 2
            if DEBUG_STOP == 1:
                dbg_dump([(0, M_cs.ap, 0, 128, ["m_cs"])])
                return
            for hf in range(2):
                th = hf * 1024
                def emit_ssq_q(mm):
                    sqm = M_sq.ap[:, mm % 2, :]
                    for tc in range(2):
                        P.op("pe", I("matmul", bank(6 + tc), lhsT=ones_f, rhs=sqm[:, tc * 512:(tc + 1) * 512],
                                     start=(mm == 0), stop=(mm == 5)),
                             reads=[("m_sq", mm % 2), "cst"], writes=pk(6 + tc))

                for sl in range(3):
                    w, wk = wget()
                    for mi in range(2):
                        m = sl * 2 + mi
                        sidx = m % 3
                        qps = slot2(sidx)
                        for tc in range(2):
                            for kc in range(8):
                                P.op("pe", (I("matmul", qps[:, tc * 512:(tc + 1) * 512],
                                    lhsT=w[:, kc * 256 + mi * 128: kc * 256 + mi * 128 + 128],
                                    rhs=hb[:, kc, th + tc * 512: th + tc * 512 + 512],
                                    start=(kc == 0), stop=(kc == 7))),
                                    reads=[wk] + hk(kc, th + tc * 512, th + tc * 512 + 512), writes=pk(2 * sidx + tc))
                        P.op("act", (I("activation", out=M_qg.ap[:, m, th:th + 1024], in_=qps, func=AF.Identity, scale=vecs[:, gq0 + m: gq0 + m + 1])),
                            reads=pk(2 * sidx, 2 * sidx + 1) + ["vecs"], writes=[("m_qg", m, hf)])
                        sq = M_sq.ap[:, m % 2, :]
                        P.op("act", (I("activation", out=sq, in_=qps, func=AF.Square)),
                             reads=pk(2 * sidx, 2 * sidx + 1), writes=[("m_sq", m % 2)])
                        if m >= 1:
                            emit_ssq_q(m - 1)

                def finish_q():
                    emit_ssq_q(5)
                    sd = M_sd.ap
                    P.op("dve", I("tensor_scalar", out=sd, in0=slot2(3), scalar1=1.0 / QL, scalar2=RMS_EPS,
                                  op0=ALU.mult, op1=ALU.add), reads=pk(6, 7), writes=["m_sd"])
                    P.op("act", I("activation", out=sd, in_=sd, func=AF.Sqrt), reads=["m_sd"], writes=["m_sd"])
                    P.op("dve", I("reciprocal", out=rq[:, th:th + 1024], in_=sd), reads=["m_sd"], writes=[("m_rq", hf)])
                w, wk = wget()
                for m in range(2):
                    sidx = m
                    kps = slot2(sidx)
                    for tc in range(2):
                        for kc in range(8):
                            P.op("pe", (I("matmul", kps[:, tc * 512:(tc + 1) * 512],
                                lhsT=w[:, kc * 256 + m * 128: kc * 256 + m * 128 + 128],
                                rhs=hb[:, kc, th + tc * 512: th + tc * 512 + 512],
                                start=(kc == 0), stop=(kc == 7))),
                                reads=[wk] + hk(kc, th + tc * 512, th + tc * 512 + 512), writes=pk(2 * sidx + tc))
                    P.op("dve", (I("tensor_copy", out=M_kvlat.ap[:, m, :], in_=kps)),
                         reads=pk(2 * sidx, 2 * sidx + 1), writes=[("m_kvlat", m)])
                    if m == 0:
                        finish_q()
                    sq = M_sq.ap[:, m % 2, :]
                    P.op("act", (I("activation", out=sq, in_=kps, func=AF.Square)),
                         reads=pk(2 * sidx, 2 * sidx + 1), writes=[("m_sq", m % 2)])
                    if m == 1:
                        for mm in range(2):
                            sqm = M_sq.ap[:, mm % 2, :]
                            for tc in range(2):
                                P.op("pe", I("matmul", bank(4 + tc), lhsT=ones_f, rhs=sqm[:, tc * 512:(tc + 1) * 512],
                                             start=(mm == 0), stop=(mm == 1)),
                                     reads=[("m_sq", mm % 2), "cst"], writes=pk(4 + tc))
                rkv = M_rkv.ap
                P.op("dve", I("tensor_scalar", out=rkv, in0=slot2(2), scalar1=1.0 / KVL, scalar2=RMS_EPS,
                                                      op0=ALU.mult, op1=ALU.add),
                     reads=pk(4, 5), writes=["m_rkv"])
                P.op("act", I("activation", out=rkv, in_=rkv, func=AF.Sqrt), reads=["m_rkv"], writes=["m_rkv"])
                P.op("dve", I("reciprocal", out=rkv, in_=rkv), reads=["m_rkv"], writes=["m_rkv"])
                for m in range(2):
                    P.op("dve", (I("scalar_tensor_tensor", out=M_kvn.ap[:, m, th:th + 1024], in0=M_kvlat.ap[:, m, :], scalar=vecs[:, gkv0 + m: gkv0 + m + 1],
                        in1=rkv, op0=ALU.mult, op1=ALU.mult)),
                        reads=[("m_kvlat", m), "m_rkv", "vecs"], writes=[("m_kvn", m, hf)])
                if DEBUG_STOP == 6:
                    return
                w, wk = wget()
                rps = slot2(3)
                for tc in range(2):
                    for kc in range(8):
                        P.op("pe", (I("matmul", rps[:, tc * 512:(tc + 1) * 512], lhsT=w[:, kc * 128: kc * 128 + 128],
                            rhs=hb[:, kc, th + tc * 512: th + tc * 512 + 512], start=(kc == 0), stop=(kc == 7))),
                            reads=[wk] + hk(kc, th + tc * 512, th + tc * 512 + 512), writes=pk(6 + tc))
                if DEBUG_STOP == 7:
                    return
                ta = M_tmpa.ap
                P.op("dve", (I("tensor_tensor", out=ta[0:64, :], in0=rps[0:64, :], in1=cs[0:64, th:th + 1024], op=ALU.mult)),
                     reads=pk(6, 7) + ["m_cs"], writes=["m_tmpa"])
                if DEBUG_STOP == 8:
                    return
                tb = M_tmpb.ap
                P.op("dve", (I("tensor_tensor", out=tb[0:64, :], in0=rps[64:128, :], in1=cs[64:128, th:th + 1024], op=ALU.mult)),
                     reads=pk(6, 7) + ["m_cs"], writes=["m_tmpb"])
                if DEBUG_STOP == 9:
                    return
                P.op("dve", (I("tensor_tensor", out=M_krope.ap[64:96, th:th + 1024], in0=ta[0:32, :],
                                                              in1=tb[0:32, :], op=ALU.add)),
                     reads=["m_tmpa", "m_tmpb"], writes=[("m_krope", hf)])
            P.op("dve", I("tensor_tensor", out=cs, in0=cs, in1=rq, op=ALU.mult),
                 reads=["m_cs", ("m_rq", 0), ("m_rq", 1)], writes=["m_cs"])
            if DEBUG_STOP == 2:
                dbg_dump([(0, M_qg.ap[:, 0, :], 0, 128, [("m_qg", 0, 0), ("m_qg", 0, 1)]),
                          (1, M_kvn.ap[:, 0, :], 0, 128, [("m_kvn", 0, 0), ("m_kvn", 0, 1)]),
                          (2, M_rq.ap, 0, 128, [("m_rq", 0), ("m_rq", 1)]),
                          (3, M_krope.ap[64:96, :], 64, 96, [("m_krope", 0), ("m_krope", 1)]),
                          (4, M_cs.ap, 0, 128, ["m_cs"])])
                return
            P.op("dve", I("memset", M_vaug.ap[:, :, :, 64:128], 1.0), writes=["m_vaug"])
            blk = 0
            for p in range(8):
                w, wk = wget()
                for hf in range(2):
                    th = hf * 1024
                    sa, sbq = 2 * hf, 2 * hf + 1
                    A_ps, B_ps = slot2(sa), slot2(sbq)
                    for tc in range(2):
                        for kc in range(6):
                            rhs = M_qg.ap[:, kc, th + tc * 512: th + tc * 512 + 512]
                            rk = [wk, ("m_qg", kc, hf)]
                            P.op("pe", I("matmul", A_ps[:, tc * 512:(tc + 1) * 512], lhsT=w[:, kc * 256: kc * 256 + 128], rhs=rhs,
                                         start=(kc == 0), stop=(kc == 5)), reads=rk, writes=pk(2 * sa + tc))
                            P.op("pe", I("matmul", B_ps[:, tc * 512:(tc + 1) * 512], lhsT=w[:, kc * 256 + 128: kc * 256 + 256], rhs=rhs,
                                         start=(kc == 0), stop=(kc == 5)), reads=rk, writes=pk(2 * sbq + tc))
                for hf in range(2):
                    th = hf * 1024
                    sa, sbq = 2 * hf, 2 * hf + 1
                    A_ps, B_ps = slot2(sa), slot2(sbq)
                    for hh in range(2):
                        P.op("dve", I("tensor_tensor", out=M_qk.ap[0:64, hh, th:th + 1024], in0=A_ps[hh * 64:hh * 64 + 64, :],
                                      in1=rq[hh * 64:hh * 64 + 64, th:th + 1024], op=ALU.mult),
                             reads=pk(2 * sa, 2 * sa + 1) + [("m_rq", hf)], writes=[("m_qk", hh, hf, "n")])
                    ta = M_tmpa.ap
                    P.op("dve", I("tensor_tensor", out=ta[0:64, :], in0=B_ps[0:64, :], in1=cs[0:64, th:th + 1024], op=ALU.mult),
                         reads=pk(2 * sbq, 2 * sbq + 1) + ["m_cs"], writes=["m_tmpa"])
                    tb = M_tmpb.ap
                    P.op("dve", I("tensor_tensor", out=tb[0:64, :], in0=B_ps[64:128, :], in1=cs[64:128, th:th + 1024], op=ALU.mult),
                         reads=pk(2 * sbq, 2 * sbq + 1) + ["m_cs"], writes=["m_tmpb"])
                    for hh in range(2):
                        P.op("dve", I("tensor_tensor", out=M_qk.ap[64:96, hh, th:th + 1024],
                                      in0=ta[hh * 32:hh * 32 + 32, :], in1=tb[hh * 32:hh * 32 + 32, :], op=ALU.add),
                             reads=["m_tmpa", "m_tmpb"], writes=[("m_qk", hh, hf, "r")])
                for hf in range(2):
                    th = hf * 1024
                    sk, sv = 2 * hf, 2 * hf + 1
                    K_ps, V_ps = slot2(sk), slot2(sv)
                    for tc in range(2):
                        for kc in range(2):
                            rhs = M_kvn.ap[:, kc, th + tc * 512: th + tc * 512 + 512]
                            P.op("pe", I("matmul", K_ps[:, tc * 512:(tc + 1) * 512], lhsT=w[:, 1536 + kc * 256: 1536 + kc * 256 + 128], rhs=rhs,
                                         start=(kc == 0), stop=(kc == 1)), reads=[wk, ("m_kvn", kc, hf)], writes=pk(2 * sk + tc))
                    for tt in range(8):
                        for kc in range(2):
                            P.op("pe", I("matmul", V_ps[:, tt * 128:(tt + 1) * 128],
                                         lhsT=M_kvn.ap[:, kc, th + tt * 128: th + tt * 128 + 128],
                                         rhs=w[:, 1536 + kc * 256 + 128: 1536 + kc * 256 + 256],
                                         start=(kc == 0), stop=(kc == 1)),
                                 reads=[wk, ("m_kvn", kc, hf)], writes=pk(2 * sv + tt // 4))
                for hf in range(2):
                    th = hf * 1024
                    sk, sv = 2 * hf, 2 * hf + 1
                    K_ps, V_ps = slot2(sk), slot2(sv)
                    for hh in range(2):
                        P.op("act", I("activation", out=M_qk.ap[0:64, 2 + hh, th:th + 1024], in_=K_ps[hh * 64:hh * 64 + 64, :], func=AF.Copy),
                             reads=pk(2 * sk, 2 * sk + 1), writes=[("m_qk", 2 + hh, hf, "n")])
                        P.op("act", I("activation", out=M_qk.ap[64:96, 2 + hh, th:th + 1024], in_=M_krope.ap[64:96, th:th + 1024], func=AF.Copy),
                             reads=[("m_krope", hf)], writes=[("m_qk", 2 + hh, hf, "r")])
                        P.op("act", I("activation", out=M_vaug.ap[:, hh, hf * 8:(hf + 1) * 8, 0:64],
                                      in_=V_ps.rearrange("p (t c) -> p t c", c=128)[:, :, hh * 64:hh * 64 + 64], func=AF.Copy),
                             reads=pk(2 * sv, 2 * sv + 1), writes=[("m_vaug", hh, hf)])
                if DEBUG_STOP == 3:
                    ks = lambda i: [("m_qk", i, 0, "n"), ("m_qk", i, 0, "r"), ("m_qk", i, 1, "n"), ("m_qk", i, 1, "r")]
                    dbg_dump([(0, M_qk.ap[0:96, 0, :], 0, 96, ks(0)), (1, M_qk.ap[0:96, 1, :], 0, 96, ks(1)),
                              (2, M_qk.ap[0:96, 2, :], 0, 96, ks(2)), (3, M_qk.ap[0:96, 3, :], 0, 96, ks(3)),
                              (4, M_vaug.ap[:, 0, :, :].rearrange("p a b -> p (a b)"), 0, 128, [("m_vaug", 0, 0), ("m_vaug", 0, 1), "m_vaug"]),
                              (5, M_vaug.ap[:, 1, :, :].rearrange("p a b -> p (a b)"), 0, 128, [("m_vaug", 1, 0), ("m_vaug", 1, 1), "m_vaug"])])
                    return
                blocks = []
                for qgrp in ((0, 1), (2, 3)):
                    for hh in range(2):
                        for qc in qgrp:
                            nkb = 4 * qc + 4
                            for kb in range(nkb):
                                jd = kb - 4 * qc
                                q0 = qc * 512 + max(jd, 0) * 128
                                blocks.append(dict(hh=hh, qc=qc, kb=kb, jd=jd, q0=q0, N=qc * 512 + 512 - q0,
                                                   first=(kb == 0), last=(kb == nkb - 1)))
                LA = 2

                def emit_st(bd, sb_i):
                    hh, qc, kb, jd, q0, N = bd["hh"], bd["qc"], bd["kb"], bd["jd"], bd["q0"], bd["N"]
                    Qh = M_qk.ap[0:96, hh, :]
                    Kh = M_qk.ap[0:96, 2 + hh, :]
                    ST = bank(sb_i)
                    hq, hk_ = qc // 2, kb // 8
                    P.op("pe", I("matmul", ST[:, 0:N], lhsT=Kh[:, kb * 128:(kb + 1) * 128], rhs=Qh[:, q0:q0 + N],
                                 start=True, stop=(jd < 0)),
                         reads=[("m_qk", hh, hq, "n"), ("m_qk", hh, hq, "r"), ("m_qk", 2 + hh, hk_, "n"), ("m_qk", 2 + hh, hk_, "r")],
                         writes=pk(sb_i))
                    if jd >= 0:
                        P.op("pe", I("matmul", ST[:, 0:128], lhsT=ident_bf[:], rhs=maskT_bf[:], start=False, stop=True),
                             reads=["ident_bf", "maskT_bf"], writes=pk(sb_i))
                    PT = M_pt.ap[:, sb_i, :]
                    P.op("act", I("activation", out=PT[:, 0:N], in_=ST[:, 0:N], func=AF.Exp, scale=ATT_SCALE),
                         reads=pk(sb_i), writes=[("m_pt", sb_i)])

                def emit_pv(bd, sb_i):
                    hh, qc, kb, q0, N = bd["hh"], bd["qc"], bd["kb"], bd["q0"], bd["N"]
                    h = 2 * p + hh
                    ob = 4 + (qc % 2)
                    O = bank(ob)
                    PT = M_pt.ap[:, sb_i, :]
                    P.op("pe", I("matmul", O[:, q0 - qc * 512: 512], lhsT=M_vaug.ap[:, hh, kb, :], rhs=PT[:, 0:N],
                                 start=bd["first"], stop=bd["last"]),
                         reads=[("m_pt", sb_i), ("m_vaug", hh, kb // 8), "m_vaug"], writes=pk(ob))
                    if bd["last"]:
                        rcp = M_rcp.ap
                        P.op("dve", I("reciprocal", out=rcp[64:128, :], in_=O[64:128, :]), reads=pk(ob), writes=["m_rcp"])
                        P.op("dve", I("tensor_tensor", out=hb[(h % 2) * 64:(h % 2) * 64 + 64, h // 2, qc * 512:(qc + 1) * 512],
                                      in0=O[0:64, :], in1=rcp[64:128, :], op=ALU.mult),
                             reads=pk(ob) + ["m_rcp"], writes=hk(h // 2, qc * 512, qc * 512 + 512))

                nb = len(blocks)
                base = blk
                for r in range(nb + LA):
                    if r < nb:
                        emit_st(blocks[r], (base + r) % 4)
                    if r >= LA:
                        emit_pv(blocks[r - LA], (base + r - LA) % 4)
                blk += nb
            if DEBUG_STOP == 4:
                dbg_dump([(c, hb[:, c, :], 0, 128, hk(c, 0, S)) for c in range(8)])
                return
            out_proj(sub, s_, is_last)

        def hgrn(sub, s_, j, is_last):
            T0, T1, T2, T3 = [g.ap for g in G_T]
            kT0, kT1, kT2, kT3 = ["g_T0", "g_T1", "g_T2", "g_T3"]
            smask = G_smask.ap
            P.op("dve", I("memset", smask, 1.0), writes=["g_smask"])
            P.op("dve", I("memset", smask.rearrange("p (n c) -> p n c", c=64)[:, :, 0:1], 0.0),
                 writes=["g_smask"])
            gn = vcol("gn", j)
            ebl = G_ebl2.ap
            sbf_keys = [("g_sbf", i) for i in range(5)]

            def stage_a1(hd):
                w0, wk0 = wget()
                for which, (coff, dst, dkey, func) in enumerate(((0, T0, kT0, AF.Silu), (128, T1, kT1, AF.Sigmoid))):
                    for hf in range(2):
                        th = hf * 1024
                        sidx = (which * 2 + hf) % 4
                        pp = slot2(sidx)
                        for tc in range(2):
                            for kc in range(8):
                                P.op("pe", I("matmul", pp[:, tc * 512:(tc + 1) * 512],
                                             lhsT=w0[:, kc * 256 + coff: kc * 256 + coff + 128],
                                             rhs=hb[:, kc, th + tc * 512: th + tc * 512 + 512], start=(kc == 0), stop=(kc == 7)),
                                     reads=[wk0] + hk(kc, th + tc * 512, th + tc * 512 + 512), writes=pk(2 * sidx + tc))
                        P.op("act", I("activation", out=dst[:, th:th + 1024], in_=pp, func=func),
                             reads=pk(2 * sidx, 2 * sidx + 1), writes=[(dkey, hf)])

            def stage_a2(hd):
                w1, wk1 = wget()
                for hf in range(2):
                    th = hf * 1024
                    sidx = hf
                    vp = slot2(sidx)
                    for tt in range(8):
                        for kc in range(8):
                            P.op("pe", I("matmul", vp[:, tt * 128:(tt + 1) * 128],
                                         lhsT=hb[:, kc, th + tt * 128: th + tt * 128 + 128],
                                         rhs=w1[:, kc * 256: kc * 256 + 128], start=(kc == 0), stop=(kc == 7)),
                                 reads=[wk1] + hk(kc, th + tt * 128, th + tt * 128 + 128), writes=pk(2 * sidx + tt // 4))
                    P.op("act", I("activation", out=G_vtok.ap[:, hf * 8:(hf + 1) * 8, :],
                                  in_=vp.rearrange("p (t c) -> p t c", c=128), func=AF.Copy),
                         reads=pk(2 * sidx, 2 * sidx + 1), writes=[("g_vtok", hf)])
                for hf in range(2):
                    th = hf * 1024
                    sidx = 2 + hf
                    gp = slot2(sidx)
                    for tc in range(2):
                        for kc in range(8):
                            P.op("pe", I("matmul", gp[:, tc * 512:(tc + 1) * 512], lhsT=w1[:, kc * 256 + 128: kc * 256 + 256],
                                         rhs=hb[:, kc, th + tc * 512: th + tc * 512 + 512], start=(kc == 0), stop=(kc == 7)),
                                 reads=[wk1] + hk(kc, th + tc * 512, th + tc * 512 + 512), writes=pk(2 * sidx + tc))
                    P.op("act", I("activation", out=G_sg.ap[:, th:th + 1024], in_=gp, func=AF.Silu),
                         reads=pk(2 * sidx, 2 * sidx + 1), writes=[("g_sg", hf)])

            def stage_b(hd):
                lbc = T_LB[:, j, hd:hd + 1]
                omlc = T_OML[:, j, hd:hd + 1]
                nomlc = T_NOML[:, j, hd:hd + 1]
                P.op("act", I("activation", out=T3, in_=T1, func=AF.Identity, scale=nomlc, bias=omlc),
                     reads=[(kT1, 0), (kT1, 1), "oml", "noml"], writes=[kT3])
                P.op("act", I("activation", out=T2, in_=T1, func=AF.Ln, scale=omlc, bias=lbc),
                     reads=[(kT1, 0), (kT1, 1), "oml", "lb_0", "lb_1"], writes=[kT2])
                P.op("dve", I("tensor_tensor_scan", out=T1, data0=smask, data1=T2, initial=0.0, op0=ALU.mult, op1=ALU.add),
                     reads=[kT2, "g_smask"], writes=[(kT1, 0), (kT1, 1)])
                P.op("act", I("activation", out=ebl, in_=T1.rearrange("p (n c) -> p n c", c=64)[:, :, 63], func=AF.Exp),
                     reads=[(kT1, 0), (kT1, 1)], writes=["g_ebl2"])
                P.op("act", I("activation", out=T2, in_=T1, func=AF.Exp), reads=[(kT1, 0), (kT1, 1)], writes=[kT2])
                P.op("dve", I("tensor_tensor", out=G_qt.ap, in0=T0, in1=T2, op=ALU.mult),
                     reads=[(kT0, 0), (kT0, 1), kT2], writes=["g_qt"])
                P.op("act", I("activation", out=T0, in_=T1, func=AF.Exp, scale=-1.0),
                     reads=[(kT1, 0), (kT1, 1)], writes=[(kT0, 0), (kT0, 1)])
                P.op("dve", I("tensor_tensor", out=T3, in0=T3, in1=T0, op=ALU.mult),
                     reads=[kT3, (kT0, 0), (kT0, 1)], writes=[kT3])
                P.op("act", I("activation", out=G_kt.ap, in_=T3, func=AF.Copy), reads=[kT3], writes=["g_kt"])
                P.op("dve", I("tensor_tensor", out=G_kh.ap.rearrange("p (n c) -> p n c", c=64),
                              in0=T3.rearrange("p (n c) -> p n c", c=64),
                              in1=ebl.rearrange("p (n o) -> p n o", o=1).to_broadcast([128, 32, 64]), op=ALU.mult),
                     reads=[kT3, "g_ebl2"], writes=["g_kh"])

            def stage_c(hd):
                tps = slot2(0).bitcast(BF16)
                for tt in range(16):
                    P.op("pe", I("transpose", tps[:, tt * 128:(tt + 1) * 128], G_kh.ap[:, tt * 128:(tt + 1) * 128], ident_bf[:]),
                         reads=["g_kh", "ident_bf"], writes=pk(tt // 8))
                for hb_ in range(2):
                    P.op("act", I("activation", out=G_khT.ap[:, hb_ * 8:(hb_ + 1) * 8, :].rearrange("p a b -> p (a b)"),
                                  in_=tps[:, hb_ * 1024:(hb_ + 1) * 1024], func=AF.Copy),
                         reads=pk(hb_), writes=[("g_khT", hb_)])
                U3 = G_u.ap.rearrange("p (v n) -> p n v", n=32)
                for g8 in range(4):
                    bE = 2 + 2 * (g8 % 2)
                    bO = bE + 1
                    for i8 in range(8):
                        n = g8 * 8 + i8
                        if n == 31:
                            continue
                        pb = (n % 2) * 64
                        bk = bO if (n % 2) else bE
                        P.op("pe", I("matmul", bank(bk, 128, (i8 // 2) * 128), lhsT=G_khT.ap[pb:pb + 64, n // 2, :],
                                     rhs=G_vtok.ap[pb:pb + 64, n // 2, :], start=True, stop=True),
                             reads=[("g_khT", n // 16), ("g_vtok", n // 16)], writes=pk(bk))
                    P.op("act", I("activation", out=U3[:, g8 * 8: g8 * 8 + 8: 2, :],
                                  in_=bank(bE, 512).rearrange("p (n v) -> p n v", v=128), func=AF.Copy),
                         reads=pk(bE), writes=[("g_u", 2 * g8)])
                    nn = 4 if g8 < 3 else 3
                    P.op("act", I("activation", out=U3[:, g8 * 8 + 1: g8 * 8 + 1 + 2 * nn: 2, :],
                                  in_=bank(bO, nn * 128).rearrange("p (n v) -> p n v", v=128), func=AF.Copy),
                         reads=pk(bO), writes=[("g_u", 2 * g8 + 1)])
                P.op("dve", I("memset", U3[:, 31:32, :], 0.0), writes=[("g_u", 8)])
                er = G_eblrep.ap.rearrange("p (v n) -> p v n", n=32)
                P.op("dve", I("tensor_copy", out=er, in_=ebl.rearrange("p (o n) -> p o n", o=1).to_broadcast([128, 32, 32])),
                     reads=["g_ebl2"], writes=["g_eblrep"])
                P.op("dve", I("memset", er[:, :, 0:1], 0.0), reads=["g_eblrep"], writes=["g_eblrep"])
                sbf_flat = G_sbf.ap.rearrange("p a b -> p (a b)")
                for vq in range(4):
                    P.op("dve", I("tensor_tensor_scan", out=sbf_flat[:, vq * 1024:(vq + 1) * 1024], data0=G_eblrep.ap,
                                  data1=G_u.ap[:, vq * 1024:(vq + 1) * 1024], initial=0.0, op0=ALU.mult, op1=ALU.add),
                         reads=["g_eblrep"] + [("g_u", g) for g in range(9)], writes=[("g_sbf", 1 + vq)])
                LA = 3
                sbf_vn = sbf_flat.rearrange("p (v n) -> p n v", n=32)

                def emit_at(jj):
                    ab = jj % 4
                    AT = bank(ab, 128)
                    P.op("pe", I("matmul", AT, lhsT=G_kt.ap[:, jj * 128:(jj + 1) * 128], rhs=G_qt.ap[:, jj * 128:(jj + 1) * 128],
                                 start=True, stop=True), reads=["g_kt", "g_qt"], writes=pk(ab))
                    P.op("dve", I("tensor_tensor", out=G_asb.ap[:, ab, :], in0=AT, in1=mask2_f, op=ALU.mult),
                         reads=pk(ab) + ["cst"], writes=[("g_asb", ab)])

                def emit_o(jj):
                    ab = jj % 4
                    hf, jt = jj // 8, jj % 8
                    osl = 2 + hf
                    oc = slot2(osl)[:, jt * 128:(jt + 1) * 128]
                    okey = pk(2 * osl + jt // 4)
                    P.op("pe", I("matmul", oc, lhsT=G_vtok.ap[:, jj, :], rhs=G_asb.ap[:, ab, :], start=True, stop=False),
                         reads=[("g_asb", ab), ("g_vtok", hf)], writes=okey)
                    for i2 in range(2):
                        n = 2 * jj + i2
                        if n == 0:
                            continue
                        P.op("pe", I("matmul", oc[:, i2 * 64:(i2 + 1) * 64], lhsT=sbf_vn[:, n - 1, :],
                                     rhs=G_qt.ap[:, n * 64:(n + 1) * 64], start=False, stop=(i2 == 1)),
                             reads=sbf_keys + ["g_qt"], writes=okey)

                for r in range(16 + LA):
                    if r < 16:
                        emit_at(r)
                    if r >= LA:
                        emit_o(r - LA)
                for hf in range(2):
                    th = hf * 1024
                    osl = 2 + hf
                    Ops = slot2(osl)
                    osq = G_osq.ap
                    P.op("act", I("activation", out=osq, in_=Ops, func=AF.Square),
                         reads=pk(2 * osl, 2 * osl + 1), writes=["g_osq"])
                    for tc in range(2):
                        P.op("pe", I("matmul", bank(tc), lhsT=ones_f, rhs=osq[:, tc * 512:(tc + 1) * 512], start=True, stop=True),
                             reads=["g_osq", "cst"], writes=pk(tc))
                    rstd = G_rstd.ap
                    P.op("act", I("activation", out=rstd, in_=slot2(0), func=AF.Sqrt, scale=1.0 / 128, bias=eps_rms),
                         reads=pk(0, 1) + ["eps_t"], writes=["g_rstd"])
                    P.op("dve", I("reciprocal", out=rstd, in_=rstd), reads=["g_rstd"], writes=["g_rstd"])
                    to = G_to.ap
                    P.op("dve", I("scalar_tensor_tensor", out=to, in0=Ops, scalar=gn, in1=rstd, op0=ALU.mult, op1=ALU.mult),
                         reads=pk(2 * osl, 2 * osl + 1) + ["g_rstd", "vecs"], writes=["g_to"])
                    P.op("dve", I("tensor_tensor", out=G_og.ap, in0=to, in1=G_sg.ap[:, th:th + 1024], op=ALU.mult),
                         reads=["g_to", ("g_sg", hf)], writes=["g_og"])
                    P.dma("sp", I("dma_start", out=ogd[hd, :, th:th + 1024], in_=G_og.ap), "st_og",
                          reads=["g_og"], writes=[("ogd", hd, hf)])

            stage_a1(0)
            for hd in range(8):
                stage_a2(hd)
                stage_b(hd)
                if hd + 1 < 8:
                    stage_a1(hd + 1)
                stage_c(hd)
            for c in range(8):
                P.dma("sp", I("dma_start", out=hb[:, c, :], in_=ogd[c]), "ld_og%d" % c,
                      reads=[("ogd", c, 0), ("ogd", c, 1)], writes=hk(c, 0, S))
            if DEBUG_STOP == 13:
                dbg_dump([(c, hb[:, c, :], 0, 128, hk(c, 0, S)) for c in range(8)])
                return
            out_proj(sub, s_, is_last)

        for s_ in range(nseq):
            for c in range(8):
                P.dma("sp", (I("dma_start", out=xp[:, c, :], in_=x_in[s_, :, c, :])), "ld_x%d" % c,
                      writes=xk(c, 0, S))
            first = sublayers[0]
            for c in range(8):
                P.op("act", (I("activation", out=hb[:, c, :], in_=xp[:, c, :], func=AF.Identity,
                                                         scale=T_S1[:, first, s_, c:c + 1], bias=T_SH[:, first, s_, c:c + 1])),
                     reads=xk(c, 0, S) + [("T_S1", first), ("T_SH", first)], writes=hk(c, 0, S))
                P.op("dve", (I("tensor_scalar", out=xp[:, c, :], in0=xp[:, c, :], scalar1=ALPHA, scalar2=None,
                                                            op0=ALU.mult)),
                     reads=xk(c, 0, S), writes=xk(c, 0, S))
            for idx, sub in enumerate(sublayers):
                is_last = (idx == len(sublayers) - 1)
                l, typ = sub // 2, sub % 2
                if typ == 1:
                    ffn(sub, s_, is_last)
                else:
                    if l % 2 == 0:
                        mla(sub, s_, l // 2, is_last)
                    else:
                        hgrn(sub, s_, l // 2, is_last)
            for c in range(8):
                P.dma("sp", (I("dma_start", out=y_out[s_, :, c, :], in_=xp[:, c, :])), "st_y%d" % c,
                      reads=xk(c, 0, S))
        assert DEBUG_STOP or wstate["next"] == len(wlist), (wstate, len(wlist))
        cnt = P.emit(nc)
    return nc, cnt, len(P.ops)


_CACHE = {}


def run(inputs, sublayers, nseq, ncores, x_override=None):
    shared = prep_shared(inputs)
    key = (tuple(sublayers), nseq)
    if key not in _CACHE:
        _CACHE[key] = build(list(sublayers), nseq)
    nc, cnt, nops = _CACHE[key]
    in_maps = []
    for core in range(ncores):
        m = dict(shared)
        inp2 = inputs if x_override is None else dict(inputs, x=x_override)
        m.update(prep_core(inp2, core * nseq, nseq))
        in_maps.append(m)
    res = run_bass_kernel_spmd(nc, in_maps, core_ids=list(range(ncores)))
    outs = []
    for core in range(ncores):
        y = res.results[core]["y_out"]
        for s_ in range(nseq):
            outs.append(np.ascontiguousarray(y[s_].transpose(1, 0, 2).reshape(D, S).T))
    return np.stack(outs, 0)


def kernel(**inputs):
    inputs = {k: np.asarray(v) for k, v in inputs.items()}
    out = run(inputs, list(range(8)), 2, NCORES)
    return out.astype(np.float32)
```

```python
import math
from contextlib import ExitStack

import numpy as np
import concourse.bass as bass
import concourse.mybir as mybir
from concourse.bass_utils import run_bass_kernel_spmd

F32 = mybir.dt.float32
BF16 = mybir.dt.bfloat16
I32 = mybir.dt.int32
AF = mybir.ActivationFunctionType
ALU = mybir.AluOpType

D = 1024
S = 2048
DEPTH = 4
NCORES = 8
H = 16
QL = 768
KVL = 256
DFF = 2816
ALPHA = (2.0 * DEPTH) ** 0.25
LN_EPS = 1e-5
RMS_EPS = 1e-6
ATT_SCALE = 96.0 ** -0.5
NSLOT = 4
DEBUG_STOP = 0
LN_POOL = False
LN_DEFER = True
SLOT = 2048
MASKNEG = -30000.0

ENGS = ("pe", "act", "dve", "pool", "sp")


def I(name, *args, **kw):
    def fn(e):
        return getattr(e, name)(*args, **kw)
    return fn


class Op:
    __slots__ = ("id", "eng", "fn", "deps", "is_dma", "dsem", "dcount", "signal", "count")


class Prog:
    def __init__(self, same_eng_sync=True):
        self.ops = []
        self.last_write = {}
        self.readers = {}
        self.dma_counts = {}
        self.same_eng_sync = same_eng_sync
        self.overlaps = {}
        self.touch = {}

    def declare_alias(self, name, others):
        for o in others:
            self.overlaps.setdefault(name, set()).add(o)
            self.overlaps.setdefault(o, set()).add(name)

    def _add(self, eng, fn, reads, writes, is_dma, dsem):
        op = Op()
        op.id = len(self.ops)
        op.eng = eng
        op.fn = fn
        op.is_dma = is_dma
        op.dsem = dsem
        op.signal = False
        op.count = 0
        op.dcount = 0
        if is_dma:
            self.dma_counts[dsem] = self.dma_counts.get(dsem, 0) + 16
            op.dcount = self.dma_counts[dsem]
        deps = set()
        names = set()
        for k in reads:
            names.add(k[0] if isinstance(k, tuple) else k)
            w = self.last_write.get(k)
            if w is not None:
                deps.add(w)
            if isinstance(k, tuple) and k[0] == "ps":
                for ek, r in self.readers.get(k, {}).items():
                    if ek != eng:
                        deps.add(r)
        for k in writes:
            names.add(k[0] if isinstance(k, tuple) else k)
            w = self.last_write.get(k)
            if w is not None:
                deps.add(w)
            for r in self.readers.get(k, {}).values():
                deps.add(r)
        rk = ("dma", op.id) if is_dma else eng
        for n in names:
            for o in self.overlaps.get(n, ()):
                t = self.touch.get(o)
                if t:
                    deps.update(t.values())
        for n in names:
            if n in self.overlaps:
                t = self.touch.setdefault(n, {})
                t[rk] = op.id
                if len(t) > 24:
                    ks = sorted((k for k in t if isinstance(k, tuple)), key=lambda k: t[k])
                    for k in ks[:-8]:
                        del t[k]
        deps.discard(op.id)
        op.deps = deps
        for k in reads:
            self.readers.setdefault(k, {})[rk] = op.id
        for k in writes:
            self.last_write[k] = op.id
            self.readers[k] = {}
        self.ops.append(op)
        return op

    def op(self, eng, fn, reads=(), writes=()):
        return self._add(eng, fn, tuple(reads), tuple(writes), False, None)

    def dma(self, eng, fn, dsem, reads=(), writes=()):
        return self._add(eng, fn, tuple(reads), tuple(writes), True, dsem)

    def emit(self, nc, final_wait_eng="sp"):
        ops = self.ops
        ses = self.same_eng_sync

        def skip(a, op):
            return a.eng == op.eng and (not op.is_dma) and (a.eng == "pe" or not ses)

        for op in ops:
            for d in op.deps:
                a = ops[d]
                if a.is_dma or skip(a, op):
                    continue
                a.signal = True
        cnt = {e: 0 for e in ENGS}
        for op in ops:
            if op.signal:
                cnt[op.eng] += 1
                op.count = cnt[op.eng]
        dsems = sorted(self.dma_counts.keys())
        with ExitStack() as st:
            esem = {e: st.enter_context(nc.semaphore("E_" + e)) for e in ENGS}
            dsem = {d: st.enter_context(nc.semaphore("D_" + str(d))) for d in dsems}
            block = st.enter_context(nc.Block())

            def run_stream(ename, eng):
                waited = {}
                for op in ops:
                    if op.eng != ename:
                        continue
                    need = {}
                    for d in op.deps:
                        a = ops[d]
                        if a.is_dma:
                            key = ("d", a.dsem)
                            val = a.dcount
                        else:
                            if skip(a, op):
                                continue
                            key = ("e", a.eng)
                            val = a.count
                        if val > need.get(key, 0):
                            need[key] = val
                    for key, val in need.items():
                        if waited.get(key, 0) >= val:
                            continue
                        waited[key] = val
                        s = dsem[key[1]] if key[0] == "d" else esem[key[1]]
                        eng.wait_ge(s, val)
                    ins = op.fn(eng)
                    if op.is_dma:
                        ins.then_inc(dsem[op.dsem], 16)
                    elif op.signal:
                        ins.then_inc(esem[ename], 1)
                if ename == final_wait_eng:
                    for d in dsems:
                        if waited.get(("d", d), 0) < self.dma_counts[d]:
                            eng.wait_ge(dsem[d], self.dma_counts[d])

            @block.tensor
            def _(e):
                run_stream("pe", e)

            @block.scalar
            def _(e):
                run_stream("act", e)

            @block.vector
            def _(e):
                run_stream("dve", e)

            @block.gpsimd
            def _(e):
                run_stream("pool", e)

            @block.sync
            def _(e):
                run_stream("sp", e)
        return cnt


def _kc_slab(w, cols):
    K = w.shape[0]
    sub = w[:, cols].reshape(K // 128, 128, len(cols))
    return np.ascontiguousarray(sub.transpose(1, 0, 2)).reshape(128, -1)


def _fm(v):
    return np.ascontiguousarray(v.reshape(-1, 128).T)


VEC_LAYOUT = {}


def _vec_layout():
    if VEC_LAYOUT:
        return VEC_LAYOUT
    off = 0
    for name, n in (("adab", 8 * 24), ("lng", 8 * 8), ("lnb", 8 * 8), ("gq", 2 * 6), ("gkv", 2 * 2),
                    ("lb", 2 * 8), ("gn", 2), ("invf", 1), ("sgn", 1), ("phase", 1)):
        VEC_LAYOUT[name] = (off, n)
        off += n
    VEC_LAYOUT["_total"] = (off, 0)
    return VEC_LAYOUT


def prep_shared(inp):
    a = {}
    f = np.float32
    ar = np.arange
    ada = np.empty((8, 12, 128, SLOT), f)
    for l in range(DEPTH):
        for s2 in range(2):
            w = inp["ada_w"][l, s2]
            for sl in range(12):
                ada[l * 2 + s2, sl] = _kc_slab(w, ar(sl * 256, sl * 256 + 256))
    a["w_ada"] = ada
    mi = np.empty((2, 5, 128, SLOT), f)
    mp = np.empty((2, 8, 128, SLOT), f)
    mo = np.empty((2, 4, 128, SLOT), f)
    for j in range(2):
        w = inp["mla_w_in"][j]
        for sl in range(4):
            mi[j, sl] = _kc_slab(w, ar(sl * 256, sl * 256 + 256))
        rm = ar(1024, 1056)
        rs = np.concatenate([ar(1040, 1056), ar(1024, 1040)])
        mi[j, 4] = 0
        mi[j, 4, :, :1024] = _kc_slab(w, np.concatenate([rm, rm, rs, rs]))
        wq = inp["mla_w_qb"][j]
        wkv = inp["mla_w_kvb"][j]
        for p in range(8):
            h0, h1 = 2 * p, 2 * p + 1
            nope = np.concatenate([ar(h0 * 96, h0 * 96 + 64), ar(h1 * 96, h1 * 96 + 64)])

            def rmain(h):
                return ar(h * 96 + 64, h * 96 + 96)

            def rswap(h):
                return np.concatenate([ar(h * 96 + 80, h * 96 + 96), ar(h * 96 + 64, h * 96 + 80)])
            qcols = np.concatenate([nope, rmain(h0), rmain(h1), rswap(h0), rswap(h1)])
            kn = np.concatenate([ar(h0 * 128, h0 * 128 + 64), ar(h1 * 128, h1 * 128 + 64)])
            vv = np.concatenate([ar(h0 * 128 + 64, h0 * 128 + 128), ar(h1 * 128 + 64, h1 * 128 + 128)])
            mp[j, p, :, :1536] = _kc_slab(wq, qcols)
            mp[j, p, :, 1536:] = _kc_slab(wkv, np.concatenate([kn, vv]))
        for sl in range(4):
            mo[j, sl] = _kc_slab(inp["mla_w_o"][j], ar(sl * 256, sl * 256 + 256))
    a["w_mla_in"] = mi
    a["w_mla_pair"] = mp
    a["w_mla_o"] = mo
    hi = np.empty((2, 8, 2, 128, SLOT), f)
    ho = np.empty((2, 4, 128, SLOT), f)
    for j in range(2):
        w = inp["hgrn_w_in"][j]
        for hd in range(8):
            c0 = ar(hd * 128, hd * 128 + 128)
            hi[j, hd, 0] = _kc_slab(w, np.concatenate([c0, 1024 + c0]))
            hi[j, hd, 1] = _kc_slab(w, np.concatenate([2048 + c0, 3072 + c0]))
        for sl in range(4):
            ho[j, sl] = _kc_slab(inp["hgrn_w_o"][j], ar(sl * 256, sl * 256 + 256))
    a["w_hgrn_in"] = hi
    a["w_hgrn_o"] = ho
    fi = np.empty((DEPTH, 22, 128, SLOT), f)
    fo = np.zeros((DEPTH, 8, 2, 128, SLOT), f)
    for l in range(DEPTH):
        w = inp["ffn_w_in"][l]
        for jj in range(22):
            c0 = ar(jj * 128, jj * 128 + 128)
            fi[l, jj] = _kc_slab(w, np.concatenate([c0, DFF + c0]))
        w2 = inp["ffn_w_out"][l]
        for m in range(8):
            full = _kc_slab(w2, ar(m * 128, m * 128 + 128))
            fo[l, m, 0, :, :1408] = full[:, :1408]
            fo[l, m, 1, :, :1408] = full[:, 1408:]
    a["w_ffn_in"] = fi
    a["w_ffn_out"] = fo
    L = _vec_layout()
    vec = np.zeros((128, L["_total"][0]), f)

    def put(name, arr):
        o, n = L[name]
        vec[:, o:o + n] = arr.reshape(128, n)
    put("adab", np.stack([_fm(inp["ada_b"][l, s2]) for l in range(DEPTH) for s2 in range(2)], 1))
    put("lng", np.stack([_fm(inp["ln_g"][l, s2]) for l in range(DEPTH) for s2 in range(2)], 1))
    put("lnb", np.stack([_fm(inp["ln_b"][l, s2]) for l in range(DEPTH) for s2 in range(2)], 1))
    put("gq", np.stack([_fm(inp["mla_q_norm"][j]) for j in range(2)], 1))
    put("gkv", np.stack([_fm(inp["mla_kv_norm"][j]) for j in range(2)], 1))
    put("lb", np.stack([_fm(inp["hgrn_lb"][j]) for j in range(2)], 1))
    put("gn", np.stack([inp["hgrn_g_norm"][j] for j in range(2)], 1))
    r = ar(128)
    inv_freq = (10000.0 ** (-np.arange(0, 32, 2, dtype=np.float32) / 32)).astype(f)
    put("invf", inv_freq[(r % 32) % 16])
    put("sgn", np.where(r < 64, 1.0, np.where((r % 32) < 16, -1.0, 1.0)).astype(f))
    put("phase", np.where(r < 64, np.pi / 2, 0.0).astype(f))
    a["vecs"] = vec
    cst = np.zeros((128, 4 * 128), f)
    cst[:, 0:128] = 1.0
    cst[:, 128:256] = np.eye(128, dtype=f)
    kk = r[:, None]
    qq = r[None, :]
    cst[:, 256:384] = np.where(qq >= kk, 0.0, MASKNEG)
    cst[:, 384:512] = ((kk // 64 == qq // 64) & (kk <= qq)).astype(f)
    a["consts"] = cst
    return a


def prep_core(inp, b0, nseq):
    f = np.float32
    xs = []
    for b in range(b0, b0 + nseq):
        xt = inp["x"][b].T.reshape(8, 128, S).transpose(1, 0, 2)
        xs.append(np.ascontiguousarray(xt))
    ct = np.stack([_fm(inp["c"][b]) for b in range(b0, b0 + nseq)], 2)
    pos = np.stack([np.broadcast_to(inp["positions"][b][None, :], (128, S)) for b in range(b0, b0 + nseq)], 0)
    return {"x_in": np.stack(xs, 0).astype(f), "c_in": np.ascontiguousarray(ct).astype(f),
            "pos_in": np.ascontiguousarray(pos).astype(np.int32)}


def build(sublayers, nseq, x_is_scaled_input=True):
    nc = bass.Bass("TRN2", target_bir_lowering=False)
    L = _vec_layout()
    NV = L["_total"][0]

    def din(name, shape, dt=F32):
        return nc.dram_tensor(name, list(shape), dt, kind="ExternalInput").ap()
    x_in = din("x_in", [nseq, 128, 8, S])
    c_in = din("c_in", [128, 8, nseq])
    pos_in = din("pos_in", [nseq, 128, S], I32)
    w_ada = din("w_ada", [8, 12, 128, SLOT])
    w_mla_in = din("w_mla_in", [2, 5, 128, SLOT])
    w_mla_pair = din("w_mla_pair", [2, 8, 128, SLOT])
    w_mla_o = din("w_mla_o", [2, 4, 128, SLOT])
    w_hgrn_in = din("w_hgrn_in", [2, 8, 2, 128, SLOT])
    w_hgrn_o = din("w_hgrn_o", [2, 4, 128, SLOT])
    w_ffn_in = din("w_ffn_in", [DEPTH, 22, 128, SLOT])
    w_ffn_out = din("w_ffn_out", [DEPTH, 8, 2, 128, SLOT])
    vecs_in = din("vecs", [128, NV])
    consts_in = din("consts", [128, 512])
    y_out = nc.dram_tensor("y_out", [nseq, 128, 8, S], F32, kind="ExternalOutput").ap()
    ogd = nc.dram_tensor("og_scratch", [8, 128, S], BF16, kind="Internal").ap()

    P = Prog()
    last_sub = sublayers[-1]
    with ExitStack() as st:
        def sb(name, shape, dt):
            return st.enter_context(nc.sbuf_tensor(name, list(shape), dt))
        xp = sb("xp", [128, 8, S], F32)
        hb = sb("hb", [128, 8, S], BF16)
        ring = sb("ring", [128, NSLOT, SLOT], BF16)
        vecs = sb("vecs_sb", [128, NV], F32)
        cst = sb("cst_sb", [128, 512], F32)
        ones_f = cst[:, 0:128]
        mask2_f = cst[:, 384:512]
        ones_bf = sb("ones_bf", [128, 128], BF16)
        ident_bf = sb("ident_bf", [128, 128], BF16)
        maskT_bf = sb("maskT_bf", [128, 128], BF16)
        modall = sb("modall", [128, 8, 24, nseq], F32)
        c_sb = sb("c_sb", [128, 8 * nseq], F32)
        sc_bf = sb("sc_bf", [128, 8, nseq], BF16)
        T_GATE1 = sb("T_GATE1", [128, 8, nseq, 8], F32)
        T_S1 = sb("T_S1", [128, 8, nseq, 8], F32)
        T_SH = sb("T_SH", [128, 8, nseq, 8], F32)
        T_GA = sb("T_GA", [128, 8, 8], F32)
        T_BA = sb("T_BA", [128, 8, 8], F32)
        T_HG = sb("T_HG", [128, 8, nseq, 8], F32)
        T_HB = sb("T_HB", [128, 8, nseq, 8], F32)
        T_H0 = sb("T_H0", [128, nseq, 8], F32)
        T_LB = sb("T_LB", [128, 2, 8], F32)
        T_OML = sb("T_OML", [128, 2, 8], F32)
        T_NOML = sb("T_NOML", [128, 2, 8], F32)
        T_tmp = sb("T_tmp", [128, 64], F32)
        eps_t = sb("eps_t", [128, 2], F32)
        eps_rms = eps_t[:, 0:1]
        remaining = nc.sbuf_bytes_remaining
        ARENA = (remaining - 256) // 4 * 4
        print('ARENA bytes', ARENA)
        arena = sb("arena", [128, ARENA // 4], F32)
        ps = st.enter_context(nc.psum_tensor("ps", [128, 4096], F32))

        class AV:
            def __init__(self, name, off, shape, dt):
                self.name = name
                esz = 2 if dt == BF16 else 4
                n = int(np.prod(shape))
                self.off = off
                self.end = off + n * esz
                assert self.end <= ARENA, (name, self.end, ARENA)
                v = arena[:, off // 4:(off + n * esz) // 4]
                if dt == BF16:
                    v = v.bitcast(BF16)
                if len(shape) == 2:
                    v = v.rearrange("p (a b) -> p a b", b=shape[1])
                elif len(shape) == 3:
                    v = v.rearrange("p (a b c) -> p a b c", b=shape[1], c=shape[2])
                self.ap = v

        avs = []

        def av(name, off, shape, dt):
            a = AV(name, off, shape, dt)
            for o in avs:
                if a.off < o.end and o.off < a.end:
                    P.declare_alias(a.name, [o.name])
            avs.append(a)
            return a

        K = 1024
        A_act = av("f_act", 0, [22, 1024], BF16)
        A_sgt = av("f_sgt", 45056, [2, 1024], BF16)
        LNO = 49152
        A_lsq = av("ln_sq", LNO, [2, 512], F32)
        A_lmean = av("ln_mean", LNO + 4096, [512], F32)
        A_lvar = av("ln_var", LNO + 6144, [512], F32)
        A_lrstd = av("ln_rstd", LNO + 8192, [512], F32)
        A_lt = av("ln_t", LNO + 10240, [2, 512], F32)
        M_qg = av("m_qg", 0, [6, S], BF16)
        M_kvn = av("m_kvn", 24576, [2, S], BF16)
        M_rq = av("m_rq", 32768, [S], F32)
        M_cs = av("m_cs", 40960, [S], F32)
        M_krope = av("m_krope", 63488, [S], BF16)
        M_qk = av("m_qk", 67584, [4, S], BF16)
        M_kvlat = av("m_kvlat", 67584, [2, 1024], F32)
        M_sq = av("m_sq", 75776, [2, 1024], F32)
        M_rkv = av("m_rkv", 49152, [1024], F32)
        M_sd = av("m_sd", 53248, [1024], F32)
        M_vaug = av("m_vaug", 49152, [2, 16, 128], BF16)
        M_pt = av("m_pt", 57344, [4, 512], BF16)
        M_rcp = av("m_rcp", 61440, [512], F32)
        M_tmpa = av("m_tmpa", 83968, [1024], F32)
        M_tmpb = av("m_tmpb", 57344, [1024], F32)
        M_posi = av("m_posi", 0, [S], F32)
        M_ki = av("m_ki", 24576, [S], F32)
        M_kf = av("m_kf", 32768, [S], F32)
        G_T = [av("g_T%d" % i, i * 8192, [S], F32) for i in range(4)]
        G_qt = av("g_qt", 32768, [S], BF16)
        G_kt = av("g_kt", 36864, [S], BF16)
        G_kh = av("g_kh", 40960, [S], BF16)
        G_khT = av("g_khT", 45056, [16, 128], BF16)
        G_vtok = av("g_vtok", 63488, [16, 128], BF16)
        G_sg = av("g_sg", 67584, [S], F32)
        G_sbf = av("g_sbf", 75776, [32, 128], BF16)
        G_smask = av("g_smask", 83968, [S], BF16)
        G_u = av("g_u", 16384, [4096], F32)
        G_eblrep = av("g_eblrep", 49152, [1024], F32)
        G_sout = av("g_sout", 53248, [1024], F32)
        G_asb = av("g_asb", 57344, [4, 128], BF16)
        G_osq = av("g_osq", 49152, [1024], F32)
        G_rstd = av("g_rstd", 53248, [1024], F32)
        G_to = av("g_to", 57344, [1024], F32)
        G_og = av("g_og", 61440, [1024], BF16)
        assert ARENA >= 88064 + 256, ARENA
        G_ebl2 = av("g_ebl2", 88064, [32], F32)

        def xk(c, t0, t1):
            return [("xp", c, q) for q in range(t0 // 512, (t1 + 511) // 512)]

        def hk(c, t0, t1):
            return [("hb", c, q) for q in range(t0 // 512, (t1 + 511) // 512)]

        def pk(*banks):
            return [("ps", b) for b in banks]

        def bank(b, n=512, off=0):
            return ps[:, b * 512 + off: b * 512 + off + n]

        def slot2(s):
            return ps[:, s * 1024:(s + 1) * 1024]

        wlist = []
        wstate = {"issued": 0, "next": 0}

        def wissue(upto):
            while wstate["issued"] < min(upto, len(wlist)):
                i = wstate["issued"]
                src, n = wlist[i]
                sl = i % NSLOT
                if n > 1024:
                    dst = ring[:, sl, 0:n].rearrange("p (a b) -> p a b", a=2)
                    srcv = src[:, 0:n].rearrange("p (a b) -> p a b", a=2)
                else:
                    dst = ring[:, sl, 0:n]
                    srcv = src[:, 0:n]
                P.dma("pool", (I("dma_start", out=dst, in_=srcv)), "ring%d" % sl,
                      writes=[("ring", sl)])
                wstate["issued"] += 1

        def wget():
            i = wstate["next"]
            wstate["next"] += 1
            wissue(i + NSLOT)
            sl = i % NSLOT
            return ring[:, sl, :], ("ring", sl)

        subs_needed = sorted(set(sublayers))
        for sub in subs_needed:
            for sl in range(12):
                wlist.append((w_ada[sub, sl], 2048))
        for s_ in range(nseq):
            for sub in sublayers:
                l, typ = sub // 2, sub % 2
                j = l // 2
                if typ == 1:
                    for half in range(2):
                        for jj in range(22):
                            wlist.append((w_ffn_in[l, jj], 2048))
                        for m in range(8):
                            for kh in range(2):
                                wlist.append((w_ffn_out[l, m, kh], 1408))
                elif l % 2 == 0:
                    for half in range(2):
                        for sl in range(4):
                            wlist.append((w_mla_in[j, sl], 2048))
                        wlist.append((w_mla_in[j, 4], 1024))
                    for p in range(8):
                        wlist.append((w_mla_pair[j, p], 2048))
                    for _h in range(2):
                        for sl in range(4):
                            wlist.append((w_mla_o[j, sl], 2048))
                else:
                    for hd in range(8):
                        wlist.append((w_hgrn_in[j, hd, 0], 2048))
                        wlist.append((w_hgrn_in[j, hd, 1], 2048))
                    for _h in range(2):
                        for sl in range(4):
                            wlist.append((w_hgrn_o[j, sl], 2048))

        P.dma("sp", I("dma_start", out=vecs[:], in_=vecs_in), "ld_v", writes=["vecs"])
        P.dma("sp", I("dma_start", out=cst[:], in_=consts_in), "ld_c", writes=["cst"])
        P.dma("sp", I("dma_start", out=c_sb[:], in_=c_in.rearrange("p a b -> p (a b)")), "ld_cc",
              writes=["c_sb"])
        wissue(NSLOT)
        P.op("dve", I("memset", eps_t[:, 0:1], RMS_EPS), writes=["eps_t"])
        P.op("dve", I("tensor_copy", out=ones_bf[:], in_=cst[:, 0:128]), reads=["cst"], writes=["ones_bf"])
        P.op("dve", I("tensor_copy", out=ident_bf[:], in_=cst[:, 128:256]), reads=["cst"], writes=["ident_bf"])
        P.op("dve", I("tensor_copy", out=maskT_bf[:], in_=cst[:, 256:384]), reads=["cst"], writes=["maskT_bf"])
        P.op("act", I("activation", out=sc_bf[:].rearrange("p a b -> p (a b)"), in_=c_sb[:], func=AF.Silu),
             reads=["c_sb"], writes=["sc_bf"])

        def vcol(name, i):
            o, n = L[name]
            return vecs[:, o + i:o + i + 1]

        def vrange(name, i0, n):
            o, _ = L[name]
            return vecs[:, o + i0:o + i0 + n]

        for sub in subs_needed:
            for sl in range(12):
                w, wk = wget()
                for mi in range(2):
                    m = sl * 2 + mi
                    for kc in range(8):
                        P.op("pe", (I("matmul", ps[:, 7 * 512 + m * nseq: 7 * 512 + (m + 1) * nseq],
                            lhsT=w[:, kc * 256 + mi * 128: kc * 256 + mi * 128 + 128],
                            rhs=sc_bf[:, kc, :], start=(kc == 0), stop=(kc == 7))),
                            reads=[wk, "sc_bf"], writes=pk(7))
            o_ab = L["adab"][0]
            P.op("dve", (I("tensor_tensor", out=modall[:, sub, :, :],
                in0=ps[:, 7 * 512: 7 * 512 + 24 * nseq].rearrange("p (a b) -> p a b", b=nseq),
                in1=vecs[:, o_ab + sub * 24: o_ab + sub * 24 + 24].rearrange("p (a b) -> p a b", b=1).to_broadcast([128, 24, nseq]),
                op=ALU.add)), reads=pk(7) + ["vecs"], writes=[("mod", sub)])
            P.op("dve", (I("tensor_copy", out=T_SH[:, sub, :, :], in_=modall[:, sub, 0:8, :].rearrange("p c s -> p s c"))),
                reads=[("mod", sub)], writes=[("T_SH", sub)])
            P.op("dve", (I("tensor_scalar", out=T_S1[:, sub, :, :], in0=modall[:, sub, 8:16, :].rearrange("p c s -> p s c"),
                scalar1=1.0, scalar2=None, op0=ALU.add)),
                reads=[("mod", sub)], writes=[("T_S1", sub)])
            P.op("dve", (I("tensor_scalar", out=T_GATE1[:, sub, :, :], in0=modall[:, sub, 16:24, :].rearrange("p c s -> p s c"),
                scalar1=1.0, scalar2=None, op0=ALU.add)),
                reads=[("mod", sub)], writes=[("T_GATE1", sub)])
        o_g, o_b = L["lng"][0], L["lnb"][0]
        for idx, sub in enumerate(sublayers):
            is_last = (idx == len(sublayers) - 1)
            fac = 1.0 if is_last else ALPHA
            P.op("dve", (I("tensor_scalar", out=T_GA[:, sub, :], in0=vecs[:, o_g + sub * 8:o_g + sub * 8 + 8], scalar1=fac, scalar2=None, op0=ALU.mult)),
                reads=["vecs"], writes=[("T_GA", sub)])
            P.op("dve", (I("tensor_scalar", out=T_BA[:, sub, :], in0=vecs[:, o_b + sub * 8:o_b + sub * 8 + 8], scalar1=fac, scalar2=None, op0=ALU.mult)),
                reads=["vecs"], writes=[("T_BA", sub)])
            if not is_last:
                nx = sublayers[idx + 1]
                for s_ in range(nseq):
                    P.op("dve", (I("tensor_tensor", out=T_HG[:, sub, s_, :], in0=T_S1[:, nx, s_, :], in1=vecs[:, o_g + sub * 8:o_g + sub * 8 + 8], op=ALU.mult)),
                        reads=["vecs", ("T_S1", nx)], writes=[("T_HG", sub, s_)])
                    P.op("dve", (I("tensor_tensor", out=T_HB[:, sub, s_, :], in0=T_S1[:, nx, s_, :], in1=vecs[:, o_b + sub * 8:o_b + sub * 8 + 8], op=ALU.mult)),
                        reads=["vecs", ("T_S1", nx)], writes=[("T_HB", sub, s_)])
                    P.op("dve", (I("tensor_tensor", out=T_HB[:, sub, s_, :], in0=T_HB[:, sub, s_, :], in1=T_SH[:, nx, s_, :], op=ALU.add)),
                        reads=[("T_SH", nx), ("T_HB", sub, s_)], writes=[("T_HB", sub, s_)])
        o_lb = L["lb"][0]
        lb0 = vecs[:, o_lb:o_lb + 8]
        lb1 = vecs[:, o_lb + 8:o_lb + 16]
        tm = T_tmp[:, 0:8]
        e0 = T_tmp[:, 8:16]
        e1 = T_tmp[:, 16:24]
        ssum = T_tmp[:, 24:32]
        rs_ = T_tmp[:, 32:40]
        s0 = T_tmp[:, 40:48]
        s1 = T_tmp[:, 48:56]
        cum = T_tmp[:, 56:64]
        P.op("dve", I("tensor_tensor", out=tm, in0=lb0, in1=lb1, op=ALU.max), reads=["vecs"], writes=["tt0"])
        P.op("dve", I("tensor_tensor", out=e0, in0=lb0, in1=tm, op=ALU.subtract), reads=["vecs", "tt0"], writes=["tt1"])
        P.op("dve", I("tensor_tensor", out=e1, in0=lb1, in1=tm, op=ALU.subtract), reads=["vecs", "tt0"], writes=["tt2"])
        P.op("act", I("activation", out=T_tmp[:, 8:24], in_=T_tmp[:, 8:24], func=AF.Exp), reads=["tt1", "tt2"], writes=["tt3"])
        P.op("dve", I("tensor_tensor", out=ssum, in0=e0, in1=e1, op=ALU.add), reads=["tt3"], writes=["tt4"])
        P.op("dve", I("reciprocal", out=rs_, in_=ssum), reads=["tt4"], writes=["tt5"])
        P.op("dve", I("tensor_tensor", out=s0, in0=e0, in1=rs_, op=ALU.mult), reads=["tt3", "tt5"], writes=["tt6"])
        P.op("dve", I("tensor_tensor", out=s1, in0=e1, in1=rs_, op=ALU.mult), reads=["tt3", "tt5"], writes=["tt7"])
        P.op("dve", I("tensor_tensor", out=cum, in0=s0, in1=s1, op=ALU.add), reads=["tt6", "tt7"], writes=["tt8"])
        P.op("dve", I("tensor_tensor", out=T_LB[:, 0, :], in0=s0, in1=s0, op=ALU.subtract), reads=["tt6"], writes=["lb_0"])
        P.op("dve", I("tensor_tensor", out=T_LB[:, 1, :], in0=cum, in1=s0, op=ALU.subtract), reads=["tt8", "tt6"], writes=["lb_1"])
        P.op("dve", I("tensor_scalar", out=T_OML[:].rearrange("p a b -> p (a b)"), in0=T_LB[:].rearrange("p a b -> p (a b)"),
                                              scalar1=-1.0, scalar2=1.0, op0=ALU.mult, op1=ALU.add),
             reads=["lb_0", "lb_1"], writes=["oml"])
        P.op("dve", I("tensor_scalar", out=T_NOML[:].rearrange("p a b -> p (a b)"), in0=T_LB[:].rearrange("p a b -> p (a b)"),
                                              scalar1=1.0, scalar2=-1.0, op0=ALU.mult, op1=ALU.add),
             reads=["lb_0", "lb_1"], writes=["noml"])

        def ln_chunk(sub, s_, tq, is_last):
            t0 = tq * 512
            b1, b2 = (2 * tq) % 8, (2 * tq + 1) % 8
            mean = A_lmean.ap
            var = A_lvar.ap
            rstd = A_lrstd.ap
            zs, ss = var, rstd
            for c in range(8):
                sq = A_lsq.ap[:, c % 2, :]
                P.op("act", (I("activation", out=sq, in_=xp[:, c, t0:t0 + 512], func=AF.Square)),
                     reads=xk(c, t0, t0 + 512), writes=[("ln_sq", c % 2)])
                if not LN_POOL:
                    P.op("pe", I("matmul", bank(b1), lhsT=ones_f, rhs=xp[:, c, t0:t0 + 512], start=(c == 0), stop=(c == 7)),
                         reads=xk(c, t0, t0 + 512) + ["cst"], writes=pk(b1))
                    P.op("pe", I("matmul", bank(b2), lhsT=ones_f, rhs=sq, start=(c == 0), stop=(c == 7)),
                         reads=[("ln_sq", c % 2), "cst"], writes=pk(b2))
                elif c == 1:
                    P.op("pool", I("tensor_tensor", out=zs, in0=xp[:, 0, t0:t0 + 512], in1=xp[:, 1, t0:t0 + 512], op=ALU.add),
                         reads=xk(0, t0, t0 + 512) + xk(1, t0, t0 + 512), writes=["ln_var"])
                    P.op("pool", I("tensor_tensor", out=ss, in0=A_lsq.ap[:, 0, :], in1=A_lsq.ap[:, 1, :], op=ALU.add),
                         reads=[("ln_sq", 0), ("ln_sq", 1)], writes=["ln_rstd"])
                elif c > 1:
                    P.op("pool", I("tensor_tensor", out=zs, in0=zs, in1=xp[:, c, t0:t0 + 512], op=ALU.add),
                         reads=xk(c, t0, t0 + 512) + ["ln_var"], writes=["ln_var"])
                    P.op("pool", I("tensor_tensor", out=ss, in0=ss, in1=sq, op=ALU.add),
                         reads=[("ln_sq", c % 2), "ln_rstd"], writes=["ln_rstd"])
            if LN_POOL:
                P.op("pe", I("matmul", bank(b1), lhsT=ones_f, rhs=zs, start=True, stop=True),
                     reads=["ln_var", "cst"], writes=pk(b1))
                P.op("pe", I("matmul", bank(b2), lhsT=ones_f, rhs=ss, start=True, stop=True),
                     reads=["ln_rstd", "cst"], writes=pk(b2))
            P.op("act", I("activation", out=mean, in_=bank(b1), func=AF.Copy, scale=1.0 / D),
                 reads=pk(b1), writes=["ln_mean"])
            P.op("dve", I("tensor_tensor", out=var, in0=mean, in1=mean, op=ALU.mult),
                 reads=["ln_mean"], writes=["ln_var"])
            P.op("dve", I("scalar_tensor_tensor", out=var, in0=bank(b2), scalar=1.0 / D, in1=var,
                                                         op0=ALU.mult, op1=ALU.subtract),
                 reads=pk(b2) + ["ln_var"], writes=["ln_var"])
            P.op("dve", I("tensor_scalar", out=var, in0=var, scalar1=LN_EPS, scalar2=None, op0=ALU.add),
                 reads=["ln_var"], writes=["ln_var"])
            P.op("act", I("activation", out=rstd, in_=var, func=AF.Sqrt),
                 reads=["ln_var"], writes=["ln_rstd"])
            P.op("dve", I("reciprocal", out=rstd, in_=rstd), reads=["ln_rstd"], writes=["ln_rstd"])
            for c in range(8):
                t = A_lt.ap[:, c % 2, :]
                tk = ("ln_t", c % 2)
                P.op("dve", (I("tensor_tensor", out=t, in0=xp[:, c, t0:t0 + 512], in1=mean, op=ALU.subtract)),
                     reads=xk(c, t0, t0 + 512) + ["ln_mean"], writes=[tk])
                P.op("dve", (I("tensor_tensor", out=t, in0=t, in1=rstd, op=ALU.mult)),
                     reads=[tk, "ln_rstd"], writes=[tk])
                P.op("act", (I("activation", out=xp[:, c, t0:t0 + 512], in_=t, func=AF.Identity,
                                                              scale=T_GA[:, sub, c:c + 1], bias=T_BA[:, sub, c:c + 1])),
                     reads=[tk, ("T_GA", sub), ("T_BA", sub)], writes=xk(c, t0, t0 + 512))
                if not is_last:
                    P.op("act", (I("activation", out=hb[:, c, t0:t0 + 512], in_=t, func=AF.Identity,
                                                                  scale=T_HG[:, sub, s_, c:c + 1], bias=T_HB[:, sub, s_, c:c + 1])),
                         reads=[tk, ("T_HG", sub, s_), ("T_HB", sub, s_)], writes=hk(c, t0, t0 + 512))

        def out_proj(sub, s_, is_last):
            cnt = 0
            for hf in range(2):
                th = hf * 1024
                for sl in range(4):
                    w, wk = wget()
                    for mi in range(2):
                        m = sl * 2 + mi
                        sidx = cnt % 4
                        cnt += 1
                        y = slot2(sidx)
                        for tc in range(2):
                            for kc in range(8):
                                P.op("pe", I("matmul", y[:, tc * 512:(tc + 1) * 512],
                                             lhsT=w[:, kc * 256 + mi * 128: kc * 256 + mi * 128 + 128],
                                             rhs=hb[:, kc, th + tc * 512: th + tc * 512 + 512],
                                             start=(kc == 0), stop=(kc == 7)),
                                     reads=[wk] + hk(kc, th + tc * 512, th + tc * 512 + 512),
                                     writes=pk(2 * sidx + tc))
                        P.op("dve", I("scalar_tensor_tensor", out=xp[:, m, th:th + 1024], in0=y,
                                      scalar=T_GATE1[:, sub, s_, m:m + 1], in1=xp[:, m, th:th + 1024],
                                      op0=ALU.mult, op1=ALU.add),
                             reads=pk(2 * sidx, 2 * sidx + 1) + [("T_GATE1", sub)] + xk(m, th, th + 1024),
                             writes=xk(m, th, th + 1024))
                    if LN_DEFER and hf == 1 and sl in (0, 1) and not DEBUG_STOP:
                        ln_chunk(sub, s_, sl, is_last)
                if (not LN_DEFER) and hf == 0 and not DEBUG_STOP:
                    ln_chunk(sub, s_, 0, is_last)
                    ln_chunk(sub, s_, 1, is_last)
            if not DEBUG_STOP:
                ln_chunk(sub, s_, 2, is_last)
                ln_chunk(sub, s_, 3, is_last)

        def ffn(sub, s_, is_last):
            for half in range(2):
                th = half * 1024
                for jj in range(22):
                    w, wk = wget()
                    sg_i, su_i = (2 * jj) % 4, (2 * jj + 1) % 4
                    g_ps, u_ps = slot2(sg_i), slot2(su_i)
                    for tc in range(2):
                        for kc in range(8):
                            rhs = hb[:, kc, th + tc * 512: th + tc * 512 + 512]
                            rk = [wk] + hk(kc, th + tc * 512, th + tc * 512 + 512)
                            P.op("pe", (I("matmul", g_ps[:, tc * 512:(tc + 1) * 512], lhsT=w[:, kc * 256: kc * 256 + 128], rhs=rhs,
                                start=(kc == 0), stop=(kc == 7))), reads=rk, writes=pk(2 * sg_i + tc))
                            P.op("pe", (I("matmul", u_ps[:, tc * 512:(tc + 1) * 512], lhsT=w[:, kc * 256 + 128: kc * 256 + 256], rhs=rhs,
                                start=(kc == 0), stop=(kc == 7))), reads=rk, writes=pk(2 * su_i + tc))
                    sgt = A_sgt.ap[:, jj % 2, :]
                    P.op("act", (I("activation", out=sgt, in_=g_ps, func=AF.Silu)),
                         reads=pk(2 * sg_i, 2 * sg_i + 1), writes=[("f_sgt", jj % 2)])
                    P.op("dve", (I("tensor_tensor", out=A_act.ap[:, jj, :], in0=u_ps, in1=sgt, op=ALU.mult)),
                        reads=pk(2 * su_i, 2 * su_i + 1) + [("f_sgt", jj % 2)], writes=[("f_act", jj)])
                    if LN_DEFER and half == 1 and jj in (1, 3):
                        ln_chunk(sub, s_, jj // 2, is_last)
                for m in range(8):
                    sidx = m % 4
                    y = slot2(sidx)
                    for kh in range(2):
                        w, wk = wget()
                        for tc in range(2):
                            for kk in range(11):
                                kc = kh * 11 + kk
                                P.op("pe", (I("matmul", y[:, tc * 512:(tc + 1) * 512], lhsT=w[:, kk * 128: kk * 128 + 128],
                                    rhs=A_act.ap[:, kc, tc * 512:(tc + 1) * 512],
                                    start=(kc == 0), stop=(kc == 21))),
                                    reads=[wk, ("f_act", kc)], writes=pk(2 * sidx + tc))
                    P.op("dve", (I("scalar_tensor_tensor", out=xp[:, m, th:th + 1024], in0=y, scalar=T_GATE1[:, sub, s_, m:m + 1],
                        in1=xp[:, m, th:th + 1024], op0=ALU.mult, op1=ALU.add)),
                        reads=pk(2 * sidx, 2 * sidx + 1) + [("T_GATE1", sub)] + xk(m, th, th + 1024),
                        writes=xk(m, th, th + 1024))
                if not LN_DEFER:
                    ln_chunk(sub, s_, 2 * half, is_last)
                    ln_chunk(sub, s_, 2 * half + 1, is_last)
                elif half == 1:
                    ln_chunk(sub, s_, 2, is_last)
                    ln_chunk(sub, s_, 3, is_last)

        def dbg_dump(items):
            for c, apx, r0, r1, keys in items:
                n = apx.shape[-1]
                P.op("dve", I("tensor_copy", out=xp[r0:r1, c, 0:n], in_=apx), reads=keys, writes=xk(c, 0, S))

        def rope_tables(s_):
            cs = M_cs.ap
            posi = M_posi.ap.bitcast(I32)
            ki = M_ki.ap.bitcast(I32)
            kf = M_kf.ap
            P.dma("sp", I("dma_start", out=posi, in_=pos_in[s_]), "ld_pos", writes=["m_posi"])
            P.op("dve", I("tensor_copy", out=cs, in_=posi), reads=["m_posi"], writes=["m_cs"])
            P.op("dve", I("tensor_scalar", out=cs, in0=cs, scalar1=vcol("invf", 0), scalar2=None, op0=ALU.mult),
                 reads=["m_cs", "vecs"], writes=["m_cs"])
            P.op("dve", I("tensor_scalar", out=cs, in0=cs, scalar1=vcol("phase", 0), scalar2=None, op0=ALU.add),
                 reads=["m_cs", "vecs"], writes=["m_cs"])
            P.op("dve", I("tensor_scalar", out=kf, in0=cs, scalar1=1.0 / (2 * math.pi), scalar2=None, op0=ALU.mult),
                 reads=["m_cs"], writes=["m_kf"])
            P.op("dve", I("tensor_copy", out=ki, in_=kf), reads=["m_kf"], writes=["m_ki"])
            P.op("dve", I("tensor_copy", out=kf, in_=ki), reads=["m_ki"], writes=["m_kf"])
            P.op("dve", I("scalar_tensor_tensor", out=cs, in0=kf, scalar=-2.0 * math.pi, in1=cs,
                          op0=ALU.mult, op1=ALU.add),
                 reads=["m_kf", "m_cs"], writes=["m_cs"])
            for _ in range(2):
                P.op("dve", I("tensor_scalar", out=kf, in0=cs, scalar1=math.pi, scalar2=-2.0 * math.pi,
                              op0=ALU.is_gt, op1=ALU.mult),
                     reads=["m_cs"], writes=["m_kf"])
                P.op("dve", I("tensor_tensor", out=cs, in0=cs, in1=kf, op=ALU.add),
                     reads=["m_cs", "m_kf"], writes=["m_cs"])
                P.op("dve", I("tensor_scalar", out=kf, in0=cs, scalar1=-math.pi, scalar2=2.0 * math.pi,
                              op0=ALU.is_lt, op1=ALU.mult),
                     reads=["m_cs"], writes=["m_kf"])
                P.op("dve", I("tensor_tensor", out=cs, in0=cs, in1=kf, op=ALU.add),
                     reads=["m_cs", "m_kf"], writes=["m_cs"])
            P.op("dve", I("tensor_scalar", out=cs, in0=cs, scalar1=math.pi, scalar2=-math.pi,
                          op0=ALU.min, op1=ALU.max),
                 reads=["m_cs"], writes=["m_cs"])
            P.op("act", I("activation", out=cs, in_=cs, func=AF.Sin), reads=["m_cs"], writes=["m_cs"])
            P.op("dve", I("tensor_scalar", out=cs, in0=cs, scalar1=vcol("sgn", 0), scalar2=None, op0=ALU.mult),
                 reads=["m_cs", "vecs"], writes=["m_cs"])

        def mla(sub, s_, j, is_last):
            rope_tables(s_)
            cs = M_cs.ap
            rq = M_rq.ap
            gq0 = L["gq"][0] + j * 6
            gkv0 = L["gkv"][0] + j * 2
            if DEBUG_STOP == 1:
                dbg_dump([(0, M_cs.ap, 0, 128, ["m_cs"])])
                return
            for hf in range(2):
                th = hf * 1024
                def emit_ssq_q(mm):
                    sqm = M_sq.ap[:, mm % 2, :]
                    for tc in range(2):
                        P.op("pe", I("matmul", bank(6 + tc), lhsT=ones_f, rhs=sqm[:, tc * 512:(tc + 1) * 512],
                                     start=(mm == 0), stop=(mm == 5)),
                             reads=[("m_sq", mm % 2), "cst"], writes=pk(6 + tc))

                for sl in range(3):
                    w, wk = wget()
                    for mi in range(2):
                        m = sl * 2 + mi
                        sidx = m % 3
                        qps = slot2(sidx)
                        for tc in range(2):
                            for kc in range(8):
                                P.op("pe", (I("matmul", qps[:, tc * 512:(tc + 1) * 512],
                                    lhsT=w[:, kc * 256 + mi * 128: kc * 256 + mi * 128 + 128],
                                    rhs=hb[:, kc, th + tc * 512: th + tc * 512 + 512],
                                    start=(kc == 0), stop=(kc == 7))),
                                    reads=[wk] + hk(kc, th + tc * 512, th + tc * 512 + 512), writes=pk(2 * sidx + tc))
                        P.op("act", (I("activation", out=M_qg.ap[:, m, th:th + 1024], in_=qps, func=AF.Identity, scale=vecs[:, gq0 + m: gq0 + m + 1])),
                            reads=pk(2 * sidx, 2 * sidx + 1) + ["vecs"], writes=[("m_qg", m, hf)])
                        sq = M_sq.ap[:, m % 2, :]
                        P.op("act", (I("activation", out=sq, in_=qps, func=AF.Square)),
                             reads=pk(2 * sidx, 2 * sidx + 1), writes=[("m_sq", m % 2)])
                        if m >= 1:
                            emit_ssq_q(m - 1)

                def finish_q():
                    emit_ssq_q(5)
                    sd = M_sd.ap
                    P.op("dve", I("tensor_scalar", out=sd, in0=slot2(3), scalar1=1.0 / QL, scalar2=RMS_EPS,
                                  op0=ALU.mult, op1=ALU.add), reads=pk(6, 7), writes=["m_sd"])
                    P.op("act", I("activation", out=sd, in_=sd, func=AF.Sqrt), reads=["m_sd"], writes=["m_sd"])
                    P.op("dve", I("reciprocal", out=rq[:, th:th + 1024], in_=sd), reads=["m_sd"], writes=[("m_rq", hf)])
                w, wk = wget()
                for m in range(2):
                    sidx = m
                    kps = slot2(sidx)
                    for tc in range(2):
                        for kc in range(8):
                            P.op("pe", (I("matmul", kps[:, tc * 512:(tc + 1) * 512],
                                lhsT=w[:, kc * 256 + m * 128: kc * 256 + m * 128 + 128],
                                rhs=hb[:, kc, th + tc * 512: th + tc * 512 + 512],
                                start=(kc == 0), stop=(kc == 7))),
                                reads=[wk] + hk(kc, th + tc * 512, th + tc * 512 + 512), writes=pk(2 * sidx + tc))
                    P.op("dve", (I("tensor_copy", out=M_kvlat.ap[:, m, :], in_=kps)),
                         reads=pk(2 * sidx, 2 * sidx + 1), writes=[("m_kvlat", m)])
                    if m == 0:
                        finish_q()
                    sq = M_sq.ap[:, m % 2, :]
                    P.op("act", (I("activation", out=sq, in_=kps, func=AF.Square)),
                         reads=pk(2 * sidx, 2 * sidx + 1), writes=[("m_sq", m % 2)])
                    if m == 1:
                        for mm in range(2):
                            sqm = M_sq.ap[:, mm % 2, :]
                            for tc in range(2):
                                P.op("pe", I("matmul", bank(4 + tc), lhsT=ones_f, rhs=sqm[:, tc * 512:(tc + 1) * 512],
                                             start=(mm == 0), stop=(mm == 1)),
                                     reads=[("m_sq", mm % 2), "cst"], writes=pk(4 + tc))
                rkv = M_rkv.ap
                P.op("dve", I("tensor_scalar", out=rkv, in0=slot2(2), scalar1=1.0 / KVL, scalar2=RMS_EPS,
                                                      op0=ALU.mult, op1=ALU.add),
                     reads=pk(4, 5), writes=["m_rkv"])
                P.op("act", I("activation", out=rkv, in_=rkv, func=AF.Sqrt), reads=["m_rkv"], writes=["m_rkv"])
                P.op("dve", I("reciprocal", out=rkv, in_=rkv), reads=["m_rkv"], writes=["m_rkv"])
                for m in range(2):
                    P.op("dve", (I("scalar_tensor_tensor", out=M_kvn.ap[:, m, th:th + 1024], in0=M_kvlat.ap[:, m, :], scalar=vecs[:, gkv0 + m: gkv0 + m + 1],
                        in1=rkv, op0=ALU.mult, op1=ALU.mult)),
                        reads=[("m_kvlat", m), "m_rkv", "vecs"], writes=[("m_kvn", m, hf)])
                if DEBUG_STOP == 6:
                    return
                w, wk = wget()
                rps = slot2(3)
                for tc in range(2):
                    for kc in range(8):
                        P.op("pe", (I("matmul", rps[:, tc * 512:(tc + 1) * 512], lhsT=w[:, kc * 128: kc * 128 + 128],
                            rhs=hb[:, kc, th + tc * 512: th + tc * 512 + 512], start=(kc == 0), stop=(kc == 7))),
                            reads=[wk] + hk(kc, th + tc * 512, th + tc * 512 + 512), writes=pk(6 + tc))
                if DEBUG_STOP == 7:
                    return
                ta = M_tmpa.ap
                P.op("dve", (I("tensor_tensor", out=ta[0:64, :], in0=rps[0:64, :], in1=cs[0:64, th:th + 1024], op=ALU.mult)),
                     reads=pk(6, 7) + ["m_cs"], writes=["m_tmpa"])
                if DEBUG_STOP == 8:
                    return
                tb = M_tmpb.ap
                P.op("dve", (I("tensor_tensor", out=tb[0:64, :], in0=rps[64:128, :], in1=cs[64:128, th:th + 1024], op=ALU.mult)),
                     reads=pk(6, 7) + ["m_cs"], writes=["m_tmpb"])
                if DEBUG_STOP == 9:
                    return
                P.op("dve", (I("tensor_tensor", out=M_krope.ap[64:96, th:th + 1024], in0=ta[0:32, :],
                                                              in1=tb[0:32, :], op=ALU.add)),
                     reads=["m_tmpa", "m_tmpb"], writes=[("m_krope", hf)])
            P.op("dve", I("tensor_tensor", out=cs, in0=cs, in1=rq, op=ALU.mult),
                 reads=["m_cs", ("m_rq", 0), ("m_rq", 1)], writes=["m_cs"])
            if DEBUG_STOP == 2:
                dbg_dump([(0, M_qg.ap[:, 0, :], 0, 128, [("m_qg", 0, 0), ("m_qg", 0, 1)]),
                          (1, M_kvn.ap[:, 0, :], 0, 128, [("m_kvn", 0, 0), ("m_kvn", 0, 1)]),
                          (2, M_rq.ap, 0, 128, [("m_rq", 0), ("m_rq", 1)]),
                          (3, M_krope.ap[64:96, :], 64, 96, [("m_krope", 0), ("m_krope", 1)]),
                          (4, M_cs.ap, 0, 128, ["m_cs"])])
                return
            P.op("dve", I("memset", M_vaug.ap[:, :, :, 64:128], 1.0), writes=["m_vaug"])
            blk = 0
            for p in range(8):
                w, wk = wget()
                for hf in range(2):
                    th = hf * 1024
                    sa, sbq = 2 * hf, 2 * hf + 1
                    A_ps, B_ps = slot2(sa), slot2(sbq)
                    for tc in range(2):
                        for kc in range(6):
                            rhs = M_qg.ap[:, kc, th + tc * 512: th + tc * 512 + 512]
                            rk = [wk, ("m_qg", kc, hf)]
                            P.op("pe", I("matmul", A_ps[:, tc * 512:(tc + 1) * 512], lhsT=w[:, kc * 256: kc * 256 + 128], rhs=rhs,
                                         start=(kc == 0), stop=(kc == 5)), reads=rk, writes=pk(2 * sa + tc))
                            P.op("pe", I("matmul", B_ps[:, tc * 512:(tc + 1) * 512], lhsT=w[:, kc * 256 + 128: kc * 256 + 256], rhs=rhs,
                                         start=(kc == 0), stop=(kc == 5)), reads=rk, writes=pk(2 * sbq + tc))
                for hf in range(2):
                    th = hf * 1024
                    sa, sbq = 2 * hf, 2 * hf + 1
                    A_ps, B_ps = slot2(sa), slot2(sbq)
                    for hh in range(2):
                        P.op("dve", I("tensor_tensor", out=M_qk.ap[0:64, hh, th:th + 1024], in0=A_ps[hh * 64:hh * 64 + 64, :],
                                      in1=rq[hh * 64:hh * 64 + 64, th:th + 1024], op=ALU.mult),
                             reads=pk(2 * sa, 2 * sa + 1) + [("m_rq", hf)], writes=[("m_qk", hh, hf, "n")])
                    ta = M_tmpa.ap
                    P.op("dve", I("tensor_tensor", out=ta[0:64, :], in0=B_ps[0:64, :], in1=cs[0:64, th:th + 1024], op=ALU.mult),
                         reads=pk(2 * sbq, 2 * sbq + 1) + ["m_cs"], writes=["m_tmpa"])
                    tb = M_tmpb.ap
                    P.op("dve", I("tensor_tensor", out=tb[0:64, :], in0=B_ps[64:128, :], in1=cs[64:128, th:th + 1024], op=ALU.mult),
                         reads=pk(2 * sbq, 2 * sbq + 1) + ["m_cs"], writes=["m_tmpb"])
                    for hh in range(2):
                        P.op("dve", I("tensor_tensor", out=M_qk.ap[64:96, hh, th:th + 1024],
                                      in0=ta[hh * 32:hh * 32 + 32, :], in1=tb[hh * 32:hh * 32 + 32, :], op=ALU.add),
                             reads=["m_tmpa", "m_tmpb"], writes=[("m_qk", hh, hf, "r")])
                for hf in range(2):
                    th = hf * 1024
                    sk, sv = 2 * hf, 2 * hf + 1
                    K_ps, V_ps = slot2(sk), slot2(sv)
                    for tc in range(2):
                        for kc in range(2):
                            rhs = M_kvn.ap[:, kc, th + tc * 512: th + tc * 512 + 512]
                            P.op("pe", I("matmul", K_ps[:, tc * 512:(tc + 1) * 512], lhsT=w[:, 1536 + kc * 256: 1536 + kc * 256 + 128], rhs=rhs,
                                         start=(kc == 0), stop=(kc == 1)), reads=[wk, ("m_kvn", kc, hf)], writes=pk(2 * sk + tc))
                    for tt in range(8):
                        for kc in range(2):
                            P.op("pe", I("matmul", V_ps[:, tt * 128:(tt + 1) * 128],
                                         lhsT=M_kvn.ap[:, kc, th + tt * 128: th + tt * 128 + 128],
                                         rhs=w[:, 1536 + kc * 256 + 128: 1536 + kc * 256 + 256],
                                         start=(kc == 0), stop=(kc == 1)),
                                 reads=[wk, ("m_kvn", kc, hf)], writes=pk(2 * sv + tt // 4))
                for hf in range(2):
                    th = hf * 1024
                    sk, sv = 2 * hf, 2 * hf + 1
                    K_ps, V_ps = slot2(sk), slot2(sv)
                    for hh in range(2):
                        P.op("act", I("activation", out=M_qk.ap[0:64, 2 + hh, th:th + 1024], in_=K_ps[hh * 64:hh * 64 + 64, :], func=AF.Copy),
                             reads=pk(2 * sk, 2 * sk + 1), writes=[("m_qk", 2 + hh, hf, "n")])
                        P.op("act", I("activation", out=M_qk.ap[64:96, 2 + hh, th:th + 1024], in_=M_krope.ap[64:96, th:th + 1024], func=AF.Copy),
                             reads=[("m_krope", hf)], writes=[("m_qk", 2 + hh, hf, "r")])
                        P.op("act", I("activation", out=M_vaug.ap[:, hh, hf * 8:(hf + 1) * 8, 0:64],
                                      in_=V_ps.rearrange("p (t c) -> p t c", c=128)[:, :, hh * 64:hh * 64 + 64], func=AF.Copy),
                             reads=pk(2 * sv, 2 * sv + 1), writes=[("m_vaug", hh, hf)])
                if DEBUG_STOP == 3:
                    ks = lambda i: [("m_qk", i, 0, "n"), ("m_qk", i, 0, "r"), ("m_qk", i, 1, "n"), ("m_qk", i, 1, "r")]
                    dbg_dump([(0, M_qk.ap[0:96, 0, :], 0, 96, ks(0)), (1, M_qk.ap[0:96, 1, :], 0, 96, ks(1)),
                              (2, M_qk.ap[0:96, 2, :], 0, 96, ks(2)), (3, M_qk.ap[0:96, 3, :], 0, 96, ks(3)),
                              (4, M_vaug.ap[:, 0, :, :].rearrange("p a b -> p (a b)"), 0, 128, [("m_vaug", 0, 0), ("m_vaug", 0, 1), "m_vaug"]),
                              (5, M_vaug.ap[:, 1, :, :].rearrange("p a b -> p (a b)"), 0, 128, [("m_vaug", 1, 0), ("m_vaug", 1, 1), "m_vaug"])])
                    return
                blocks = []
                for qgrp in ((0, 1), (2, 3)):
                    for hh in range(2):
                        for qc in qgrp:
                            nkb = 4 * qc + 4
                            for kb in range(nkb):
                                jd = kb - 4 * qc
                                q0 = qc * 512 + max(jd, 0) * 128
                                blocks.append(dict(hh=hh, qc=qc, kb=kb, jd=jd, q0=q0, N=qc * 512 + 512 - q0,
                                                   first=(kb == 0), last=(kb == nkb - 1)))
                LA = 3

                def emit_st(bd, sb_i):
                    hh, qc, kb, jd, q0, N = bd["hh"], bd["qc"], bd["kb"], bd["jd"], bd["q0"], bd["N"]
                    Qh = M_qk.ap[0:96, hh, :]
                    Kh = M_qk.ap[0:96, 2 + hh, :]
                    ST = bank(sb_i)
                    hq, hk_ = qc // 2, kb // 8
                    P.op("pe", I("matmul", ST[:, 0:N], lhsT=Kh[:, kb * 128:(kb + 1) * 128], rhs=Qh[:, q0:q0 + N],
                                 start=True, stop=(jd < 0)),
                         reads=[("m_qk", hh, hq, "n"), ("m_qk", hh, hq, "r"), ("m_qk", 2 + hh, hk_, "n"), ("m_qk", 2 + hh, hk_, "r")],
                         writes=pk(sb_i))
                    if jd >= 0:
                        P.op("pe", I("matmul", ST[:, 0:128], lhsT=ident_bf[:], rhs=maskT_bf[:], start=False, stop=True),
                             reads=["ident_bf", "maskT_bf"], writes=pk(sb_i))
                    PT = M_pt.ap[:, sb_i, :]
                    P.op("act", I("activation", out=PT[:, 0:N], in_=ST[:, 0:N], func=AF.Exp, scale=ATT_SCALE),
                         reads=pk(sb_i), writes=[("m_pt", sb_i)])

                def emit_pv(bd, sb_i):
                    hh, qc, kb, q0, N = bd["hh"], bd["qc"], bd["kb"], bd["q0"], bd["N"]
                    h = 2 * p + hh
                    ob = 4 + (qc % 2)
                    O = bank(ob)
                    PT = M_pt.ap[:, sb_i, :]
                    P.op("pe", I("matmul", O[:, q0 - qc * 512: 512], lhsT=M_vaug.ap[:, hh, kb, :], rhs=PT[:, 0:N],
                                 start=bd["first"], stop=bd["last"]),
                         reads=[("m_pt", sb_i), ("m_vaug", hh, kb // 8), "m_vaug"], writes=pk(ob))
                    if bd["last"]:
                        rcp = M_rcp.ap
                        P.op("dve", I("reciprocal", out=rcp[64:128, :], in_=O[64:128, :]), reads=pk(ob), writes=["m_rcp"])
                        P.op("dve", I("tensor_tensor", out=hb[(h % 2) * 64:(h % 2) * 64 + 64, h // 2, qc * 512:(qc + 1) * 512],
                                      in0=O[0:64, :], in1=rcp[64:128, :], op=ALU.mult),
                             reads=pk(ob) + ["m_rcp"], writes=hk(h // 2, qc * 512, qc * 512 + 512))

                nb = len(blocks)
                base = blk
                for r in range(nb + LA):
                    if r < nb:
                        emit_st(blocks[r], (base + r) % 4)
                    if r >= LA:
                        emit_pv(blocks[r - LA], (base + r - LA) % 4)
                blk += nb
            if DEBUG_STOP == 4:
                dbg_dump([(c, hb[:, c, :], 0, 128, hk(c, 0, S)) for c in range(8)])
                return
            out_proj(sub, s_, is_last)

        def hgrn(sub, s_, j, is_last):
            T0, T1, T2, T3 = [g.ap for g in G_T]
            kT0, kT1, kT2, kT3 = ["g_T0", "g_T1", "g_T2", "g_T3"]
            smask = G_smask.ap
            P.op("dve", I("memset", smask, 1.0), writes=["g_smask"])
            P.op("dve", I("memset", smask.rearrange("p (n c) -> p n c", c=64)[:, :, 0:1], 0.0),
                 writes=["g_smask"])
            gn = vcol("gn", j)
            ebl = G_ebl2.ap
            sbf_keys = [("g_sbf", i) for i in range(5)]

            def stage_a1(hd):
                w0, wk0 = wget()
                for which, (coff, dst, dkey, func) in enumerate(((0, T0, kT0, AF.Silu), (128, T1, kT1, AF.Sigmoid))):
                    for hf in range(2):
                        th = hf * 1024
                        sidx = (which * 2 + hf) % 4
                        pp = slot2(sidx)
                        for tc in range(2):
                            for kc in range(8):
                                P.op("pe", I("matmul", pp[:, tc * 512:(tc + 1) * 512],
                                             lhsT=w0[:, kc * 256 + coff: kc * 256 + coff + 128],
                                             rhs=hb[:, kc, th + tc * 512: th + tc * 512 + 512], start=(kc == 0), stop=(kc == 7)),
                                     reads=[wk0] + hk(kc, th + tc * 512, th + tc * 512 + 512), writes=pk(2 * sidx + tc))
                        P.op("act", I("activation", out=dst[:, th:th + 1024], in_=pp, func=func),
                             reads=pk(2 * sidx, 2 * sidx + 1), writes=[(dkey, hf)])

            def stage_a2(hd):
                w1, wk1 = wget()
                for hf in range(2):
                    th = hf * 1024
                    sidx = hf
                    vp = slot2(sidx)
                    for tt in range(8):
                        for kc in range(8):
                            P.op("pe", I("matmul", vp[:, tt * 128:(tt + 1) * 128],
                                         lhsT=hb[:, kc, th + tt * 128: th + tt * 128 + 128],
                                         rhs=w1[:, kc * 256: kc * 256 + 128], start=(kc == 0), stop=(kc == 7)),
                                 reads=[wk1] + hk(kc, th + tt * 128, th + tt * 128 + 128), writes=pk(2 * sidx + tt // 4))
                    P.op("act", I("activation", out=G_vtok.ap[:, hf * 8:(hf + 1) * 8, :],
                                  in_=vp.rearrange("p (t c) -> p t c", c=128), func=AF.Copy),
                         reads=pk(2 * sidx, 2 * sidx + 1), writes=[("g_vtok", hf)])
                for hf in range(2):
                    th = hf * 1024
                    sidx = 2 + hf
                    gp = slot2(sidx)
                    for tc in range(2):
                        for kc in range(8):
                            P.op("pe", I("matmul", gp[:, tc * 512:(tc + 1) * 512], lhsT=w1[:, kc * 256 + 128: kc * 256 + 256],
                                         rhs=hb[:, kc, th + tc * 512: th + tc * 512 + 512], start=(kc == 0), stop=(kc == 7)),
                                 reads=[wk1] + hk(kc, th + tc * 512, th + tc * 512 + 512), writes=pk(2 * sidx + tc))
                    P.op("act", I("activation", out=G_sg.ap[:, th:th + 1024], in_=gp, func=AF.Silu),
                         reads=pk(2 * sidx, 2 * sidx + 1), writes=[("g_sg", hf)])

            def stage_b(hd):
                lbc = T_LB[:, j, hd:hd + 1]
                omlc = T_OML[:, j, hd:hd + 1]
                nomlc = T_NOML[:, j, hd:hd + 1]
                P.op("act", I("activation", out=T3, in_=T1, func=AF.Identity, scale=nomlc, bias=omlc),
                     reads=[(kT1, 0), (kT1, 1), "oml", "noml"], writes=[kT3])
                P.op("act", I("activation", out=T2, in_=T1, func=AF.Ln, scale=omlc, bias=lbc),
                     reads=[(kT1, 0), (kT1, 1), "oml", "lb_0", "lb_1"], writes=[kT2])
                P.op("dve", I("tensor_tensor_scan", out=T1, data0=smask, data1=T2, initial=0.0, op0=ALU.mult, op1=ALU.add),
                     reads=[kT2, "g_smask"], writes=[(kT1, 0), (kT1, 1)])
                P.op("act", I("activation", out=ebl, in_=T1.rearrange("p (n c) -> p n c", c=64)[:, :, 63], func=AF.Exp),
                     reads=[(kT1, 0), (kT1, 1)], writes=["g_ebl2"])
                P.op("act", I("activation", out=T2, in_=T1, func=AF.Exp), reads=[(kT1, 0), (kT1, 1)], writes=[kT2])
                P.op("dve", I("tensor_tensor", out=G_qt.ap, in0=T0, in1=T2, op=ALU.mult),
                     reads=[(kT0, 0), (kT0, 1), kT2], writes=["g_qt"])
                P.op("act", I("activation", out=T0, in_=T1, func=AF.Exp, scale=-1.0),
                     reads=[(kT1, 0), (kT1, 1)], writes=[(kT0, 0), (kT0, 1)])
                P.op("dve", I("tensor_tensor", out=T3, in0=T3, in1=T0, op=ALU.mult),
                     reads=[kT3, (kT0, 0), (kT0, 1)], writes=[kT3])
                P.op("act", I("activation", out=G_kt.ap, in_=T3, func=AF.Copy), reads=[kT3], writes=["g_kt"])
                P.op("dve", I("tensor_tensor", out=G_kh.ap.rearrange("p (n c) -> p n c", c=64),
                              in0=T3.rearrange("p (n c) -> p n c", c=64),
                              in1=ebl.rearrange("p (n o) -> p n o", o=1).to_broadcast([128, 32, 64]), op=ALU.mult),
                     reads=[kT3, "g_ebl2"], writes=["g_kh"])

            def stage_c(hd):
                tps = slot2(0).bitcast(BF16)
                for tt in range(16):
                    P.op("pe", I("transpose", tps[:, tt * 128:(tt + 1) * 128], G_kh.ap[:, tt * 128:(tt + 1) * 128], ident_bf[:]),
                         reads=["g_kh", "ident_bf"], writes=pk(tt // 8))
                for hb_ in range(2):
                    P.op("act", I("activation", out=G_khT.ap[:, hb_ * 8:(hb_ + 1) * 8, :].rearrange("p a b -> p (a b)"),
                                  in_=tps[:, hb_ * 1024:(hb_ + 1) * 1024], func=AF.Copy),
                         reads=pk(hb_), writes=[("g_khT", hb_)])
                U3 = G_u.ap.rearrange("p (v n) -> p n v", n=32)
                for g8 in range(4):
                    bE = 2 + 2 * (g8 % 2)
                    bO = bE + 1
                    for i8 in range(8):
                        n = g8 * 8 + i8
                        if n == 31:
                            continue
                        pb = (n % 2) * 64
                        bk = bO if (n % 2) else bE
                        P.op("pe", I("matmul", bank(bk, 128, (i8 // 2) * 128), lhsT=G_khT.ap[pb:pb + 64, n // 2, :],
                                     rhs=G_vtok.ap[pb:pb + 64, n // 2, :], start=True, stop=True),
                             reads=[("g_khT", n // 16), ("g_vtok", n // 16)], writes=pk(bk))
                    P.op("act", I("activation", out=U3[:, g8 * 8: g8 * 8 + 8: 2, :],
                                  in_=bank(bE, 512).rearrange("p (n v) -> p n v", v=128), func=AF.Copy),
                         reads=pk(bE), writes=[("g_u", 2 * g8)])
                    nn = 4 if g8 < 3 else 3
                    P.op("act", I("activation", out=U3[:, g8 * 8 + 1: g8 * 8 + 1 + 2 * nn: 2, :],
                                  in_=bank(bO, nn * 128).rearrange("p (n v) -> p n v", v=128), func=AF.Copy),
                         reads=pk(bO), writes=[("g_u", 2 * g8 + 1)])
                P.op("dve", I("memset", U3[:, 31:32, :], 0.0), writes=[("g_u", 8)])
                er = G_eblrep.ap.rearrange("p (v n) -> p v n", n=32)
                P.op("dve", I("tensor_copy", out=er, in_=ebl.rearrange("p (o n) -> p o n", o=1).to_broadcast([128, 32, 32])),
                     reads=["g_ebl2"], writes=["g_eblrep"])
                P.op("dve", I("memset", er[:, :, 0:1], 0.0), reads=["g_eblrep"], writes=["g_eblrep"])
                sbf_flat = G_sbf.ap.rearrange("p a b -> p (a b)")
                for vq in range(4):
                    P.op("dve", I("tensor_tensor_scan", out=sbf_flat[:, vq * 1024:(vq + 1) * 1024], data0=G_eblrep.ap,
                                  data1=G_u.ap[:, vq * 1024:(vq + 1) * 1024], initial=0.0, op0=ALU.mult, op1=ALU.add),
                         reads=["g_eblrep"] + [("g_u", g) for g in range(9)], writes=[("g_sbf", 1 + vq)])
                LA = 3
                sbf_vn = sbf_flat.rearrange("p (v n) -> p n v", n=32)

                def emit_at(jj):
                    ab = jj % 4
                    AT = bank(ab, 128)
                    P.op("pe", I("matmul", AT, lhsT=G_kt.ap[:, jj * 128:(jj + 1) * 128], rhs=G_qt.ap[:, jj * 128:(jj + 1) * 128],
                                 start=True, stop=True), reads=["g_kt", "g_qt"], writes=pk(ab))
                    P.op("dve", I("tensor_tensor", out=G_asb.ap[:, ab, :], in0=AT, in1=mask2_f, op=ALU.mult),
                         reads=pk(ab) + ["cst"], writes=[("g_asb", ab)])

                def emit_o(jj):
                    ab = jj % 4
                    hf, jt = jj // 8, jj % 8
                    osl = 2 + hf
                    oc = slot2(osl)[:, jt * 128:(jt + 1) * 128]
                    okey = pk(2 * osl + jt // 4)
                    P.op("pe", I("matmul", oc, lhsT=G_vtok.ap[:, jj, :], rhs=G_asb.ap[:, ab, :], start=True, stop=False),
                         reads=[("g_asb", ab), ("g_vtok", hf)], writes=okey)
                    for i2 in range(2):
                        n = 2 * jj + i2
                        if n == 0:
                            continue
                        P.op("pe", I("matmul", oc[:, i2 * 64:(i2 + 1) * 64], lhsT=sbf_vn[:, n - 1, :],
                                     rhs=G_qt.ap[:, n * 64:(n + 1) * 64], start=False, stop=(i2 == 1)),
                             reads=sbf_keys + ["g_qt"], writes=okey)

                for r in range(16 + LA):
                    if r < 16:
                        emit_at(r)
                    if r >= LA:
                        emit_o(r - LA)
                for hf in range(2):
                    th = hf * 1024
                    osl = 2 + hf
                    Ops = slot2(osl)
                    osq = G_osq.ap
                    P.op("act", I("activation", out=osq, in_=Ops, func=AF.Square),
                         reads=pk(2 * osl, 2 * osl + 1), writes=["g_osq"])
                    for tc in range(2):
                        P.op("pe", I("matmul", bank(tc), lhsT=ones_f, rhs=osq[:, tc * 512:(tc + 1) * 512], start=True, stop=True),
                             reads=["g_osq", "cst"], writes=pk(tc))
                    rstd = G_rstd.ap
                    P.op("act", I("activation", out=rstd, in_=slot2(0), func=AF.Sqrt, scale=1.0 / 128, bias=eps_rms),
                         reads=pk(0, 1) + ["eps_t"], writes=["g_rstd"])
                    P.op("dve", I("reciprocal", out=rstd, in_=rstd), reads=["g_rstd"], writes=["g_rstd"])
                    to = G_to.ap
                    P.op("dve", I("scalar_tensor_tensor", out=to, in0=Ops, scalar=gn, in1=rstd, op0=ALU.mult, op1=ALU.mult),
                         reads=pk(2 * osl, 2 * osl + 1) + ["g_rstd", "vecs"], writes=["g_to"])
                    P.op("dve", I("tensor_tensor", out=G_og.ap, in0=to, in1=G_sg.ap[:, th:th + 1024], op=ALU.mult),
                         reads=["g_to", ("g_sg", hf)], writes=["g_og"])
                    P.dma("sp", I("dma_start", out=ogd[hd, :, th:th + 1024], in_=G_og.ap), "st_og",
                          reads=["g_og"], writes=[("ogd", hd, hf)])

            stage_a1(0)
            for hd in range(8):
                stage_a2(hd)
                stage_b(hd)
                if hd + 1 < 8:
                    stage_a1(hd + 1)
                stage_c(hd)
            for c in range(8):
                P.dma("sp", I("dma_start", out=hb[:, c, :], in_=ogd[c]), "ld_og%d" % c,
                      reads=[("ogd", c, 0), ("ogd", c, 1)], writes=hk(c, 0, S))
            if DEBUG_STOP == 13:
                dbg_dump([(c, hb[:, c, :], 0, 128, hk(c, 0, S)) for c in range(8)])
                return
            out_proj(sub, s_, is_last)

        for s_ in range(nseq):
            for c in range(8):
                P.dma("sp", (I("dma_start", out=xp[:, c, :], in_=x_in[s_, :, c, :])), "ld_x%d" % c,
                      writes=xk(c, 0, S))
            first = sublayers[0]
            for c in range(8):
                P.op("act", (I("activation", out=hb[:, c, :], in_=xp[:, c, :], func=AF.Identity,
                                                         scale=T_S1[:, first, s_, c:c + 1], bias=T_SH[:, first, s_, c:c + 1])),
                     reads=xk(c, 0, S) + [("T_S1", first), ("T_SH", first)], writes=hk(c, 0, S))
                P.op("dve", (I("tensor_scalar", out=xp[:, c, :], in0=xp[:, c, :], scalar1=ALPHA, scalar2=None,
                                                            op0=ALU.mult)),
                     reads=xk(c, 0, S), writes=xk(c, 0, S))
            for idx, sub in enumerate(sublayers):
                is_last = (idx == len(sublayers) - 1)
                l, typ = sub // 2, sub % 2
                if typ == 1:
                    ffn(sub, s_, is_last)
                else:
                    if l % 2 == 0:
                        mla(sub, s_, l // 2, is_last)
                    else:
                        hgrn(sub, s_, l // 2, is_last)
            for c in range(8):
                P.dma("sp", (I("dma_start", out=y_out[s_, :, c, :], in_=xp[:, c, :])), "st_y%d" % c,
                      reads=xk(c, 0, S))
        assert DEBUG_STOP or wstate["next"] == len(wlist), (wstate, len(wlist))
        cnt = P.emit(nc)
    return nc, cnt, len(P.ops)


_CACHE = {}


def run(inputs, sublayers, nseq, ncores, x_override=None):
    shared = prep_shared(inputs)
    key = (tuple(sublayers), nseq)
    if key not in _CACHE:
        _CACHE[key] = build(list(sublayers), nseq)
    nc, cnt, nops = _CACHE[key]
    in_maps = []
    for core in range(ncores):
        m = dict(shared)
        inp2 = inputs if x_override is None else dict(inputs, x=x_override)
        m.update(prep_core(inp2, core * nseq, nseq))
        in_maps.append(m)
    res = run_bass_kernel_spmd(nc, in_maps, core_ids=list(range(ncores)))
    outs = []
    for core in range(ncores):
        y = res.results[core]["y_out"]
        for s_ in range(nseq):
            outs.append(np.ascontiguousarray(y[s_].transpose(1, 0, 2).reshape(D, S).T))
    return np.stack(outs, 0)


def kernel(**inputs):
    inputs = {k: np.asarray(v) for k, v in inputs.items()}
    out = run(inputs, list(range(8)), 2, NCORES)
    return out.astype(np.float32)
```
